# Optimizing a Trainium2 kernel written in Bass

```python
import math
import jax, jax.numpy as jnp
from jax import lax
import numpy as np

D_MODEL = 1024
BATCH = 8
SEQ = 2048
DEPTH = 1

EPS = 1e-6
PLE_DIM = 256
S5_GROUP_CH = 16
S5_GROUPS = D_MODEL // 32
S5_WIDTH = S5_GROUPS * S5_GROUP_CH
S5_STATE = 64
LRU_HEAD_DIM = 64
LRU_WIDTH = D_MODEL
LRU_HEADS = LRU_WIDTH // LRU_HEAD_DIM
LRU_C = 8.0
CONV_WIDTH = 4
FFN_HIDDEN = -(-8 * D_MODEL // (3 * 256)) * 256
IN_COLS = S5_WIDTH + LRU_WIDTH + 2 * D_MODEL

kernel_name = "hybrid_s5_rglru_gated_block"


def rms_norm(x, g):
    xf = x.astype(jnp.float32)
    y = xf * lax.rsqrt(jnp.mean(xf * xf, axis=-1, keepdims=True) + EPS)
    return (y * g.astype(jnp.float32)).astype(x.dtype)


def s5_mixer(u, lam_re, lam_im, log_dt, b_re, b_im, c_re, c_im, d_skip, w_glu, b_glu):
    f32 = jnp.float32
    bsz, L, _ = u.shape
    uf = u.astype(f32).reshape(bsz, L, S5_GROUPS, S5_GROUP_CH)
    lr = lam_re.astype(f32)
    li = lam_im.astype(f32)
    dt = jnp.exp(log_dt.astype(f32))[:, None]
    mag = jnp.exp(lr * dt)
    ar = mag * jnp.cos(li * dt)
    ai = mag * jnp.sin(li * dt)
    den = lr * lr + li * li
    nr = ar - 1.0
    fr = (nr * lr + ai * li) / den
    fi = (ai * lr - nr * li) / den
    br = b_re.astype(f32)
    bi = b_im.astype(f32)
    bbr = fr[..., None] * br - fi[..., None] * bi
    bbi = fr[..., None] * bi + fi[..., None] * br
    xr = jnp.einsum('blgp,gnp->blgn', uf, bbr)
    xi = jnp.einsum('blgp,gnp->blgn', uf, bbi)
    a_r = jnp.broadcast_to(ar, (1, L) + ar.shape)
    a_i = jnp.broadcast_to(ai, (1, L) + ai.shape)

    def combine(e1, e2):
        a1r, a1i, b1r, b1i = e1
        a2r, a2i, b2r, b2i = e2
        return (a2r * a1r - a2i * a1i,
                a2r * a1i + a2i * a1r,
                a2r * b1r - a2i * b1i + b2r,
                a2r * b1i + a2i * b1r + b2i)

    _, _, sr, si = lax.associative_scan(combine, (a_r, a_i, xr, xi), axis=1)
    y = (jnp.einsum('blgn,gpn->blgp', sr, c_re.astype(f32))
         - jnp.einsum('blgn,gpn->blgp', si, c_im.astype(f32))
         + d_skip.astype(f32) * uf)
    y = y.reshape(bsz, L, S5_WIDTH)
    z = jax.nn.gelu(y)
    out = z * jax.nn.sigmoid(z @ w_glu.astype(f32) + b_glu.astype(f32))
    return out.astype(u.dtype)


def rglru_mixer(u, conv_w, conv_b, w_r, b_r, w_i, b_i, lru_lambda):
    f32 = jnp.float32
    xc = lax.conv_general_dilated(
        u, conv_w[:, None, :].astype(u.dtype), window_strides=(1,),
        padding=[(CONV_WIDTH - 1, 0)], dimension_numbers=('NWC', 'WIO', 'NWC'),
        feature_group_count=LRU_WIDTH) + conv_b
    bsz, L, _ = xc.shape
    xh = xc.astype(f32).reshape(bsz, L, LRU_HEADS, LRU_HEAD_DIM)
    r = jax.nn.sigmoid(jnp.einsum('blhi,hij->blhj', xh, w_r.astype(f32)) + b_r.astype(f32))
    ig = jax.nn.sigmoid(jnp.einsum('blhi,hij->blhj', xh, w_i.astype(f32)) + b_i.astype(f32))
    log_a = -LRU_C * r * jax.nn.softplus(-lru_lambda.astype(f32).reshape(LRU_HEADS, LRU_HEAD_DIM))
    a = jnp.exp(log_a)
    mult = jnp.sqrt(-jnp.expm1(2.0 * log_a))
    bx = mult * ig * xh

    def combine(e1, e2):
        a1, b1 = e1
        a2, b2 = e2
        return a2 * a1, a2 * b1 + b2

    _, h = lax.associative_scan(combine, (a, bx), axis=1)
    return h.reshape(bsz, L, LRU_WIDTH).astype(u.dtype)


def setup_inputs(seed: int = 0) -> dict:
    key = jax.random.key(seed)
    ks = jax.random.split(key, 40)
    f32 = jnp.float32

    def nrm(k, shape, scale):
        return jax.random.normal(k, shape, f32) * scale

    def gain(k, shape):
        return 1.0 + 0.01 * jax.random.normal(k, shape, f32)

    G, N, P = S5_GROUPS, S5_STATE, S5_GROUP_CH
    lam_re = -0.5 + 0.01 * jax.random.normal(ks[3], (DEPTH, G, N), f32)
    lam_im = jnp.pi * jnp.arange(N, dtype=f32)[None, None, :] + 0.01 * jax.random.normal(ks[4], (DEPTH, G, N), f32)
    log_dt = jax.random.uniform(ks[5], (DEPTH, G), f32, math.log(1e-3), math.log(1e-1))
    u_a = jax.random.uniform(ks[16], (DEPTH, LRU_WIDTH), f32, 0.9, 0.999)
    a_base = u_a ** (1.0 / LRU_C)
    lru_lambda = jnp.log(a_base) - jnp.log1p(-a_base)

    return {
        "x": nrm(ks[0], (BATCH, SEQ, D_MODEL), 1.0),
        "p": nrm(ks[1], (DEPTH, BATCH, SEQ, PLE_DIM), 1.0),
        "g_mix": gain(ks[2], (DEPTH, D_MODEL)),
        "w_in": nrm(ks[6], (DEPTH, D_MODEL, IN_COLS), D_MODEL ** -0.5),
        "b_in": nrm(ks[7], (DEPTH, IN_COLS), 0.01),
        "lam_re": lam_re,
        "lam_im": lam_im,
        "log_dt": log_dt,
        "s5_b_re": nrm(ks[8], (DEPTH, G, N, P), (2 * P) ** -0.5),
        "s5_b_im": nrm(ks[9], (DEPTH, G, N, P), (2 * P) ** -0.5),
        "s5_c_re": nrm(ks[10], (DEPTH, G, P, N), N ** -0.5),
        "s5_c_im": nrm(ks[11], (DEPTH, G, P, N), N ** -0.5),
        "s5_d": nrm(ks[12], (DEPTH, G, P), 1.0),
        "w_glu": nrm(ks[13], (DEPTH, S5_WIDTH, S5_WIDTH), S5_WIDTH ** -0.5),
        "b_glu": nrm(ks[14], (DEPTH, S5_WIDTH), 0.01),
        "conv_w": nrm(ks[15], (DEPTH, CONV_WIDTH, LRU_WIDTH), CONV_WIDTH ** -0.5),
        "conv_b": nrm(ks[17], (DEPTH, LRU_WIDTH), 0.01),
        "w_r": nrm(ks[18], (DEPTH, LRU_HEADS, LRU_HEAD_DIM, LRU_HEAD_DIM), LRU_HEAD_DIM ** -0.5),
        "b_r": nrm(ks[19], (DEPTH, LRU_HEADS, LRU_HEAD_DIM), 0.01),
        "w_i": nrm(ks[20], (DEPTH, LRU_HEADS, LRU_HEAD_DIM, LRU_HEAD_DIM), LRU_HEAD_DIM ** -0.5),
        "b_i": nrm(ks[21], (DEPTH, LRU_HEADS, LRU_HEAD_DIM), 0.01),
        "lru_lambda": lru_lambda,
        "w_a_out": nrm(ks[22], (DEPTH, S5_WIDTH, D_MODEL), S5_WIDTH ** -0.5),
        "w_b_out": nrm(ks[23], (DEPTH, LRU_WIDTH, D_MODEL), LRU_WIDTH ** -0.5),
        "w_o": nrm(ks[24], (DEPTH, D_MODEL, D_MODEL), D_MODEL ** -0.5),
        "g_ffn": gain(ks[25], (DEPTH, D_MODEL)),
        "w_ffn_gate": nrm(ks[26], (DEPTH, D_MODEL, FFN_HIDDEN), D_MODEL ** -0.5),
        "w_ffn_up": nrm(ks[27], (DEPTH, D_MODEL, FFN_HIDDEN), D_MODEL ** -0.5),
        "w_ffn_down": nrm(ks[28], (DEPTH, FFN_HIDDEN, D_MODEL), FFN_HIDDEN ** -0.5),
        "g_ple_gate": gain(ks[29], (DEPTH, D_MODEL)),
        "w_ple_gate": nrm(ks[30], (DEPTH, D_MODEL, D_MODEL), D_MODEL ** -0.5),
        "b_ple_gate": nrm(ks[31], (DEPTH, D_MODEL), 0.01),
        "w_ple": nrm(ks[32], (DEPTH, PLE_DIM, D_MODEL), PLE_DIM ** -0.5),
        "g_ple": gain(ks[33], (DEPTH, D_MODEL)),
        "g_final": gain(ks[34], (D_MODEL,)),
    }


def reference(x, p, g_mix, w_in, b_in, lam_re, lam_im, log_dt, s5_b_re, s5_b_im,
              s5_c_re, s5_c_im, s5_d, w_glu, b_glu, conv_w, conv_b, w_r, b_r, w_i, b_i,
              lru_lambda, w_a_out, w_b_out, w_o, g_ffn, w_ffn_gate, w_ffn_up, w_ffn_down,
              g_ple_gate, w_ple_gate, b_ple_gate, w_ple, g_ple, g_final):
    s_a = S5_WIDTH
    s_b = S5_WIDTH + LRU_WIDTH
    s_g = s_b + D_MODEL
    for i in range(DEPTH):
        h = rms_norm(x, g_mix[i])
        z = h @ w_in[i] + b_in[i]
        u_a = z[..., :s_a]
        u_b = z[..., s_a:s_b]
        gate_a = jax.nn.sigmoid(z[..., s_b:s_g])
        gate_b = jax.nn.sigmoid(z[..., s_g:])
        y_a = s5_mixer(u_a, lam_re[i], lam_im[i], log_dt[i], s5_b_re[i], s5_b_im[i],
                       s5_c_re[i], s5_c_im[i], s5_d[i], w_glu[i], b_glu[i])
        y_b = rglru_mixer(u_b, conv_w[i], conv_b[i], w_r[i], b_r[i], w_i[i], b_i[i],
                          lru_lambda[i])
        merged = gate_a * (y_a @ w_a_out[i]) + gate_b * (y_b @ w_b_out[i])
        x = x + merged @ w_o[i]
        h2 = rms_norm(x, g_ffn[i])
        x = x + (jax.nn.silu(h2 @ w_ffn_gate[i]) * (h2 @ w_ffn_up[i])) @ w_ffn_down[i]
        gate_p = jax.nn.sigmoid(rms_norm(x, g_ple_gate[i]) @ w_ple_gate[i] + b_ple_gate[i])
        e = rms_norm(p[i] @ w_ple[i], g_ple[i])
        x = x + gate_p * e
    return rms_norm(x, g_final)
```

```python
import math
from contextlib import ExitStack

import numpy as np
import concourse.bass as bass
import concourse.mybir as mybir
from concourse.bass_utils import run_bass_kernel_spmd

F32 = mybir.dt.float32
BF16 = mybir.dt.bfloat16
I32 = mybir.dt.int32
AF = mybir.ActivationFunctionType
ALU = mybir.AluOpType

D = 1024
L = 2048
TB = 512
NB = L // TB
PLE = 256
S5W = 512
FFN = 2816
INC = 3584
TC = 256
ROT_ENG = "dve"
PIPELINE = True
NSUB = TB // TC
NSLOT = 5
SLOT_ELEMS = 4096
EPS = 1e-6
TWO_PI = float(2 * math.pi)
PI = 3.1415925
HALF_PI = float(math.pi / 2)

C_GMIX, C_BIN, C_S5D, C_BGLU, C_CONVW, C_CONVB, C_BR, C_BI, C_LAM, C_GFFN, C_GPLEG, NCOL = 0, 8, 36, 40, 44, 76, 84, 92, 100, 108, 116, 124


class Buf:
    __slots__ = ("name", "w", "r")

    def __init__(self, name=""):
        self.name = name
        self.w = None
        self.r = {}


class Em:
    def __init__(self, engines, sems):
        self.sem = sems
        self.cnt = {k: 0 for k in sems}
        self.waited = {e: {} for e in engines}
        self.thunks = {e: [] for e in engines}
        self.self_sync = True

    def _waits(self, eng_name, reads, writes):
        deps = {}

        def add(d):
            if d is None:
                return
            k, v = d
            if deps.get(k, 0) < v:
                deps[k] = v
        for b in reads:
            add(b.w)
        for b in writes:
            add(b.w)
            for d in b.r.items():
                add(d)
        for k, v in deps.items():
            if k == eng_name and (eng_name == "pe" or not self.self_sync):
                continue
            if self.waited[eng_name].get(k, 0) >= v:
                continue
            self.thunks[eng_name].append(lambda e, s=self.sem[k], v=v: e.wait_ge(s, v))
            self.waited[eng_name][k] = v

    def _commit(self, me, reads, writes):
        for b in reads:
            if b.r.get(me[0], 0) < me[1]:
                b.r[me[0]] = me[1]
        for b in writes:
            b.w = me
            b.r = {}

    def op(self, eng_name, fn, reads=(), writes=()):
        self._waits(eng_name, reads, writes)
        self.cnt[eng_name] += 1
        self.thunks[eng_name].append(lambda e, fn=fn, s=self.sem[eng_name]: fn(e).then_inc(s, 1))
        self._commit((eng_name, self.cnt[eng_name]), reads, writes)

    def dma(self, queue, sem_key, pairs, reads=(), writes=()):
        self._waits(queue, reads, writes)
        for (o, i) in pairs:
            self.cnt[sem_key] += 16
            self.thunks[queue].append(lambda e, o=o, i=i, s=self.sem[sem_key]: e.dma_start(out=o, in_=i).then_inc(s, 16))
        self._commit((sem_key, self.cnt[sem_key]), reads, writes)

    def final_wait(self, eng_name, bufs):
        self._waits(eng_name, bufs, bufs)


def build_nc(debug=False):
    nc = bass.Bass("TRN2", target_bir_lowering=False)

    def din(name, shape):
        return nc.dram_tensor(name, list(shape), F32, kind="ExternalInput").ap()

    x_d = din("x", [L, D])
    p_d = din("p", [L, PLE])
    w_in_d = din("w_in", [D, INC])
    w_glu_d = din("w_glu", [S5W, S5W])
    w_a_d = din("w_a_out", [S5W, D])
    w_b_d = din("w_b_out", [D, D])
    w_o_d = din("w_o", [D, D])
    w_fg_d = din("w_ffn_gate", [D, FFN])
    w_fu_d = din("w_ffn_up", [D, FFN])
    w_fd_d = din("w_ffn_down", [FFN, D])
    w_pg_d = din("w_ple_gate", [D, D])
    w_pl_d = din("w_ple", [PLE, D])
    colp_d = din("colp", [128, NCOL])
    s5p_d = din("s5p", [128, 48])
    rowp_d = din("rowp", [3, D])
    wrbd_d = din("wr_bd", [128, 8 * 128])
    wibd_d = din("wi_bd", [128, 8 * 128])
    s5B_d = din("s5B", [128, 2 * 16 * 128])
    s5C_d = din("s5C", [128, 2 * 16 * 32])
    ident_d = din("ident", [128, 128])
    iota_d = din("iota", [128, 16 * TC])
    out_d = nc.dram_tensor("out", [L, D], F32, kind="ExternalOutput").ap()

    with ExitStack() as es:
        def sb(name, shape, dt):
            return es.enter_context(nc.sbuf_tensor(name, list(shape), dt))

        slots = [sb(f"slot{i}", [128, SLOT_ELEMS], BF16) for i in range(NSLOT)]
        w_glu_sb = sb("w_glu_sb", [128, 4 * 512], BF16)
        wr_sb = sb("wr_sb", [128, 8 * 128], BF16)
        wi_sb = sb("wi_sb", [128, 8 * 128], BF16)
        Bsb = sb("Bsb", [128, 2 * 16 * 128], BF16)
        Csb = sb("Csb", [128, 2 * 16 * 32], BF16)
        Er = sb("Er", [128, 16 * TC], F32)
        Ei = sb("Ei", [128, 16 * TC], F32)
        ident = sb("ident_sb", [128, 128], F32)
        ident_bf = sb("ident_bf", [128, 128], BF16)
        colp = sb("colp_sb", [128, NCOL], F32)
        hcol = sb("hcol", [128, NCOL], F32)
        dcol = sb("dcol", [128, 64], F32)
        s5c = sb("s5c", [128, 16 * 24], F32)
        s5ci = sb("s5ci", [128, 16], I32)
        gple_bc = sb("gple_bc", [128, D], F32)
        gfin_bc = sb("gfin_bc", [128, D], F32)
        bpg_row = sb("bpg_row", [1, D], BF16)
        ones_bf = sb("ones_bf", [128, 128], BF16)
        ones_row = sb("ones_row", [1, 128], BF16)
        x_fm = sb("x_fm", [128, 8 * TB], F32)
        A2 = sb("A2", [128, 8 * (TB + 3)], BF16)
        A3 = sb("A3", [128, 8 * TB], BF16)
        A4 = sb("A4", [128, 4 * TB], BF16)
        FB = sb("FB", [128, 8 * TB], BF16)
        hid = sb("hid", [128, 22 * TB], BF16)
        NTS, NTF = 8, 7
        NT = NTS + NTF
        tmp = sb("tmp", [128, NT * 512], F32)
        A1 = tmp[:].bitcast(BF16)[:, 4 * 1024:8 * 1024]
        A6 = hid[:, 8 * TB:12 * TB]
        sqb = sb("sqb", [128, 2 * TB], BF16)
        p_fm = sqb
        sbf = sb("sbf", [128, 4 * TB], BF16)
        xcb = sb("xcb", [128, TB], BF16)
        carry_s5 = sb("carry_s5", [128, 32], F32)
        carry_lru = sb("carry_lru", [128, 8], F32)
        small = sb("small", [128, 96], F32)
        ps = es.enter_context(nc.psum_tensor("ps", [128, 8, 512], F32))

        eng_names = ["sp", "act", "dve", "pool", "pe"]
        sem_keys = eng_names + [f"dslot{i}" for i in range(NSLOT)] + ["dparam", "dparam2", "dx0", "dx1", "dxf0", "dxf1", "dxf2", "dp", "do0", "do1", "do2", "do3", "do4", "do5", "do6", "ddbg"]
        sems = {k: es.enter_context(nc.semaphore("s_" + k)) for k in sem_keys}
        block = es.enter_context(nc.Block())

        def program(em, plan, dbg_on):
            B_ps = [Buf(f"ps{i}") for i in range(8)]
            B_slot = [Buf(f"slot{i}") for i in range(NSLOT)]
            B_param = Buf("param")
            B_param2 = Buf("param2")
            B_cst = Buf("cst")
            B_xfm = [Buf(f"xfm{k}") for k in range(8)]
            B_A2 = [Buf(f"A2_{k}") for k in range(8)]
            B_A3 = [Buf(f"A3_{k}") for k in range(8)]
            B_A4 = [Buf(f"A4_{k}") for k in range(4)]
            B_FB = [Buf(f"FB_{k}") for k in range(8)]
            B_hid = [Buf(f"hid{k}") for k in range(22)]
            B_pst = Buf("pst")
            B_tmp = [Buf(f"tmp{i}") for i in range(NT)]
            B_A1 = [B_tmp[4 + k // 2] for k in range(8)]
            B_A6 = B_hid[8:12]
            B_sq = [Buf("sq0"), Buf("sq1")]
            B_sbf = [Buf(f"sbf{i}") for i in range(4)]
            B_xc = [Buf("xc0"), Buf("xc1")]
            B_cs5 = Buf("carry_s5")
            B_clru = Buf("carry_lru")
            B_small = Buf("small")
            B_smallF = Buf("smallF")
            B_s5q = [[Buf(f"s5q{a}{c}") for c in range(4)] for a in range(2)]
            B_sqF = [Buf("sqF0"), Buf("sqF1")]
            B_pfm = B_sqF
            B_tab = Buf("tables")
            B_out = [Buf(f"out{i}") for i in range(NTF)]

            st = {"bank": 0, "bankS": 0, "bankF": 0, "tmpS": 0, "tmpF": 0, "tmpL": 0}
            dbg_bufs = []

            def dbg(name, ap, bufs):
                if not dbg_on:
                    return
                shp = [int(v) for v in ap.shape]
                d = nc.dram_tensor("dbg_" + name, shp, ap.dtype, kind="ExternalOutput").ap()
                ob = Buf("dbg_" + name)
                em.dma("sp", "ddbg", [(d, ap)], reads=list(bufs), writes=[ob])
                dbg_bufs.append(ob)

            def bank():
                b = st["bank"]
                st["bank"] = (b + 1) % 8
                return b

            SBANKS = [0, 1, 2, 3]
            LBANKS = [0, 0]
            XBANKS = [1, 2]
            FBANKS = [4, 5, 6, 7]

            def bankS():
                b = st["bankS"]
                st["bankS"] = (b + 1) % len(SBANKS)
                return SBANKS[b]

            def bankF():
                b = st["bankF"]
                st["bankF"] = (b + 1) % len(FBANKS)
                return FBANKS[b]

            def tmpi(n=1):
                i = st["tmpS"]
                if i + n > 4:
                    i = 0
                st["tmpS"] = (i + n) % 4
                return i

            def tmpl():
                i = st["tmpL"]
                st["tmpL"] = (i + 1) % 4
                return 4 + i

            def tmpf(n=1):
                i = st["tmpF"]
                if i + n > NTF:
                    i = 0
                st["tmpF"] = (i + n) % NTF
                return NTS + i

            def T(i, n=1):
                return tmp[:, i * 512:(i + n) * 512]

            def col(c):
                return colp[:, c:c + 1]

            def hc(c):
                return hcol[:, c:c + 1]

            plan_rec = []
            ws = {"issued": 0, "next": 0}

            def _plan():
                return plan if plan is not None else plan_rec

            released = set()

            def _issue_one():
                lst = _plan()
                n = ws["issued"]
                w_ap, k0, kt, c0, cols = lst[n]
                src = w_ap.rearrange("(k p) c -> p k c", p=128)[:, k0:k0 + kt, c0:c0 + cols]
                si = n % NSLOT
                dst = slots[si][:, 0:kt * cols].rearrange("p (k c) -> p k c", c=cols)
                em.dma("pool", f"dslot{si}", [(dst, src)], writes=[B_slot[si]])
                ws["issued"] = n + 1

            def try_issue():
                lst = _plan()
                while ws["issued"] < len(lst) and (ws["issued"] < NSLOT or (ws["issued"] - NSLOT) in released):
                    _issue_one()

            def issue_slab():
                try_issue()

            def get_slab(w_ap, k0, kt, c0, cols):
                n = ws["next"]
                ws["next"] = n + 1
                if plan is None:
                    plan_rec.append((w_ap, k0, kt, c0, cols))
                else:
                    assert plan[n][0] is w_ap and tuple(plan[n][1:]) == (k0, kt, c0, cols), "slab plan mismatch"
                try_issue()
                assert ws["issued"] > n, "slab ring exhausted (too many live slabs)"
                return (n, n % NSLOT), cols

            def release_slab(h):
                released.add(h[0])
                try_issue()

            def slab_lhsT(h, cols, k, c0, n=128):
                return slots[h[1]][:, k * cols + c0:k * cols + c0 + n]

            em.dma("sp", "dparam", [
                (colp[:], colp_d), (s5c[:, 0:48], s5p_d), (ident[:], ident_d),
                (gple_bc[:], rowp_d[1:2, :].broadcast_to([128, D])),
                (gfin_bc[:], rowp_d[2:3, :].broadcast_to([128, D])),
                (tmp[:, 0:1024], s5C_d), (x_fm[:, 0:16 * TC], iota_d),
            ], writes=[B_param, B_tmp[0], B_tmp[1]] + B_xfm)
            em.dma("pool", "dparam2", [
                (w_glu_sb[:].rearrange("p (k c) -> p k c", c=512), w_glu_d.rearrange("(k p) c -> p k c", p=128)),
                (wr_sb[:], wrbd_d), (wi_sb[:], wibd_d), (Bsb[:], s5B_d), (bpg_row[:], rowp_d[0:1, :]),
            ], writes=[B_param2])
            if plan is not None:
                try_issue()

            P = [B_param, B_param2, B_cst]

            def V(eng, fn, reads, writes):
                em.op(eng, fn, reads=reads, writes=writes)

            V("dve", lambda e: e.memset(ones_bf[:], 1.0 / D), [], [B_cst])
            V("dve", lambda e: e.memset(ones_row[:], 1.0), [], [B_cst])
            V("dve", lambda e: e.tensor_copy(out=ident_bf[:], in_=ident[:]), [B_param], [B_cst])
            V("dve", lambda e: e.memset(small[:], 0.0), [], [B_cst])
            V("dve", lambda e: e.memset(small[:, 0:1], EPS), [B_cst], [B_cst])
            V("dve", lambda e: e.memset(carry_s5[:], 0.0), [], [B_cs5])
            V("dve", lambda e: e.memset(carry_lru[:], 0.0), [], [B_clru])
            V("dve", lambda e: e.memset(A2[:], 0.0), [], B_A2)
            V("dve", lambda e: e.memset(small[:, 5:6], 1.0), [B_cst], [B_cst])
            eps_col = small[:, 0:1]
            one_col = small[:, 5:6]
            V("dve", lambda e: e.tensor_scalar(out=hcol[:], in0=colp[:], scalar1=0.5, scalar2=None, op0=ALU.mult), P, [B_cst])
            V("act", lambda e: e.activation(out=dcol[:, 16:24], in_=colp[:, C_LAM:C_LAM + 8], func=AF.Exp, scale=-1.0), P, [B_cst])
            V("act", lambda e: e.activation(out=dcol[:, 24:32], in_=dcol[:, 16:24], func=AF.Ln, bias=one_col, scale=1.0), [B_cst], [B_cst])
            V("dve", lambda e: e.tensor_scalar(out=dcol[:, 0:8], in0=dcol[:, 24:32], scalar1=-4.0, scalar2=None, op0=ALU.mult), [B_cst], [B_cst])
            V("dve", lambda e: e.tensor_scalar(out=dcol[:, 8:16], in0=dcol[:, 24:32], scalar1=-8.0, scalar2=None, op0=ALU.mult), [B_cst], [B_cst])

            def g(i):
                return s5c[:, 16 * i:16 * i + 16]
            S = [B_cst]

            def reduce_angle(dst, src, n, scr_i, scr_f):
                V("dve", lambda e: e.tensor_scalar(out=scr_i, in0=src, scalar1=1.0 / TWO_PI, scalar2=None, op0=ALU.mult), S + P + [B_tab], S + [B_tab])
                V("dve", lambda e: e.tensor_copy(out=scr_f, in_=scr_i), S + [B_tab], S + [B_tab])
                V("dve", lambda e: e.scalar_tensor_tensor(out=dst, in0=scr_f, scalar=-TWO_PI, in1=src, op0=ALU.mult, op1=ALU.add), S + [B_tab], S + [B_tab])
                V("dve", lambda e: e.tensor_scalar(out=dst, in0=dst, scalar1=-PI, scalar2=PI, op0=ALU.max, op1=ALU.min), S + [B_tab], S + [B_tab])

            def cos_arg(dst, src, scr):
                V("dve", lambda e: e.tensor_scalar(out=dst, in0=src, scalar1=HALF_PI, scalar2=None, op0=ALU.add), S + [B_tab], S + [B_tab])
                V("dve", lambda e: e.tensor_scalar(out=scr, in0=dst, scalar1=PI, scalar2=None, op0=ALU.is_gt), S + [B_tab], S + [B_tab])
                V("dve", lambda e: e.scalar_tensor_tensor(out=dst, in0=scr, scalar=-TWO_PI, in1=dst, op0=ALU.mult, op1=ALU.add), S + [B_tab], S + [B_tab])
                V("dve", lambda e: e.tensor_scalar(out=dst, in0=dst, scalar1=-PI, scalar2=PI, op0=ALU.max, op1=ALU.min), S + [B_tab], S + [B_tab])

            V("act", lambda e: e.activation(out=g(3), in_=g(2), func=AF.Exp), P, S)
            V("dve", lambda e: e.tensor_tensor(out=g(15), in0=g(0), in1=g(3), op=ALU.mult), S, S)
            V("act", lambda e: e.activation(out=g(4), in_=g(15), func=AF.Exp), S, S)
            V("dve", lambda e: e.tensor_tensor(out=g(5), in0=g(1), in1=g(3), op=ALU.mult), S, S)
            reduce_angle(g(6), g(5), 16, s5ci[:], g(15))
            V("act", lambda e: e.activation(out=g(7), in_=g(6), func=AF.Sin), S, S)
            cos_arg(g(16), g(6), g(15))
            V("act", lambda e: e.activation(out=g(8), in_=g(16), func=AF.Sin), S, S)
            V("dve", lambda e: e.tensor_tensor(out=g(9), in0=g(4), in1=g(8), op=ALU.mult), S, S)
            V("dve", lambda e: e.tensor_tensor(out=g(10), in0=g(4), in1=g(7), op=ALU.mult), S, S)
            V("dve", lambda e: e.tensor_tensor(out=g(11), in0=g(0), in1=g(0), op=ALU.mult), S, S)
            V("dve", lambda e: e.tensor_tensor(out=g(15), in0=g(1), in1=g(1), op=ALU.mult), S, S)
            V("dve", lambda e: e.tensor_tensor(out=g(11), in0=g(11), in1=g(15), op=ALU.add), S, S)
            V("dve", lambda e: e.reciprocal(out=g(11), in_=g(11)), S, S)
            V("dve", lambda e: e.tensor_scalar(out=g(12), in0=g(9), scalar1=-1.0, scalar2=None, op0=ALU.add), S, S)
            V("dve", lambda e: e.tensor_tensor(out=g(15), in0=g(12), in1=g(0), op=ALU.mult), S, S)
            V("dve", lambda e: e.tensor_tensor(out=g(16), in0=g(10), in1=g(1), op=ALU.mult), S, S)
            V("dve", lambda e: e.tensor_tensor(out=g(15), in0=g(15), in1=g(16), op=ALU.add), S, S)
            V("dve", lambda e: e.tensor_tensor(out=g(13), in0=g(15), in1=g(11), op=ALU.mult), S, S)
            V("dve", lambda e: e.tensor_tensor(out=g(15), in0=g(10), in1=g(0), op=ALU.mult), S, S)
            V("dve", lambda e: e.tensor_tensor(out=g(16), in0=g(12), in1=g(1), op=ALU.mult), S, S)
            V("dve", lambda e: e.tensor_tensor(out=g(15), in0=g(15), in1=g(16), op=ALU.subtract), S, S)
            V("dve", lambda e: e.tensor_tensor(out=g(14), in0=g(15), in1=g(11), op=ALU.mult), S, S)
            V("dve", lambda e: e.tensor_scalar(out=g(19), in0=g(14), scalar1=-1.0, scalar2=None, op0=ALU.mult), S, S)
            V("dve", lambda e: e.tensor_scalar(out=g(20), in0=g(6), scalar1=float(TC), scalar2=None, op0=ALU.mult), S, S)
            reduce_angle(g(21), g(20), 16, s5ci[:], g(15))
            V("act", lambda e: e.activation(out=g(18), in_=g(21), func=AF.Sin), S, S)
            cos_arg(g(22), g(21), g(15))
            V("act", lambda e: e.activation(out=g(17), in_=g(22), func=AF.Sin), S, S)

            Craw = tmp
            for i in range(16):
                cr_i = Craw[:, i * 32:(i + 1) * 32]
                ci_i = Craw[:, 512 + i * 32:512 + (i + 1) * 32]
                sc = small[:, 16:48]
                fr_c = s5c[:, 16 * 13 + i:16 * 13 + i + 1]
                fi_c = s5c[:, 16 * 14 + i:16 * 14 + i + 1]
                nfi_c = s5c[:, 16 * 19 + i:16 * 19 + i + 1]
                o_re = Csb[:, i * 32:(i + 1) * 32]
                o_im = Csb[:, 512 + i * 32:512 + (i + 1) * 32]
                V("dve", lambda e, ci_i=ci_i, fi_c=fi_c, sc=sc: e.tensor_scalar(out=sc, in0=ci_i, scalar1=fi_c, scalar2=None, op0=ALU.mult), S + [B_tmp[0], B_tmp[1]], S)
                V("dve", lambda e, cr_i=cr_i, fr_c=fr_c, sc=sc, o_re=o_re: e.scalar_tensor_tensor(out=o_re, in0=cr_i, scalar=fr_c, in1=sc, op0=ALU.mult, op1=ALU.subtract), S + [B_tmp[0], B_tmp[1]], S + [B_tab])
                V("dve", lambda e, ci_i=ci_i, fr_c=fr_c, sc=sc: e.tensor_scalar(out=sc, in0=ci_i, scalar1=fr_c, scalar2=None, op0=ALU.mult), S + [B_tmp[0], B_tmp[1]], S)
                V("dve", lambda e, cr_i=cr_i, nfi_c=nfi_c, sc=sc, o_im=o_im: e.scalar_tensor_tensor(out=o_im, in0=cr_i, scalar=nfi_c, in1=sc, op0=ALU.mult, op1=ALU.subtract), S + [B_tmp[0], B_tmp[1]], S + [B_tab])

            iota_sb = x_fm[:, 0:16 * TC]
            ph = hid[:].bitcast(F32)[:, 0:16 * TC]
            phi_i = tmp[:, 0:16 * TC].bitcast(I32)
            TT = B_xfm + B_hid + [B_tab]
            for i in range(16):
                thr_c = s5c[:, 16 * 6 + i:16 * 6 + i + 1]
                V("dve", lambda e, i=i, thr_c=thr_c: e.tensor_scalar(out=ph[:, i * TC:(i + 1) * TC], in0=iota_sb[:, i * TC:(i + 1) * TC], scalar1=thr_c, scalar2=None, op0=ALU.mult), S + TT, TT)
            scr_f = x_fm[:, 0:16 * TC]
            TT2 = TT + B_tmp[0:8]
            V("dve", lambda e: e.tensor_scalar(out=phi_i, in0=ph, scalar1=1.0 / TWO_PI, scalar2=None, op0=ALU.mult), S + TT2, TT2)
            V("dve", lambda e: e.tensor_copy(out=scr_f, in_=phi_i), TT2, TT2)
            V("dve", lambda e: e.scalar_tensor_tensor(out=ph, in0=scr_f, scalar=-TWO_PI, in1=ph, op0=ALU.mult, op1=ALU.add), TT2, TT2)
            V("dve", lambda e: e.tensor_scalar(out=ph, in0=ph, scalar1=-PI, scalar2=PI, op0=ALU.max, op1=ALU.min), TT2, TT2)
            V("act", lambda e: e.activation(out=Ei[:], in_=ph, func=AF.Sin), TT2, TT2)
            V("dve", lambda e: e.tensor_scalar(out=ph, in0=ph, scalar1=HALF_PI, scalar2=None, op0=ALU.add), TT2, TT2)
            V("dve", lambda e: e.tensor_scalar(out=scr_f, in0=ph, scalar1=PI, scalar2=None, op0=ALU.is_gt), TT2, TT2)
            V("dve", lambda e: e.scalar_tensor_tensor(out=ph, in0=scr_f, scalar=-TWO_PI, in1=ph, op0=ALU.mult, op1=ALU.add), TT2, TT2)
            V("dve", lambda e: e.tensor_scalar(out=ph, in0=ph, scalar1=-PI, scalar2=PI, op0=ALU.max, op1=ALU.min), TT2, TT2)
            V("act", lambda e: e.activation(out=Er[:], in_=ph, func=AF.Sin), TT2, TT2)
            TAB = [B_tab, B_cst]
            dbg("Er", Er[:], TAB)
            dbg("Ei", Ei[:], TAB)
            dbg("s5c", s5c[:], TAB)
            dbg("Csb", Csb[:], TAB)
            dbg("dcol", dcol[:], TAB)

            def rmsnorm_fm(gc0, dst, dstB, dst_stride):
                bn = bankF()
                for k in range(8):
                    sq = sqb[:, (k % 2) * TB:(k % 2 + 1) * TB]
                    V("act", lambda e, k=k, sq=sq: e.activation(out=sq, in_=x_fm[:, k * TB:(k + 1) * TB], func=AF.Square), [B_xfm[k]], [B_sqF[k % 2]])
                    V("pe", lambda e, k=k, sq=sq, bn=bn: e.matmul(ps[:, bn, :], lhsT=ones_bf[:], rhs=sq, start=(k == 0), stop=(k == 7)), [B_sqF[k % 2], B_cst], [B_ps[bn]])
                ti = tmpf()
                rt = T(ti)
                V("act", lambda e: e.activation(out=rt, in_=ps[:, bn, :], func=AF.Sqrt, bias=eps_col, scale=1.0), [B_ps[bn], B_cst], [B_tmp[ti]])
                yield "Y"
                V("dve", lambda e: e.reciprocal(out=rt, in_=rt), [B_tmp[ti]], [B_tmp[ti]])
                for k in range(8):
                    V("dve", lambda e, k=k: e.scalar_tensor_tensor(out=dst[:, k * dst_stride:k * dst_stride + TB], in0=x_fm[:, k * TB:(k + 1) * TB], scalar=col(gc0 + k), in1=rt, op0=ALU.mult, op1=ALU.mult),
                      [B_xfm[k], B_tmp[ti]] + P, [dstB[k]])

            def mm_group(bn, si, cols, c0, kt, rhs_fn, rhsB, extraB=()):
                for k in range(kt):
                    V("pe", lambda e, k=k: e.matmul(ps[:, bn, :], lhsT=slab_lhsT(si, cols, k, c0), rhs=rhs_fn(k), start=(k == 0), stop=(k == kt - 1)),
                      [B_slot[si[1]], rhsB[k]] + list(extraB), [B_ps[bn]])

            A2S = TB + 3

            def h_k(k):
                return A1[:, k * TB:(k + 1) * TB]

            A2S = TB + 3
            a2v = A2[:].rearrange("p (k t) -> p k t", t=A2S)

            def h_k(k):
                return A1[:, k * TB:(k + 1) * TB]

            def S_thread(b):
                t0 = b * TB
                def SA_a(tt):
                    xs = T(2 * (tt % 2), 2)
                    Bxs = [B_tmp[2 * (tt % 2)], B_tmp[2 * (tt % 2) + 1]]
                    em.dma("sp", f"dx{tt % 2}", [(xs, x_d[t0 + tt * 128:t0 + (tt + 1) * 128, :])], writes=Bxs)
                    xn = A2[:, tt * 1024:(tt + 1) * 1024]
                    ssx, rsx = small[:, 72 + 2 * tt:73 + 2 * tt], small[:, 73 + 2 * tt:74 + 2 * tt]
                    V("dve", lambda e: e.memset(ssx, 0.0), [B_small], [B_small])
                    V("act", lambda e: e.activation(out=xn, in_=xs, func=AF.Square, accum_out=ssx), Bxs + [B_small], list(B_A2) + [B_small])
                    V("dve", lambda e: e.tensor_scalar(out=rsx, in0=ssx, scalar1=1.0 / D, scalar2=EPS, op0=ALU.mult, op1=ALU.add), [B_small], [B_small])

                def SA_b(tt):
                    xs = T(2 * (tt % 2), 2)
                    Bxs = [B_tmp[2 * (tt % 2)], B_tmp[2 * (tt % 2) + 1]]
                    xn = A2[:, tt * 1024:(tt + 1) * 1024]
                    Bxn = list(B_A2)
                    rsx = small[:, 73 + 2 * tt:74 + 2 * tt]
                    V("act", lambda e: e.activation(out=rsx, in_=rsx, func=AF.Sqrt), [B_small], [B_small])
                    V("dve", lambda e: e.reciprocal(out=rsx, in_=rsx), [B_small], [B_small])
                    V("act", lambda e: e.activation(out=xn, in_=xs, func=AF.Copy, scale=rsx), Bxs + [B_small], Bxn)
                    for hh in range(2):
                        bn = bankS()
                        pb = ps[:, bn, :].bitcast(BF16)
                        for kk in range(4):
                            k = hh * 4 + kk
                            V("pe", lambda e, k=k, kk=kk, pb=pb: e.transpose(out=pb[:, kk * 128:(kk + 1) * 128], in_=xn[:, k * 128:(k + 1) * 128], identity=ident_bf[:]), Bxn + P, [B_ps[bn]])
                        gcol3 = colp[:, C_GMIX + hh * 4:C_GMIX + hh * 4 + 4].unsqueeze(2).broadcast_to([128, 4, 128])
                        dst3 = A1[:, hh * 4 * TB:(hh * 4 + 4) * TB].rearrange("p (k t) -> p k t", t=TB)[:, :, tt * 128:(tt + 1) * 128]
                        src3 = pb[:, 0:512].rearrange("p (k t) -> p k t", t=128)
                        V("dve", lambda e, dst3=dst3, src3=src3, gcol3=gcol3: e.tensor_tensor(out=dst3, in0=src3, in1=gcol3, op=ALU.mult), [B_ps[bn]] + P, [B_A1[hh * 4 + kk] for kk in range(4)])

                for step in ("a0", "a1", "b0", "a2", "b1", "a3", "b2", "b3"):
                    (SA_a if step[0] == "a" else SA_b)(int(step[1]))
                    if step[0] == "b":
                        yield 3.0
                if b == 0:
                    dbg("h", A1[:], B_A1)
                def SC_slab(s):
                    si, cols = get_slab(w_in_d, 0, 8, s * 512, 512)
                    for j in range(4):
                        oc = s * 4 + j
                        bn = bankS() if s == 0 else LBANKS[0]
                        mm_group(bn, si, cols, j * 128, 8, h_k, B_A1)
                        if oc < 4:
                            dst, dB = A4[:, oc * TB:(oc + 1) * TB], B_A4[oc]
                        else:
                            t = oc - 4
                            dst, dB = A2[:, t * A2S + 3:t * A2S + 3 + TB], B_A2[t]
                        V("act", lambda e, dst=dst, bn=bn, oc=oc: e.activation(out=dst, in_=ps[:, bn, :], func=AF.Identity, bias=col(C_BIN + oc), scale=1.0), [B_ps[bn]] + P, [dB])
                    release_slab(si)

                SC_slab(0)
                yield 8.0

                def SC_rest():
                    for s in (1, 2):
                        SC_slab(s)
                        yield 8.0
                    if b > 0:
                        V("dve", lambda e: e.tensor_copy(out=a2v[:, :, 0:3], in_=small[:, 48:48 + 24].rearrange("p (k t) -> p k t", t=3)), B_A2 + [B_small], B_A2)
                    else:
                        V("dve", lambda e: e.memset(a2v[:, :, 0:3], 0.0), B_A2, B_A2)
                if b <= 1:
                    dbg(f"ua{b}", A4[:], B_A4)
                    dbg(f"ub{b}", A2[:], B_A2)

                def LRU_gen():
                    for t in range(8):
                        ub = A2[:, t * A2S:(t + 1) * A2S]
                        xc = xcb[:, 0:TB]
                        i0, i1, i2, i3 = tmpl(), tmpl(), tmpl(), tmpl()
                        i4 = i0
                        acc = T(i0)
                        V("dve", lambda e, ub=ub, acc=acc, t=t: e.tensor_scalar(out=acc, in0=ub[:, 0:TB], scalar1=col(C_CONVW + 0 * 8 + t), scalar2=col(C_CONVB + t), op0=ALU.mult, op1=ALU.add), [B_A2[t]] + P, [B_tmp[i0]])
                        V("dve", lambda e, ub=ub, acc=acc, t=t: e.scalar_tensor_tensor(out=acc, in0=ub[:, 1:TB + 1], scalar=col(C_CONVW + 1 * 8 + t), in1=acc, op0=ALU.mult, op1=ALU.add), [B_A2[t], B_tmp[i0]] + P, [B_tmp[i0]])
                        V("dve", lambda e, ub=ub, acc=acc, t=t: e.scalar_tensor_tensor(out=acc, in0=ub[:, 2:TB + 2], scalar=col(C_CONVW + 2 * 8 + t), in1=acc, op0=ALU.mult, op1=ALU.add), [B_A2[t], B_tmp[i0]] + P, [B_tmp[i0]])
                        V("dve", lambda e, ub=ub, acc=acc, t=t, xc=xc: e.scalar_tensor_tensor(out=xc, in0=ub[:, 3:TB + 3], scalar=col(C_CONVW + 3 * 8 + t), in1=acc, op0=ALU.mult, op1=ALU.add), [B_A2[t], B_tmp[i0]] + P, [B_xc[0]])
                        br_, bi_ = LBANKS[0], LBANKS[1]
                        tr_, ti_, a_, a2_ = T(i1), T(i2), T(i3), T(i4)
                        V("pe", lambda e, t=t, xc=xc, br_=br_: e.matmul(ps[:, br_, :], lhsT=wr_sb[:, t * 128:(t + 1) * 128], rhs=xc, start=True, stop=True), [B_xc[0]] + P, [B_ps[br_]])
                        V("act", lambda e, tr_=tr_, br_=br_, t=t: e.activation(out=tr_, in_=ps[:, br_, :], func=AF.Tanh, bias=hc(C_BR + t), scale=0.5), [B_ps[br_], B_cst], [B_tmp[i1]])
                        V("pe", lambda e, t=t, xc=xc, bi_=bi_: e.matmul(ps[:, bi_, :], lhsT=wi_sb[:, t * 128:(t + 1) * 128], rhs=xc, start=True, stop=True), [B_xc[0]] + P, [B_ps[bi_]])
                        V("act", lambda e, ti_=ti_, bi_=bi_, t=t: e.activation(out=ti_, in_=ps[:, bi_, :], func=AF.Tanh, bias=hc(C_BI + t), scale=0.5), [B_ps[bi_], B_cst], [B_tmp[i2]])
                        V("act", lambda e, tr_=tr_, a_=a_, t=t: e.activation(out=a_, in_=tr_, func=AF.Exp, bias=dcol[:, t:t + 1], scale=dcol[:, t:t + 1]), [B_tmp[i1], B_cst], [B_tmp[i3]])
                        V("act", lambda e, tr_=tr_, a2_=a2_, t=t: e.activation(out=a2_, in_=tr_, func=AF.Exp, bias=dcol[:, 8 + t:9 + t], scale=dcol[:, 8 + t:9 + t]), [B_tmp[i1], B_cst], [B_tmp[i4]])
                        yield 3.0
                        V("dve", lambda e, ti_=ti_, xc=xc: e.scalar_tensor_tensor(out=ti_, in0=ti_, scalar=1.0, in1=xc, op0=ALU.add, op1=ALU.mult), [B_tmp[i2], B_xc[0]], [B_tmp[i2]])
                        V("act", lambda e, a2_=a2_: e.activation(out=a2_, in_=a2_, func=AF.Relu, bias=one_col, scale=-1.0), [B_tmp[i4], B_cst], [B_tmp[i4]])
                        V("act", lambda e, a2_=a2_: e.activation(out=a2_, in_=a2_, func=AF.Sqrt), [B_tmp[i4]], [B_tmp[i4]])
                        yield 0.7
                        V("dve", lambda e, ti_=ti_, a2_=a2_: e.tensor_tensor(out=ti_, in0=ti_, in1=a2_, op=ALU.mult), [B_tmp[i2], B_tmp[i4]], [B_tmp[i2]])
                        V("dve", lambda e, tr_=tr_, a_=a_, ti_=ti_, t=t: e.tensor_tensor_scan(out=tr_, data0=a_, data1=ti_, initial=carry_lru[:, t:t + 1], op0=ALU.mult, op1=ALU.add),
                          [B_tmp[i3], B_tmp[i2], B_clru], [B_tmp[i1]])
                        V("dve", lambda e, tr_=tr_, t=t: e.tensor_copy(out=carry_lru[:, t:t + 1], in_=tr_[:, TB - 1:TB]), [B_tmp[i1]], [B_clru])
                        V("act", lambda e, tr_=tr_, t=t: e.activation(out=A3[:, t * TB:(t + 1) * TB], in_=tr_, func=AF.Copy), [B_tmp[i1]], [B_A3[t]])
                        yield 2.2
                    V("dve", lambda e: e.tensor_copy(out=small[:, 48:48 + 24].rearrange("p (k t) -> p k t", t=3), in_=a2v[:, :, TB:TB + 3]), B_A2, [B_small])
                    if b <= 1:
                        dbg(f"yb{b}", A3[:], B_A3)

                yb = SBANKS[3]

                def v3(ap):
                    return ap.rearrange("p (s t) -> p s t", t=TC)

                def emit_bmm(i):
                    f_ = i // 4
                    ua_ = A4[:, f_ * TB:(f_ + 1) * TB]
                    V("pe", lambda e: e.matmul(ps[:, XBANKS[0], :], lhsT=Bsb[:, i * 128:(i + 1) * 128], rhs=ua_, start=True, stop=True), [B_A4[f_]] + P, [B_ps[XBANKS[0]]])
                    V("pe", lambda e: e.matmul(ps[:, XBANKS[1], :], lhsT=Bsb[:, 2048 + i * 128:2048 + (i + 1) * 128], rhs=ua_, start=True, stop=True), [B_A4[f_]] + P, [B_ps[XBANKS[1]]])

                def S5_gen():
                    for f in range(4):
                        pend = []
                        ua = A4[:, f * TB:(f + 1) * TB]
                        for q in range(4):
                            i = 4 * f + q
                            bxr, bxi = XBANKS[0], XBANKS[1]
                            if i == 0:
                                emit_bmm(0)
                            er3 = Er[:, i * TC:(i + 1) * TC].unsqueeze(1).broadcast_to([128, NSUB, TC])
                            ei3 = Ei[:, i * TC:(i + 1) * TC].unsqueeze(1).broadcast_to([128, NSUB, TC])
                            j1, j2, j3, j4 = tmpi(), tmpi(), tmpi(), tmpi()
                            t1, t2, t3, t4 = T(j1), T(j2), T(j3), T(j4)
                            pxr, pxi = ps[:, bxr, :], ps[:, bxi, :]
                            V("dve", lambda e, t1=t1, pxr=pxr, er3=er3: e.tensor_tensor(out=v3(t1), in0=v3(pxr), in1=er3, op=ALU.mult), [B_ps[bxr]] + TAB, [B_tmp[j1]])
                            V("dve", lambda e, t2=t2, pxi=pxi, ei3=ei3: e.tensor_tensor(out=v3(t2), in0=v3(pxi), in1=ei3, op=ALU.mult), [B_ps[bxi]] + TAB, [B_tmp[j2]])
                            V("dve", lambda e, t3=t3, pxi=pxi, er3=er3: e.tensor_tensor(out=v3(t3), in0=v3(pxi), in1=er3, op=ALU.mult), [B_ps[bxi]] + TAB, [B_tmp[j3]])
                            V("dve", lambda e, t4=t4, pxr=pxr, ei3=ei3: e.tensor_tensor(out=v3(t4), in0=v3(pxr), in1=ei3, op=ALU.mult), [B_ps[bxr]] + TAB, [B_tmp[j4]])
                            if i + 1 < 16:
                                emit_bmm(i + 1)
                            V("dve", lambda e, t1=t1, t2=t2: e.tensor_tensor(out=t1, in0=t1, in1=t2, op=ALU.add), [B_tmp[j1], B_tmp[j2]], [B_tmp[j1]])
                            V("dve", lambda e, t3=t3, t4=t4: e.tensor_tensor(out=t3, in0=t3, in1=t4, op=ALU.subtract), [B_tmp[j3], B_tmp[j4]], [B_tmp[j3]])
                            m_c = s5c[:, 16 * 4 + i:16 * 4 + i + 1]
                            ebr = s5c[:, 16 * 17 + i:16 * 17 + i + 1]
                            ebi = s5c[:, 16 * 18 + i:16 * 18 + i + 1]
                            m_bc = m_c.broadcast_to([128, TC])
                            for sc in range(NSUB):
                                if sc == 0:
                                    lr_, li_ = carry_s5[:, i:i + 1], carry_s5[:, 16 + i:17 + i]
                                    lB = [B_cs5]
                                else:
                                    lr_, li_ = t2[:, sc * TC - 1:sc * TC], t4[:, sc * TC - 1:sc * TC]
                                    lB = [B_tmp[j2], B_tmp[j4]]
                                o_ = 16 + 4 * (sc % 2)
                                ir, ii, sx1, sx2 = small[:, o_:o_ + 1], small[:, o_ + 1:o_ + 2], small[:, o_ + 2:o_ + 3], small[:, o_ + 3:o_ + 4]
                                Bq = B_s5q[sc % 2]
                                V("dve", lambda e, li_=li_, ebi=ebi, sx1=sx1: e.tensor_scalar(out=sx1, in0=li_, scalar1=ebi, scalar2=None, op0=ALU.mult), lB + TAB, [Bq[0]])
                                V("dve", lambda e, lr_=lr_, ebi=ebi, sx2=sx2: e.tensor_scalar(out=sx2, in0=lr_, scalar1=ebi, scalar2=None, op0=ALU.mult), lB + TAB, [Bq[1]])
                                V("dve", lambda e, lr_=lr_, ebr=ebr, sx1=sx1, ir=ir: e.scalar_tensor_tensor(out=ir, in0=lr_, scalar=ebr, in1=sx1, op0=ALU.mult, op1=ALU.subtract), lB + TAB + [Bq[0]], [Bq[2]])
                                V("dve", lambda e, li_=li_, ebr=ebr, sx2=sx2, ii=ii: e.scalar_tensor_tensor(out=ii, in0=li_, scalar=ebr, in1=sx2, op0=ALU.mult, op1=ALU.add), lB + TAB + [Bq[1]], [Bq[3]])
                                sl = slice(sc * TC, (sc + 1) * TC)
                                V("dve", lambda e, t2=t2, t1=t1, sl=sl, ir=ir, m_bc=m_bc: e.tensor_tensor_scan(out=t2[:, sl], data0=m_bc, data1=t1[:, sl], initial=ir, op0=ALU.mult, op1=ALU.add), [B_tmp[j1], Bq[2]] + TAB, [B_tmp[j2]])
                                V("dve", lambda e, t4=t4, t3=t3, sl=sl, ii=ii, m_bc=m_bc: e.tensor_tensor_scan(out=t4[:, sl], data0=m_bc, data1=t3[:, sl], initial=ii, op0=ALU.mult, op1=ALU.add), [B_tmp[j3], Bq[3]] + TAB, [B_tmp[j4]])
                            V("dve", lambda e, t2=t2, i=i: e.tensor_copy(out=carry_s5[:, i:i + 1], in_=t2[:, TB - 1:TB]), [B_tmp[j2]], [B_cs5])
                            V("dve", lambda e, t4=t4, i=i: e.tensor_copy(out=carry_s5[:, 16 + i:17 + i], in_=t4[:, TB - 1:TB]), [B_tmp[j4]], [B_cs5])
                            yield 13.0
                            so = (i % 2) * 2
                            s_re = sbf[:, so * TB:(so + 1) * TB]
                            s_im = sbf[:, (so + 1) * TB:(so + 2) * TB]
                            V("dve", lambda e, t1=t1, t2=t2, er3=er3: e.tensor_tensor(out=v3(t1), in0=v3(t2), in1=er3, op=ALU.mult), [B_tmp[j2]] + TAB, [B_tmp[j1]])
                            V("dve", lambda e, t3=t3, t4=t4, ei3=ei3: e.tensor_tensor(out=v3(t3), in0=v3(t4), in1=ei3, op=ALU.mult), [B_tmp[j4]] + TAB, [B_tmp[j3]])
                            V("dve", lambda e, t2=t2, ei3=ei3: e.tensor_tensor(out=v3(t2), in0=v3(t2), in1=ei3, op=ALU.mult), [B_tmp[j2]] + TAB, [B_tmp[j2]])
                            V("dve", lambda e, t4=t4, er3=er3: e.tensor_tensor(out=v3(t4), in0=v3(t4), in1=er3, op=ALU.mult), [B_tmp[j4]] + TAB, [B_tmp[j4]])
                            V("dve", lambda e, t1=t1, t3=t3, s_re=s_re: e.tensor_tensor(out=s_re, in0=t1, in1=t3, op=ALU.subtract), [B_tmp[j1], B_tmp[j3]], [B_sbf[so]])
                            V("dve", lambda e, t2=t2, t4=t4, s_im=s_im: e.tensor_tensor(out=s_im, in0=t4, in1=t2, op=ALU.add), [B_tmp[j2], B_tmp[j4]], [B_sbf[so + 1]])
                            for fn_ in pend:
                                fn_()
                            pend = []

                            def cmm(i=i, q=q, s_re=s_re, s_im=s_im, so=so):
                                V("pe", lambda e: e.matmul(ps[32 * q:32 * q + 32, yb, :], lhsT=Csb[:, i * 32:(i + 1) * 32], rhs=s_re, start=True, stop=False, tile_position=(0, 32 * q)), [B_sbf[so]] + TAB, [B_ps[yb]])
                                V("pe", lambda e: e.matmul(ps[32 * q:32 * q + 32, yb, :], lhsT=Csb[:, 512 + i * 32:512 + (i + 1) * 32], rhs=s_im, start=False, stop=True, tile_position=(0, 32 * q)), [B_sbf[so + 1]] + TAB, [B_ps[yb]])
                            pend.append(cmm)
                            if q == 3:
                                for fn_ in pend:
                                    fn_()
                                pend = []
                                k1, k2 = j1, j3
                                y_, w_ = T(k1), T(k2)
                                V("dve", lambda e, f=f, y_=y_: e.scalar_tensor_tensor(out=y_, in0=A4[:, f * TB:(f + 1) * TB], scalar=col(C_S5D + f), in1=ps[:, yb, :], op0=ALU.mult, op1=ALU.add), [B_A4[f], B_ps[yb]] + P, [B_tmp[k1]])
                                V("act", lambda e, f=f, y_=y_: e.activation(out=A4[:, f * TB:(f + 1) * TB], in_=y_, func=AF.Gelu_apprx_tanh), [B_tmp[k1]], [B_A4[f]])
                            yield 5.0

                gC, gL, gS5 = SC_rest(), LRU_gen(), S5_gen()
                dC = dL = dS5 = False
                while not (dL and dS5):
                    adv = False
                    if not dC:
                        adv = True
                        try:
                            c_ = next(gC)
                            yield c_
                        except StopIteration:
                            dC = True
                    elif not dL and prog["merged"] >= b - 1:
                        adv = True
                        try:
                            c_ = next(gL)
                            yield c_
                        except StopIteration:
                            dL = True
                    if not dS5:
                        adv = True
                        try:
                            c_ = next(gS5)
                            yield c_
                        except StopIteration:
                            dS5 = True
                    if not adv:
                        yield "W"
                if b <= 1:
                    dbg(f"z{b}", A4[:], B_A4)
                while prog["down"] < b - 1:
                    yield "W"
                for f in range(4):
                    bn = bankS()
                    for k in range(4):
                        V("pe", lambda e, f=f, k=k, bn=bn: e.matmul(ps[:, bn, :], lhsT=w_glu_sb[:, k * 512 + f * 128:k * 512 + (f + 1) * 128], rhs=A4[:, k * TB:(k + 1) * TB], start=(k == 0), stop=(k == 3)), [B_A4[k]] + P, [B_ps[bn]])
                    k1 = tmpi()
                    g_ = T(k1)
                    V("act", lambda e, f=f, bn=bn, g_=g_: e.activation(out=g_, in_=ps[:, bn, :], func=AF.Tanh, bias=hc(C_BGLU + f), scale=0.5), [B_ps[bn], B_cst], [B_tmp[k1]])
                    V("dve", lambda e, f=f, g_=g_: e.scalar_tensor_tensor(out=A6[:, f * TB:(f + 1) * TB], in0=g_, scalar=1.0, in1=A4[:, f * TB:(f + 1) * TB], op0=ALU.add, op1=ALU.mult), [B_tmp[k1], B_A4[f]], [B_A6[f]])
                yield 6.0
                if b <= 1:
                    dbg(f"ya{b}", A6[:], B_A6)

            def F_head(b):
                t0 = b * TB
                fbk = lambda k: FB[:, k * TB:(k + 1) * TB]
                for tt in range(4):
                    pi_ = tmpf(2)
                    xs = T(pi_, 2)
                    Bxs = [B_tmp[pi_], B_tmp[pi_ + 1]]
                    em.dma("sp", f"dxf{(pi_ - NTS) // 2}", [(xs, x_d[t0 + tt * 128:t0 + (tt + 1) * 128, :])], writes=Bxs)
                    for hh in range(2):
                        bn = bankF()
                        for kk in range(4):
                            k = hh * 4 + kk
                            V("pe", lambda e, k=k, kk=kk, bn=bn, xs=xs: e.transpose(out=ps[:, bn, kk * 128:(kk + 1) * 128], in_=xs[:, k * 128:(k + 1) * 128], identity=ident[:]),
                              Bxs + P, [B_ps[bn]])
                        dst = x_fm[:, hh * 4 * TB:(hh * 4 + 4) * TB].rearrange("p (k t) -> p k t", t=TB)[:, :, tt * 128:(tt + 1) * 128]
                        src = ps[:, bn, :].rearrange("p (k t) -> p k t", t=128)
                        V("act", lambda e, dst=dst, src=src: e.activation(out=dst, in_=src, func=AF.Copy), [B_ps[bn]], [B_xfm[hh * 4 + kk] for kk in range(4)])
                    yield 3.0
                yield from rmsnorm_fm(C_GMIX, FB, B_FB, TB)
                yield 5.0
                for hf in range(2):
                    sga, cga = get_slab(w_in_d, 0, 8, 1536 + hf * 512, 512)
                    sgb, cgb = get_slab(w_in_d, 0, 8, 2560 + hf * 512, 512)
                    sa, ca = get_slab(w_a_d, 0, 4, hf * 512, 512)
                    sbo, cbo = get_slab(w_b_d, 0, 8, hf * 512, 512)
                    for jj in range(4):
                        j = hf * 4 + jj
                        k1, k2 = tmpf(), tmpf()
                        ga_, gb_ = T(k1), T(k2)
                        bga, bpa, bgb, bpb = bankF(), bankF(), bankF(), bankF()
                        mm_group(bga, sga, cga, jj * 128, 8, fbk, B_FB)
                        V("act", lambda e, ga_=ga_, bga=bga, j=j: e.activation(out=ga_, in_=ps[:, bga, :], func=AF.Tanh, bias=hc(C_BIN + 12 + j), scale=0.5), [B_ps[bga], B_cst], [B_tmp[k1]])
                        mm_group(bpa, sa, ca, jj * 128, 4, lambda k: A6[:, k * TB:(k + 1) * TB], B_A6)
                        mm_group(bgb, sgb, cgb, jj * 128, 8, fbk, B_FB)
                        V("act", lambda e, gb_=gb_, bgb=bgb, j=j: e.activation(out=gb_, in_=ps[:, bgb, :], func=AF.Tanh, bias=hc(C_BIN + 20 + j), scale=0.5), [B_ps[bgb], B_cst], [B_tmp[k2]])
                        mm_group(bpb, sbo, cbo, jj * 128, 8, lambda k: A3[:, k * TB:(k + 1) * TB], B_A3)
                        yield "Y"
                        V("dve", lambda e, ga_=ga_, bpa=bpa: e.scalar_tensor_tensor(out=ga_, in0=ga_, scalar=1.0, in1=ps[:, bpa, :], op0=ALU.add, op1=ALU.mult), [B_tmp[k1], B_ps[bpa]], [B_tmp[k1]])
                        V("dve", lambda e, gb_=gb_, bpb=bpb: e.scalar_tensor_tensor(out=gb_, in0=gb_, scalar=1.0, in1=ps[:, bpb, :], op0=ALU.add, op1=ALU.mult), [B_tmp[k2], B_ps[bpb]], [B_tmp[k2]])
                        V("dve", lambda e, ga_=ga_, gb_=gb_, j=j: e.tensor_tensor(out=hid[:, j * TB:(j + 1) * TB], in0=gb_, in1=ga_, op=ALU.add), [B_tmp[k1], B_tmp[k2]], [B_hid[j]])
                        if jj == 3:
                            release_slab(sga); release_slab(sgb); release_slab(sa); release_slab(sbo)
                        yield 7.5
                prog["merged"] = b
                if b == 0:
                    dbg("merged", hid[:, 0:8 * TB], B_hid[0:8])
                for hf in range(2):
                    si, cols = get_slab(w_o_d, 0, 8, hf * 512, 512)
                    for jj in range(4):
                        j = hf * 4 + jj
                        bn = bankF()
                        mm_group(bn, si, cols, jj * 128, 8, lambda k: hid[:, k * TB:(k + 1) * TB], B_hid)
                        yield "Y"
                        V("dve", lambda e, j=j, bn=bn: e.scalar_tensor_tensor(out=x_fm[:, j * TB:(j + 1) * TB], in0=ps[:, bn, :], scalar=0.25, in1=x_fm[:, j * TB:(j + 1) * TB], op0=ALU.mult, op1=ALU.add), [B_ps[bn], B_xfm[j]], [B_xfm[j]])
                        if jj == 3:
                            release_slab(si)
                        yield 2.2
                if b == 0:
                    dbg("x1", x_fm[:], B_xfm)

            def F_rest(b):
                t0 = b * TB
                fb_k = lambda k: FB[:, k * TB:(k + 1) * TB]
                yield from rmsnorm_fm(C_GFFN, FB, B_FB, TB)
                yield 4.0
                pend_ev = []
                for s in range(6):
                    cols_ = 512 if s < 5 else 256
                    sg, cg = get_slab(w_fg_d, 0, 8, s * 512, cols_)
                    su, cu = get_slab(w_fu_d, 0, 8, s * 512, cols_)
                    for jj in range(cols_ // 128):
                        c = s * 4 + jj
                        bg_, bu_ = bankF(), bankF()
                        mm_group(bg_, sg, cg, jj * 128, 8, fb_k, B_FB)
                        mm_group(bu_, su, cu, jj * 128, 8, fb_k, B_FB)
                        for fn_ in pend_ev:
                            fn_()
                        pend_ev = []

                        def ev(bg_=bg_, bu_=bu_, c=c):
                            k1 = tmpf()
                            tg = T(k1)
                            V("act", lambda e: e.activation(out=tg, in_=ps[:, bg_, :], func=AF.Silu), [B_ps[bg_]], [B_tmp[k1]])
                            V("dve", lambda e: e.tensor_tensor(out=hid[:, c * TB:(c + 1) * TB], in0=tg, in1=ps[:, bu_, :], op=ALU.mult), [B_tmp[k1], B_ps[bu_]], [B_hid[c]])
                        pend_ev.append(ev)
                        if jj == cols_ // 128 - 1:
                            release_slab(sg); release_slab(su)
                        yield 4.3
                for fn_ in pend_ev:
                    fn_()
                pend_ev = []
                for j in range(8):
                    si, cols = get_slab(w_fd_d, 0, 22, j * 128, 128)
                    bn = bankF()
                    for k in range(22):
                        V("pe", lambda e, k=k, si=si, cols=cols, bn=bn: e.matmul(ps[:, bn, :], lhsT=slab_lhsT(si, cols, k, 0), rhs=hid[:, k * TB:(k + 1) * TB], start=(k == 0), stop=(k == 21)), [B_slot[si[1]], B_hid[k]], [B_ps[bn]])
                    for fn_ in pend_ev:
                        fn_()
                    pend_ev = []

                    def evd(j=j, bn=bn):
                        V("dve", lambda e: e.tensor_tensor(out=x_fm[:, j * TB:(j + 1) * TB], in0=ps[:, bn, :], in1=x_fm[:, j * TB:(j + 1) * TB], op=ALU.add), [B_ps[bn], B_xfm[j]], [B_xfm[j]])
                    pend_ev.append(evd)
                    release_slab(si)
                    yield 6.0
                for fn_ in pend_ev:
                    fn_()
                pend_ev = []
                prog["down"] = b
                if b == 0:
                    dbg("x2", x_fm[:], B_xfm)
                yield from rmsnorm_fm(C_GPLEG, FB, B_FB, TB)
                yield 4.0
                pi_ = tmpf(2)
                pstage = T(pi_, 2)
                Bp = [B_tmp[pi_], B_tmp[pi_ + 1]]
                em.dma("sp", "dp", [(pstage.rearrange("p (t c) -> p t c", c=PLE), p_d[t0:t0 + TB, :].rearrange("(t r) c -> r t c", r=128))], writes=Bp)
                for kk in range(2):
                    bn = bankF()
                    for tt in range(4):
                        V("pe", lambda e, kk=kk, tt=tt, bn=bn, pstage=pstage: e.transpose(out=ps[:, bn, tt * 128:(tt + 1) * 128], in_=pstage[:, tt * PLE + kk * 128:tt * PLE + (kk + 1) * 128], identity=ident[:]), Bp + P, [B_ps[bn]])
                    V("act", lambda e, kk=kk, bn=bn: e.activation(out=p_fm[:, kk * TB:(kk + 1) * TB], in_=ps[:, bn, :], func=AF.Copy), [B_ps[bn]], [B_pfm[kk]])
                spl, cpl = get_slab(w_pl_d, 0, 2, 0, 1024)
                spg = [get_slab(w_pg_d, 0, 8, hf * 512, 512) for hf in range(2)]
                for tt in range(4):
                    tsl = slice(tt * 128, (tt + 1) * 128)
                    ei_ = tmpf(2)
                    et = T(ei_, 2)
                    Be = [B_tmp[ei_], B_tmp[ei_ + 1]]
                    ti2 = tmpf(2)
                    tp = T(ti2, 2)
                    Bt = [B_tmp[ti2], B_tmp[ti2 + 1]]
                    for hf in range(2):
                        be = bankF()
                        for kk in range(2):
                            V("pe", lambda e, hf=hf, kk=kk, tsl=tsl, spl=spl, be=be: e.matmul(ps[:, be, :], lhsT=p_fm[:, kk * TB:(kk + 1) * TB][:, tsl], rhs=slots[spl[1]][:, kk * 1024 + hf * 512:kk * 1024 + (hf + 1) * 512], start=(kk == 0), stop=(kk == 1)), [B_pfm[kk], B_slot[spl[1]]], [B_ps[be]])
                        V("act", lambda e, hf=hf, be=be, et=et: e.activation(out=et[:, hf * 512:(hf + 1) * 512], in_=ps[:, be, :], func=AF.Copy), [B_ps[be]], [Be[hf]])
                    sse, rs = small[:, 1:2], small[:, 2:3]
                    V("dve", lambda e, sse=sse: e.memset(sse, 0.0), [B_smallF], [B_smallF])
                    V("act", lambda e, tp=tp, et=et, sse=sse: e.activation(out=tp, in_=et, func=AF.Square, accum_out=sse), Be + [B_smallF], Bt + [B_smallF])
                    yield "Y"
                    V("dve", lambda e, sse=sse, rs=rs: e.tensor_scalar(out=rs, in0=sse, scalar1=1.0 / D, scalar2=EPS, op0=ALU.mult, op1=ALU.add), [B_smallF], [B_smallF])
                    V("act", lambda e, rs=rs: e.activation(out=rs, in_=rs, func=AF.Sqrt), [B_smallF], [B_smallF])
                    V("dve", lambda e, rs=rs: e.reciprocal(out=rs, in_=rs), [B_smallF], [B_smallF])
                    for hf in range(2):
                        hs = slice(hf * 512, (hf + 1) * 512)
                        sgi, cgi = spg[hf]
                        bg_ = bankF()
                        for k in range(8):
                            V("pe", lambda e, k=k, tsl=tsl, sgi=sgi, bg_=bg_: e.matmul(ps[:, bg_, :], lhsT=fb_k(k)[:, tsl], rhs=slots[sgi[1]][:, k * 512:(k + 1) * 512], start=(k == 0), stop=False), [B_FB[k], B_slot[sgi[1]]], [B_ps[bg_]])
                        V("pe", lambda e, hf=hf, bg_=bg_: e.matmul(ps[:, bg_, :], lhsT=ones_row[0:1, :], rhs=bpg_row[0:1, hf * 512:(hf + 1) * 512], start=False, stop=True), P, [B_ps[bg_]])
                        V("act", lambda e, tp=tp, hs=hs, bg_=bg_: e.activation(out=tp[:, hs], in_=ps[:, bg_, :], func=AF.Tanh, scale=0.5), [B_ps[bg_]], [Bt[hf]])
                        bx = bankF()
                        for kk in range(4):
                            k = hf * 4 + kk
                            V("pe", lambda e, k=k, kk=kk, tsl=tsl, bx=bx: e.transpose(out=ps[:, bx, kk * 128:(kk + 1) * 128], in_=x_fm[:, k * TB:(k + 1) * TB][:, tsl], identity=ident[:]), [B_xfm[k]] + P, [B_ps[bx]])
                        yield "Y"
                        V("dve", lambda e, et=et, hs=hs, rs=rs: e.scalar_tensor_tensor(out=et[:, hs], in0=et[:, hs], scalar=rs, in1=gple_bc[:, hs], op0=ALU.mult, op1=ALU.mult), [Be[hf], B_smallF] + P, [Be[hf]])
                        V("dve", lambda e, et=et, tp=tp, hs=hs: e.scalar_tensor_tensor(out=et[:, hs], in0=tp[:, hs], scalar=1.0, in1=et[:, hs], op0=ALU.add, op1=ALU.mult), [Bt[hf], Be[hf]], [Be[hf]])
                        V("dve", lambda e, et=et, hs=hs, bx=bx: e.scalar_tensor_tensor(out=et[:, hs], in0=et[:, hs], scalar=0.5, in1=ps[:, bx, :], op0=ALU.mult, op1=ALU.add), [Be[hf], B_ps[bx]], [Be[hf]])
                    ss3, r3 = small[:, 3:4], small[:, 4:5]
                    V("dve", lambda e, ss3=ss3: e.memset(ss3, 0.0), [B_smallF], [B_smallF])
                    V("act", lambda e, tp=tp, et=et, ss3=ss3: e.activation(out=tp, in_=et, func=AF.Square, accum_out=ss3), Be + [B_smallF], Bt + [B_smallF])
                    yield "Y"
                    V("dve", lambda e, ss3=ss3, r3=r3: e.tensor_scalar(out=r3, in0=ss3, scalar1=1.0 / D, scalar2=EPS, op0=ALU.mult, op1=ALU.add), [B_smallF], [B_smallF])
                    V("act", lambda e, r3=r3: e.activation(out=r3, in_=r3, func=AF.Sqrt), [B_smallF], [B_smallF])
                    V("dve", lambda e, r3=r3: e.reciprocal(out=r3, in_=r3), [B_smallF], [B_smallF])
                    V("dve", lambda e, et=et, r3=r3: e.scalar_tensor_tensor(out=et, in0=et, scalar=r3, in1=gfin_bc[:], op0=ALU.mult, op1=ALU.mult), Be + [B_smallF] + P, Be)
                    oi = ei_ - NTS
                    em.dma("sp", f"do{oi}", [(out_d[t0 + tt * 128:t0 + (tt + 1) * 128, :], et)], reads=Be, writes=[B_out[oi]])
                    if tt == 3:
                        release_slab(spl); release_slab(spg[0][0]); release_slab(spg[1][0])
                    yield 7.5

            def run_all(g):
                for _ in g:
                    pass

            prog = {"merged": -1, "down": -1}

            def interleave(gF, gS, totF, totS):
                pF = pS = 0.0
                doneF = doneS = False
                s_blocked = False
                force_s = False
                while not (doneF and doneS):
                    pickF = (not doneF) and (doneS or s_blocked or pF / totF <= pS / totS)
                    if force_s and not doneS and not s_blocked:
                        pickF = False
                    force_s = False
                    if pickF:
                        s_blocked = False
                        try:
                            c_ = next(gF)
                            if c_ == "Y":
                                force_s = True
                            else:
                                pF += c_ or 1.0
                        except StopIteration:
                            doneF = True
                    else:
                        try:
                            c_ = next(gS)
                            if c_ == "W":
                                assert not doneF, "scan thread blocked forever"
                                s_blocked = True
                            else:
                                pS += c_ or 1.0
                        except StopIteration:
                            doneS = True

            def F_thread(b):
                yield from F_head(b)
                yield from F_rest(b)

            run_all(S_thread(0))
            for b in range(NB):
                if b + 1 < NB and PIPELINE:
                    interleave(F_thread(b), S_thread(b + 1), 330.0, 400.0)
                else:
                    run_all(F_thread(b))
                    if b + 1 < NB:
                        run_all(S_thread(b + 1))
            em.final_wait("sp", B_out + dbg_bufs)
            return plan_rec


        dry = Em(eng_names, sems)
        plan0 = program(dry, None, False)
        em = Em(eng_names, sems)
        program(em, plan0, debug)

        def replay(name):
            def f(e):
                for t in em.thunks[name]:
                    t(e)
            return f
        block.sync(replay("sp"))
        block.scalar(replay("act"))
        block.vector(replay("dve"))
        block.gpsimd(replay("pool"))
        block.tensor(replay("pe"))
    return nc


def _host_layout(inp):
    f = np.float32
    sq = lambda a: np.ascontiguousarray(np.asarray(a, dtype=f)[0])
    colp = np.zeros((128, NCOL), f)

    def put(c0, vec):
        v = np.asarray(vec, dtype=f).reshape(-1, 128)
        colp[:, c0:c0 + v.shape[0]] = v.T
    put(C_GMIX, sq(inp["g_mix"]))
    put(C_BIN, sq(inp["b_in"]))
    put(C_S5D, sq(inp["s5_d"]).reshape(-1))
    put(C_BGLU, sq(inp["b_glu"]))
    cw = sq(inp["conv_w"])
    for k in range(4):
        put(C_CONVW + 8 * k, cw[k])
    put(C_CONVB, sq(inp["conv_b"]))
    put(C_BR, sq(inp["b_r"]).reshape(-1))
    put(C_BI, sq(inp["b_i"]).reshape(-1))
    put(C_LAM, sq(inp["lru_lambda"]))
    put(C_GFFN, sq(inp["g_ffn"]))
    put(C_GPLEG, sq(inp["g_ple_gate"]))
    lam_re, lam_im, log_dt = sq(inp["lam_re"]), sq(inp["lam_im"]), sq(inp["log_dt"])
    s5p = np.zeros((128, 48), f)
    s5p[:, 0:16] = lam_re.reshape(16, 2, 64).transpose(1, 2, 0).reshape(128, 16)
    s5p[:, 16:32] = lam_im.reshape(16, 2, 64).transpose(1, 2, 0).reshape(128, 16)
    s5p[:, 32:48] = np.broadcast_to(log_dt.reshape(16, 2, 1), (16, 2, 64)).transpose(1, 2, 0).reshape(128, 16)
    rowp = np.stack([sq(inp["b_ple_gate"]), sq(inp["g_ple"]), np.asarray(inp["g_final"], dtype=f)], 0)
    w_r, w_i = sq(inp["w_r"]), sq(inp["w_i"])
    wr_bd = np.zeros((128, 8, 128), f)
    wi_bd = np.zeros((128, 8, 128), f)
    for t in range(8):
        for h2 in range(2):
            wr_bd[h2 * 64:(h2 + 1) * 64, t, h2 * 64:(h2 + 1) * 64] = w_r[2 * t + h2]
            wi_bd[h2 * 64:(h2 + 1) * 64, t, h2 * 64:(h2 + 1) * 64] = w_i[2 * t + h2]
    b_re, b_im = sq(inp["s5_b_re"]), sq(inp["s5_b_im"])
    c_re, c_im = sq(inp["s5_c_re"]), sq(inp["s5_c_im"])
    s5B = np.zeros((128, 2, 16, 128), f)
    s5C = np.zeros((128, 2, 16, 32), f)
    for i in range(16):
        q = i % 4
        for g2 in range(2):
            gidx = 2 * i + g2
            r0 = 32 * q + 16 * g2
            s5B[r0:r0 + 16, 0, i, g2 * 64:(g2 + 1) * 64] = b_re[gidx].T
            s5B[r0:r0 + 16, 1, i, g2 * 64:(g2 + 1) * 64] = b_im[gidx].T
            s5C[g2 * 64:(g2 + 1) * 64, 0, i, g2 * 16:(g2 + 1) * 16] = c_re[gidx].T
            s5C[g2 * 64:(g2 + 1) * 64, 1, i, g2 * 16:(g2 + 1) * 16] = c_im[gidx].T
    ident = np.eye(128, dtype=f)
    iota = np.ascontiguousarray(np.broadcast_to(np.arange(TC, dtype=f)[None, None, :], (128, 16, TC))).reshape(128, 16 * TC)
    shared = {
        "w_in": sq(inp["w_in"]), "w_glu": sq(inp["w_glu"]), "w_a_out": sq(inp["w_a_out"]), "w_b_out": sq(inp["w_b_out"]),
        "w_o": sq(inp["w_o"]), "w_ffn_gate": sq(inp["w_ffn_gate"]), "w_ffn_up": sq(inp["w_ffn_up"]), "w_ffn_down": sq(inp["w_ffn_down"]),
        "w_ple_gate": sq(inp["w_ple_gate"]), "w_ple": sq(inp["w_ple"]),
        "colp": colp, "s5p": s5p, "rowp": np.ascontiguousarray(rowp),
        "wr_bd": wr_bd.reshape(128, -1), "wi_bd": wi_bd.reshape(128, -1),
        "s5B": s5B.reshape(128, -1), "s5C": s5C.reshape(128, -1), "ident": ident, "iota": iota,
    }
    return shared


def kernel(**inputs):
    x = np.asarray(inputs["x"], dtype=np.float32)
    p = np.asarray(inputs["p"], dtype=np.float32)[0]
    shared = _host_layout(inputs)
    nc = build_nc()
    in_maps = []
    for c in range(8):
        m = dict(shared)
        m["x"] = np.ascontiguousarray(x[c])
        m["p"] = np.ascontiguousarray(p[c])
        in_maps.append(m)
    res = run_bass_kernel_spmd(nc, in_maps, core_ids=list(range(8)))
    return np.stack([np.asarray(r["out"], dtype=np.float32) for r in res.results], 0)
```

```python
import math
from contextlib import ExitStack

import numpy as np
import concourse.bass as bass
import concourse.mybir as mybir
from concourse.bass_utils import run_bass_kernel_spmd

F32 = mybir.dt.float32
BF16 = mybir.dt.bfloat16
I32 = mybir.dt.int32
AF = mybir.ActivationFunctionType
ALU = mybir.AluOpType

D = 1024
L = 2048
TB = 512
NB = L // TB
PLE = 256
S5W = 512
FFN = 2816
INC = 3584
TC = 256
ROT_ENG = "dve"
PIPELINE = True
NSUB = TB // TC
NSLOT = 5
SLOT_ELEMS = 4096
EPS = 1e-6
TWO_PI = float(2 * math.pi)
PI = 3.1415925
HALF_PI = float(math.pi / 2)

C_GMIX, C_BIN, C_S5D, C_BGLU, C_CONVW, C_CONVB, C_BR, C_BI, C_LAM, C_GFFN, C_GPLEG, NCOL = 0, 8, 36, 40, 44, 76, 84, 92, 100, 108, 116, 124


class Buf:
    __slots__ = ("name", "w", "r")

    def __init__(self, name=""):
        self.name = name
        self.w = None
        self.r = {}


class Em:
    def __init__(self, engines, sems):
        self.sem = sems
        self.cnt = {k: 0 for k in sems}
        self.waited = {e: {} for e in engines}
        self.thunks = {e: [] for e in engines}
        self.self_sync = True

    def _waits(self, eng_name, reads, writes):
        deps = {}

        def add(d):
            if d is None:
                return
            k, v = d
            if deps.get(k, 0) < v:
                deps[k] = v
        for b in reads:
            add(b.w)
        for b in writes:
            add(b.w)
            for d in b.r.items():
                add(d)
        for k, v in deps.items():
            if k == eng_name and (eng_name == "pe" or not self.self_sync):
                continue
            if self.waited[eng_name].get(k, 0) >= v:
                continue
            self.thunks[eng_name].append(lambda e, s=self.sem[k], v=v: e.wait_ge(s, v))
            self.waited[eng_name][k] = v

    def _commit(self, me, reads, writes):
        for b in reads:
            if b.r.get(me[0], 0) < me[1]:
                b.r[me[0]] = me[1]
        for b in writes:
            b.w = me
            b.r = {}

    def op(self, eng_name, fn, reads=(), writes=()):
        self._waits(eng_name, reads, writes)
        self.cnt[eng_name] += 1
        self.thunks[eng_name].append(lambda e, fn=fn, s=self.sem[eng_name]: fn(e).then_inc(s, 1))
        self._commit((eng_name, self.cnt[eng_name]), reads, writes)

    def dma(self, queue, sem_key, pairs, reads=(), writes=()):
        self._waits(queue, reads, writes)
        for (o, i) in pairs:
            self.cnt[sem_key] += 16
            self.thunks[queue].append(lambda e, o=o, i=i, s=self.sem[sem_key]: e.dma_start(out=o, in_=i).then_inc(s, 16))
        self._commit((sem_key, self.cnt[sem_key]), reads, writes)

    def final_wait(self, eng_name, bufs):
        self._waits(eng_name, bufs, bufs)


def build_nc(debug=False):
    nc = bass.Bass("TRN2", target_bir_lowering=False)

    def din(name, shape):
        return nc.dram_tensor(name, list(shape), F32, kind="ExternalInput").ap()

    x_d = din("x", [L, D])
    p_d = din("p", [L, PLE])
    w_in_d = din("w_in", [D, INC])
    w_glu_d = din("w_glu", [S5W, S5W])
    w_a_d = din("w_a_out", [S5W, D])
    w_b_d = din("w_b_out", [D, D])
    w_o_d = din("w_o", [D, D])
    w_fg_d = din("w_ffn_gate", [D, FFN])
    w_fu_d = din("w_ffn_up", [D, FFN])
    w_fd_d = din("w_ffn_down", [FFN, D])
    w_pg_d = din("w_ple_gate", [D, D])
    w_pl_d = din("w_ple", [PLE, D])
    colp_d = din("colp", [128, NCOL])
    s5p_d = din("s5p", [128, 48])
    rowp_d = din("rowp", [3, D])
    wrbd_d = din("wr_bd", [128, 8 * 128])
    wibd_d = din("wi_bd", [128, 8 * 128])
    s5B_d = din("s5B", [128, 2 * 16 * 128])
    s5C_d = din("s5C", [128, 2 * 16 * 32])
    ident_d = din("ident", [128, 128])
    iota_d = din("iota", [128, 16 * TC])
    out_d = nc.dram_tensor("out", [L, D], F32, kind="ExternalOutput").ap()

    with ExitStack() as es:
        def sb(name, shape, dt):
            return es.enter_context(nc.sbuf_tensor(name, list(shape), dt))

        slots = [sb(f"slot{i}", [128, SLOT_ELEMS], BF16) for i in range(NSLOT)]
        w_glu_sb = sb("w_glu_sb", [128, 4 * 512], BF16)
        wr_sb = sb("wr_sb", [128, 8 * 128], BF16)
        wi_sb = sb("wi_sb", [128, 8 * 128], BF16)
        Bsb = sb("Bsb", [128, 2 * 16 * 128], BF16)
        Csb = sb("Csb", [128, 2 * 16 * 32], BF16)
        Er = sb("Er", [128, 16 * TC], F32)
        Ei = sb("Ei", [128, 16 * TC], F32)
        ident = sb("ident_sb", [128, 128], F32)
        ident_bf = sb("ident_bf", [128, 128], BF16)
        colp = sb("colp_sb", [128, NCOL], F32)
        hcol = sb("hcol", [128, NCOL], F32)
        dcol = sb("dcol", [128, 64], F32)
        s5c = sb("s5c", [128, 16 * 24], F32)
        s5ci = sb("s5ci", [128, 16], I32)
        gple_bc = sb("gple_bc", [128, D], F32)
        gfin_bc = sb("gfin_bc", [128, D], F32)
        bpg_row = sb("bpg_row", [1, D], BF16)
        ones_bf = sb("ones_bf", [128, 128], BF16)
        ones_row = sb("ones_row", [1, 128], BF16)
        x_fm = sb("x_fm", [128, 8 * TB], F32)
        A2 = sb("A2", [128, 8 * (TB + 3)], BF16)
        A3 = sb("A3", [128, 8 * TB], BF16)
        A4 = sb("A4", [128, 4 * TB], BF16)
        FB = sb("FB", [128, 8 * TB], BF16)
        hid = sb("hid", [128, 22 * TB], BF16)
        NTS, NTF = 8, 7
        NT = NTS + NTF
        tmp = sb("tmp", [128, NT * 512], F32)
        A1 = tmp[:].bitcast(BF16)[:, 4 * 1024:8 * 1024]
        A6 = hid[:, 8 * TB:12 * TB]
        sqb = sb("sqb", [128, 2 * TB], BF16)
        p_fm = sqb
        sbf = sb("sbf", [128, 4 * TB], BF16)
        xcb = sb("xcb", [128, TB], BF16)
        carry_s5 = sb("carry_s5", [128, 32], F32)
        carry_lru = sb("carry_lru", [128, 8], F32)
        small = sb("small", [128, 96], F32)
        ps = es.enter_context(nc.psum_tensor("ps", [128, 8, 512], F32))

        eng_names = ["sp", "act", "dve", "pool", "pe"]
        sem_keys = eng_names + [f"dslot{i}" for i in range(NSLOT)] + ["dparam", "dparam2", "dx0", "dx1", "dxf0", "dxf1", "dxf2", "dp", "do0", "do1", "do2", "do3", "do4", "do5", "do6", "ddbg"]
        sems = {k: es.enter_context(nc.semaphore("s_" + k)) for k in sem_keys}
        block = es.enter_context(nc.Block())

        def program(em, plan, dbg_on):
            B_ps = [Buf(f"ps{i}") for i in range(8)]
            B_slot = [Buf(f"slot{i}") for i in range(NSLOT)]
            B_param = Buf("param")
            B_param2 = Buf("param2")
            B_cst = Buf("cst")
            B_xfm = [Buf(f"xfm{k}") for k in range(8)]
            B_A2 = [Buf(f"A2_{k}") for k in range(8)]
            B_A3 = [Buf(f"A3_{k}") for k in range(8)]
            B_A4 = [Buf(f"A4_{k}") for k in range(4)]
            B_FB = [Buf(f"FB_{k}") for k in range(8)]
            B_hid = [Buf(f"hid{k}") for k in range(22)]
            B_pst = Buf("pst")
            B_tmp = [Buf(f"tmp{i}") for i in range(NT)]
            B_A1 = [B_tmp[4 + k // 2] for k in range(8)]
            B_A6 = B_hid[8:12]
            B_sq = [Buf("sq0"), Buf("sq1")]
            B_sbf = [Buf(f"sbf{i}") for i in range(4)]
            B_xc = [Buf("xc0"), Buf("xc1")]
            B_cs5 = Buf("carry_s5")
            B_clru = Buf("carry_lru")
            B_small = Buf("small")
            B_smallF = Buf("smallF")
            B_s5q = [[Buf(f"s5q{a}{c}") for c in range(4)] for a in range(2)]
            B_sqF = [Buf("sqF0"), Buf("sqF1")]
            B_pfm = B_sqF
            B_tab = Buf("tables")
            B_out = [Buf(f"out{i}") for i in range(NTF)]

            st = {"bank": 0, "bankS": 0, "bankF": 0, "tmpS": 0, "tmpF": 0, "tmpL": 0}
            dbg_bufs = []

            def dbg(name, ap, bufs):
                if not dbg_on:
                    return
                shp = [int(v) for v in ap.shape]
                d = nc.dram_tensor("dbg_" + name, shp, ap.dtype, kind="ExternalOutput").ap()
                ob = Buf("dbg_" + name)
                em.dma("sp", "ddbg", [(d, ap)], reads=list(bufs), writes=[ob])
                dbg_bufs.append(ob)

            def bank():
                b = st["bank"]
                st["bank"] = (b + 1) % 8
                return b

            SBANKS = [0, 1, 2, 3]
            LBANKS = [0, 0]
            XBANKS = [1, 2]
            FBANKS = [4, 5, 6, 7]

            def bankS():
                b = st["bankS"]
                st["bankS"] = (b + 1) % len(SBANKS)
                return SBANKS[b]

            def bankF():
                b = st["bankF"]
                st["bankF"] = (b + 1) % len(FBANKS)
                return FBANKS[b]

            def tmpi(n=1):
                i = st["tmpS"]
                if i + n > 4:
                    i = 0
                st["tmpS"] = (i + n) % 4
                return i

            def tmpl():
                i = st["tmpL"]
                st["tmpL"] = (i + 1) % 4
                return 4 + i

            def tmpf(n=1):
                i = st["tmpF"]
                if i + n > NTF:
                    i = 0
                st["tmpF"] = (i + n) % NTF
                return NTS + i

            def T(i, n=1):
                return tmp[:, i * 512:(i + n) * 512]

            def col(c):
                return colp[:, c:c + 1]

            def hc(c):
                return hcol[:, c:c + 1]

            plan_rec = []
            ws = {"issued": 0, "next": 0}

            def _plan():
                return plan if plan is not None else plan_rec

            released = set()

            def _issue_one():
                lst = _plan()
                n = ws["issued"]
                w_ap, k0, kt, c0, cols = lst[n]
                src = w_ap.rearrange("(k p) c -> p k c", p=128)[:, k0:k0 + kt, c0:c0 + cols]
                si = n % NSLOT
                dst = slots[si][:, 0:kt * cols].rearrange("p (k c) -> p k c", c=cols)
                em.dma("pool", f"dslot{si}", [(dst, src)], writes=[B_slot[si]])
                ws["issued"] = n + 1

            def try_issue():
                lst = _plan()
                while ws["issued"] < len(lst) and (ws["issued"] < NSLOT or (ws["issued"] - NSLOT) in released):
                    _issue_one()

            def issue_slab():
                try_issue()

            def get_slab(w_ap, k0, kt, c0, cols):
                n = ws["next"]
                ws["next"] = n + 1
                if plan is None:
                    plan_rec.append((w_ap, k0, kt, c0, cols))
                else:
                    assert plan[n][0] is w_ap and tuple(plan[n][1:]) == (k0, kt, c0, cols), "slab plan mismatch"
                try_issue()
                assert ws["issued"] > n, "slab ring exhausted (too many live slabs)"
                return (n, n % NSLOT), cols

            def release_slab(h):
                released.add(h[0])
                try_issue()

            def slab_lhsT(h, cols, k, c0, n=128):
                return slots[h[1]][:, k * cols + c0:k * cols + c0 + n]

            em.dma("sp", "dparam", [
                (colp[:], colp_d), (s5c[:, 0:48], s5p_d), (ident[:], ident_d),
                (gple_bc[:], rowp_d[1:2, :].broadcast_to([128, D])),
                (gfin_bc[:], rowp_d[2:3, :].broadcast_to([128, D])),
                (tmp[:, 0:1024], s5C_d), (x_fm[:, 0:16 * TC], iota_d),
            ], writes=[B_param, B_tmp[0], B_tmp[1]] + B_xfm)
            em.dma("pool", "dparam2", [
                (w_glu_sb[:].rearrange("p (k c) -> p k c", c=512), w_glu_d.rearrange("(k p) c -> p k c", p=128)),
                (wr_sb[:], wrbd_d), (wi_sb[:], wibd_d), (Bsb[:], s5B_d), (bpg_row[:], rowp_d[0:1, :]),
            ], writes=[B_param2])
            if plan is not None:
                try_issue()

            P = [B_param, B_param2, B_cst]

            def V(eng, fn, reads, writes):
                em.op(eng, fn, reads=reads, writes=writes)

            V("dve", lambda e: e.memset(ones_bf[:], 1.0 / D), [], [B_cst])
            V("dve", lambda e: e.memset(ones_row[:], 1.0), [], [B_cst])
            V("dve", lambda e: e.tensor_copy(out=ident_bf[:], in_=ident[:]), [B_param], [B_cst])
            V("dve", lambda e: e.memset(small[:], 0.0), [], [B_cst])
            V("dve", lambda e: e.memset(small[:, 0:1], EPS), [B_cst], [B_cst])
            V("dve", lambda e: e.memset(carry_s5[:], 0.0), [], [B_cs5])
            V("dve", lambda e: e.memset(carry_lru[:], 0.0), [], [B_clru])
            V("dve", lambda e: e.memset(A2[:], 0.0), [], B_A2)
            V("dve", lambda e: e.memset(small[:, 5:6], 1.0), [B_cst], [B_cst])
            eps_col = small[:, 0:1]
            one_col = small[:, 5:6]
            V("dve", lambda e: e.tensor_scalar(out=hcol[:], in0=colp[:], scalar1=0.5, scalar2=None, op0=ALU.mult), P, [B_cst])
            V("act", lambda e: e.activation(out=dcol[:, 16:24], in_=colp[:, C_LAM:C_LAM + 8], func=AF.Exp, scale=-1.0), P, [B_cst])
            V("act", lambda e: e.activation(out=dcol[:, 24:32], in_=dcol[:, 16:24], func=AF.Ln, bias=one_col, scale=1.0), [B_cst], [B_cst])
            V("dve", lambda e: e.tensor_scalar(out=dcol[:, 0:8], in0=dcol[:, 24:32], scalar1=-4.0, scalar2=None, op0=ALU.mult), [B_cst], [B_cst])
            V("dve", lambda e: e.tensor_scalar(out=dcol[:, 8:16], in0=dcol[:, 24:32], scalar1=-8.0, scalar2=None, op0=ALU.mult), [B_cst], [B_cst])

            def g(i):
                return s5c[:, 16 * i:16 * i + 16]
            S = [B_cst]

            def reduce_angle(dst, src, n, scr_i, scr_f):
                V("dve", lambda e: e.tensor_scalar(out=scr_i, in0=src, scalar1=1.0 / TWO_PI, scalar2=None, op0=ALU.mult), S + P + [B_tab], S + [B_tab])
                V("dve", lambda e: e.tensor_copy(out=scr_f, in_=scr_i), S + [B_tab], S + [B_tab])
                V("dve", lambda e: e.scalar_tensor_tensor(out=dst, in0=scr_f, scalar=-TWO_PI, in1=src, op0=ALU.mult, op1=ALU.add), S + [B_tab], S + [B_tab])
                V("dve", lambda e: e.tensor_scalar(out=dst, in0=dst, scalar1=-PI, scalar2=PI, op0=ALU.max, op1=ALU.min), S + [B_tab], S + [B_tab])

            def cos_arg(dst, src, scr):
                V("dve", lambda e: e.tensor_scalar(out=dst, in0=src, scalar1=HALF_PI, scalar2=None, op0=ALU.add), S + [B_tab], S + [B_tab])
                V("dve", lambda e: e.tensor_scalar(out=scr, in0=dst, scalar1=PI, scalar2=None, op0=ALU.is_gt), S + [B_tab], S + [B_tab])
                V("dve", lambda e: e.scalar_tensor_tensor(out=dst, in0=scr, scalar=-TWO_PI, in1=dst, op0=ALU.mult, op1=ALU.add), S + [B_tab], S + [B_tab])
                V("dve", lambda e: e.tensor_scalar(out=dst, in0=dst, scalar1=-PI, scalar2=PI, op0=ALU.max, op1=ALU.min), S + [B_tab], S + [B_tab])

            V("act", lambda e: e.activation(out=g(3), in_=g(2), func=AF.Exp), P, S)
            V("dve", lambda e: e.tensor_tensor(out=g(15), in0=g(0), in1=g(3), op=ALU.mult), S, S)
            V("act", lambda e: e.activation(out=g(4), in_=g(15), func=AF.Exp), S, S)
            V("dve", lambda e: e.tensor_tensor(out=g(5), in0=g(1), in1=g(3), op=ALU.mult), S, S)
            reduce_angle(g(6), g(5), 16, s5ci[:], g(15))
            V("act", lambda e: e.activation(out=g(7), in_=g(6), func=AF.Sin), S, S)
            cos_arg(g(16), g(6), g(15))
            V("act", lambda e: e.activation(out=g(8), in_=g(16), func=AF.Sin), S, S)
            V("dve", lambda e: e.tensor_tensor(out=g(9), in0=g(4), in1=g(8), op=ALU.mult), S, S)
            V("dve", lambda e: e.tensor_tensor(out=g(10), in0=g(4), in1=g(7), op=ALU.mult), S, S)
            V("dve", lambda e: e.tensor_tensor(out=g(11), in0=g(0), in1=g(0), op=ALU.mult), S, S)
            V("dve", lambda e: e.tensor_tensor(out=g(15), in0=g(1), in1=g(1), op=ALU.mult), S, S)
            V("dve", lambda e: e.tensor_tensor(out=g(11), in0=g(11), in1=g(15), op=ALU.add), S, S)
            V("dve", lambda e: e.reciprocal(out=g(11), in_=g(11)), S, S)
            V("dve", lambda e: e.tensor_scalar(out=g(12), in0=g(9), scalar1=-1.0, scalar2=None, op0=ALU.add), S, S)
            V("dve", lambda e: e.tensor_tensor(out=g(15), in0=g(12), in1=g(0), op=ALU.mult), S, S)
            V("dve", lambda e: e.tensor_tensor(out=g(16), in0=g(10), in1=g(1), op=ALU.mult), S, S)
            V("dve", lambda e: e.tensor_tensor(out=g(15), in0=g(15), in1=g(16), op=ALU.add), S, S)
            V("dve", lambda e: e.tensor_tensor(out=g(13), in0=g(15), in1=g(11), op=ALU.mult), S, S)
            V("dve", lambda e: e.tensor_tensor(out=g(15), in0=g(10), in1=g(0), op=ALU.mult), S, S)
            V("dve", lambda e: e.tensor_tensor(out=g(16), in0=g(12), in1=g(1), op=ALU.mult), S, S)
            V("dve", lambda e: e.tensor_tensor(out=g(15), in0=g(15), in1=g(16), op=ALU.subtract), S, S)
            V("dve", lambda e: e.tensor_tensor(out=g(14), in0=g(15), in1=g(11), op=ALU.mult), S, S)
            V("dve", lambda e: e.tensor_scalar(out=g(19), in0=g(14), scalar1=-1.0, scalar2=None, op0=ALU.mult), S, S)
            V("dve", lambda e: e.tensor_scalar(out=g(20), in0=g(6), scalar1=float(TC), scalar2=None, op0=ALU.mult), S, S)
            reduce_angle(g(21), g(20), 16, s5ci[:], g(15))
            V("act", lambda e: e.activation(out=g(18), in_=g(21), func=AF.Sin), S, S)
            cos_arg(g(22), g(21), g(15))
            V("act", lambda e: e.activation(out=g(17), in_=g(22), func=AF.Sin), S, S)

            Craw = tmp
            for i in range(16):
                cr_i = Craw[:, i * 32:(i + 1) * 32]
                ci_i = Craw[:, 512 + i * 32:512 + (i + 1) * 32]
                sc = small[:, 16:48]
                fr_c = s5c[:, 16 * 13 + i:16 * 13 + i + 1]
                fi_c = s5c[:, 16 * 14 + i:16 * 14 + i + 1]
                nfi_c = s5c[:, 16 * 19 + i:16 * 19 + i + 1]
                o_re = Csb[:, i * 32:(i + 1) * 32]
                o_im = Csb[:, 512 + i * 32:512 + (i + 1) * 32]
                V("dve", lambda e, ci_i=ci_i, fi_c=fi_c, sc=sc: e.tensor_scalar(out=sc, in0=ci_i, scalar1=fi_c, scalar2=None, op0=ALU.mult), S + [B_tmp[0], B_tmp[1]], S)
                V("dve", lambda e, cr_i=cr_i, fr_c=fr_c, sc=sc, o_re=o_re: e.scalar_tensor_tensor(out=o_re, in0=cr_i, scalar=fr_c, in1=sc, op0=ALU.mult, op1=ALU.subtract), S + [B_tmp[0], B_tmp[1]], S + [B_tab])
                V("dve", lambda e, ci_i=ci_i, fr_c=fr_c, sc=sc: e.tensor_scalar(out=sc, in0=ci_i, scalar1=fr_c, scalar2=None, op0=ALU.mult), S + [B_tmp[0], B_tmp[1]], S)
                V("dve", lambda e, cr_i=cr_i, nfi_c=nfi_c, sc=sc, o_im=o_im: e.scalar_tensor_tensor(out=o_im, in0=cr_i, scalar=nfi_c, in1=sc, op0=ALU.mult, op1=ALU.subtract), S + [B_tmp[0], B_tmp[1]], S + [B_tab])

            iota_sb = x_fm[:, 0:16 * TC]
            ph = hid[:].bitcast(F32)[:, 0:16 * TC]
            phi_i = tmp[:, 0:16 * TC].bitcast(I32)
            TT = B_xfm + B_hid + [B_tab]
            for i in range(16):
                thr_c = s5c[:, 16 * 6 + i:16 * 6 + i + 1]
                V("dve", lambda e, i=i, thr_c=thr_c: e.tensor_scalar(out=ph[:, i * TC:(i + 1) * TC], in0=iota_sb[:, i * TC:(i + 1) * TC], scalar1=thr_c, scalar2=None, op0=ALU.mult), S + TT, TT)
            scr_f = x_fm[:, 0:16 * TC]
            TT2 = TT + B_tmp[0:8]
            V("dve", lambda e: e.tensor_scalar(out=phi_i, in0=ph, scalar1=1.0 / TWO_PI, scalar2=None, op0=ALU.mult), S + TT2, TT2)
            V("dve", lambda e: e.tensor_copy(out=scr_f, in_=phi_i), TT2, TT2)
            V("dve", lambda e: e.scalar_tensor_tensor(out=ph, in0=scr_f, scalar=-TWO_PI, in1=ph, op0=ALU.mult, op1=ALU.add), TT2, TT2)
            V("dve", lambda e: e.tensor_scalar(out=ph, in0=ph, scalar1=-PI, scalar2=PI, op0=ALU.max, op1=ALU.min), TT2, TT2)
            V("act", lambda e: e.activation(out=Ei[:], in_=ph, func=AF.Sin), TT2, TT2)
            V("dve", lambda e: e.tensor_scalar(out=ph, in0=ph, scalar1=HALF_PI, scalar2=None, op0=ALU.add), TT2, TT2)
            V("dve", lambda e: e.tensor_scalar(out=scr_f, in0=ph, scalar1=PI, scalar2=None, op0=ALU.is_gt), TT2, TT2)
            V("dve", lambda e: e.scalar_tensor_tensor(out=ph, in0=scr_f, scalar=-TWO_PI, in1=ph, op0=ALU.mult, op1=ALU.add), TT2, TT2)
            V("dve", lambda e: e.tensor_scalar(out=ph, in0=ph, scalar1=-PI, scalar2=PI, op0=ALU.max, op1=ALU.min), TT2, TT2)
            V("act", lambda e: e.activation(out=Er[:], in_=ph, func=AF.Sin), TT2, TT2)
            TAB = [B_tab, B_cst]
            dbg("Er", Er[:], TAB)
            dbg("Ei", Ei[:], TAB)
            dbg("s5c", s5c[:], TAB)
            dbg("Csb", Csb[:], TAB)
            dbg("dcol", dcol[:], TAB)

            def rmsnorm_fm(gc0, dst, dstB, dst_stride):
                bn = bankF()
                for k in range(8):
                    sq = sqb[:, (k % 2) * TB:(k % 2 + 1) * TB]
                    V("act", lambda e, k=k, sq=sq: e.activation(out=sq, in_=x_fm[:, k * TB:(k + 1) * TB], func=AF.Square), [B_xfm[k]], [B_sqF[k % 2]])
                    V("pe", lambda e, k=k, sq=sq, bn=bn: e.matmul(ps[:, bn, :], lhsT=ones_bf[:], rhs=sq, start=(k == 0), stop=(k == 7)), [B_sqF[k % 2], B_cst], [B_ps[bn]])
                ti = tmpf()
                rt = T(ti)
                V("act", lambda e: e.activation(out=rt, in_=ps[:, bn, :], func=AF.Sqrt, bias=eps_col, scale=1.0), [B_ps[bn], B_cst], [B_tmp[ti]])
                yield "Y"
                V("dve", lambda e: e.reciprocal(out=rt, in_=rt), [B_tmp[ti]], [B_tmp[ti]])
                for k in range(8):
                    V("dve", lambda e, k=k: e.scalar_tensor_tensor(out=dst[:, k * dst_stride:k * dst_stride + TB], in0=x_fm[:, k * TB:(k + 1) * TB], scalar=col(gc0 + k), in1=rt, op0=ALU.mult, op1=ALU.mult),
                      [B_xfm[k], B_tmp[ti]] + P, [dstB[k]])

            def mm_group(bn, si, cols, c0, kt, rhs_fn, rhsB, extraB=()):
                for k in range(kt):
                    V("pe", lambda e, k=k: e.matmul(ps[:, bn, :], lhsT=slab_lhsT(si, cols, k, c0), rhs=rhs_fn(k), start=(k == 0), stop=(k == kt - 1)),
                      [B_slot[si[1]], rhsB[k]] + list(extraB), [B_ps[bn]])

            A2S = TB + 3

            def h_k(k):
                return A1[:, k * TB:(k + 1) * TB]

            A2S = TB + 3
            a2v = A2[:].rearrange("p (k t) -> p k t", t=A2S)

            def h_k(k):
                return A1[:, k * TB:(k + 1) * TB]

            def S_thread(b):
                t0 = b * TB
                def SA_a(tt):
                    xs = T(2 * (tt % 2), 2)
                    Bxs = [B_tmp[2 * (tt % 2)], B_tmp[2 * (tt % 2) + 1]]
                    em.dma("sp", f"dx{tt % 2}", [(xs, x_d[t0 + tt * 128:t0 + (tt + 1) * 128, :])], writes=Bxs)
                    xn = A2[:, tt * 1024:(tt + 1) * 1024]
                    ssx, rsx = small[:, 72 + 2 * tt:73 + 2 * tt], small[:, 73 + 2 * tt:74 + 2 * tt]
                    V("dve", lambda e: e.memset(ssx, 0.0), [B_small], [B_small])
                    V("act", lambda e: e.activation(out=xn, in_=xs, func=AF.Square, accum_out=ssx), Bxs + [B_small], list(B_A2) + [B_small])
                    V("dve", lambda e: e.tensor_scalar(out=rsx, in0=ssx, scalar1=1.0 / D, scalar2=EPS, op0=ALU.mult, op1=ALU.add), [B_small], [B_small])

                def SA_b(tt):
                    xs = T(2 * (tt % 2), 2)
                    Bxs = [B_tmp[2 * (tt % 2)], B_tmp[2 * (tt % 2) + 1]]
                    xn = A2[:, tt * 1024:(tt + 1) * 1024]
                    Bxn = list(B_A2)
                    rsx = small[:, 73 + 2 * tt:74 + 2 * tt]
                    V("act", lambda e: e.activation(out=rsx, in_=rsx, func=AF.Sqrt), [B_small], [B_small])
                    V("dve", lambda e: e.reciprocal(out=rsx, in_=rsx), [B_small], [B_small])
                    V("act", lambda e: e.activation(out=xn, in_=xs, func=AF.Copy, scale=rsx), Bxs + [B_small], Bxn)
                    for hh in range(2):
                        bn = bankS()
                        pb = ps[:, bn, :].bitcast(BF16)
                        for kk in range(4):
                            k = hh * 4 + kk
                            V("pe", lambda e, k=k, kk=kk, pb=pb: e.transpose(out=pb[:, kk * 128:(kk + 1) * 128], in_=xn[:, k * 128:(k + 1) * 128], identity=ident_bf[:]), Bxn + P, [B_ps[bn]])
                        gcol3 = colp[:, C_GMIX + hh * 4:C_GMIX + hh * 4 + 4].unsqueeze(2).broadcast_to([128, 4, 128])
                        dst3 = A1[:, hh * 4 * TB:(hh * 4 + 4) * TB].rearrange("p (k t) -> p k t", t=TB)[:, :, tt * 128:(tt + 1) * 128]
                        src3 = pb[:, 0:512].rearrange("p (k t) -> p k t", t=128)
                        V("dve", lambda e, dst3=dst3, src3=src3, gcol3=gcol3: e.tensor_tensor(out=dst3, in0=src3, in1=gcol3, op=ALU.mult), [B_ps[bn]] + P, [B_A1[hh * 4 + kk] for kk in range(4)])

                for step in ("a0", "a1", "b0", "a2", "b1", "a3", "b2", "b3"):
                    (SA_a if step[0] == "a" else SA_b)(int(step[1]))
                    if step[0] == "b":
                        yield 3.0
                if b == 0:
                    dbg("h", A1[:], B_A1)
                def SC_slab(s):
                    si, cols = get_slab(w_in_d, 0, 8, s * 512, 512)
                    for j in range(4):
                        oc = s * 4 + j
                        bn = bankS() if s == 0 else LBANKS[0]
                        mm_group(bn, si, cols, j * 128, 8, h_k, B_A1)
                        if oc < 4:
                            dst, dB = A4[:, oc * TB:(oc + 1) * TB], B_A4[oc]
                        else:
                            t = oc - 4
                            dst, dB = A2[:, t * A2S + 3:t * A2S + 3 + TB], B_A2[t]
                        V("act", lambda e, dst=dst, bn=bn, oc=oc: e.activation(out=dst, in_=ps[:, bn, :], func=AF.Identity, bias=col(C_BIN + oc), scale=1.0), [B_ps[bn]] + P, [dB])
                    release_slab(si)

                SC_slab(0)
                yield 8.0

                def SC_rest():
                    for s in (1, 2):
                        SC_slab(s)
                        yield 8.0
                    if b > 0:
                        V("dve", lambda e: e.tensor_copy(out=a2v[:, :, 0:3], in_=small[:, 48:48 + 24].rearrange("p (k t) -> p k t", t=3)), B_A2 + [B_small], B_A2)
                    else:
                        V("dve", lambda e: e.memset(a2v[:, :, 0:3], 0.0), B_A2, B_A2)
                if b <= 1:
                    dbg(f"ua{b}", A4[:], B_A4)
                    dbg(f"ub{b}", A2[:], B_A2)

                def LRU_gen():
                    for t in range(8):
                        ub = A2[:, t * A2S:(t + 1) * A2S]
                        xc = xcb[:, 0:TB]
                        i0, i1, i2, i3 = tmpl(), tmpl(), tmpl(), tmpl()
                        i4 = i0
                        acc = T(i0)
                        V("dve", lambda e, ub=ub, acc=acc, t=t: e.tensor_scalar(out=acc, in0=ub[:, 0:TB], scalar1=col(C_CONVW + 0 * 8 + t), scalar2=col(C_CONVB + t), op0=ALU.mult, op1=ALU.add), [B_A2[t]] + P, [B_tmp[i0]])
                        V("dve", lambda e, ub=ub, acc=acc, t=t: e.scalar_tensor_tensor(out=acc, in0=ub[:, 1:TB + 1], scalar=col(C_CONVW + 1 * 8 + t), in1=acc, op0=ALU.mult, op1=ALU.add), [B_A2[t], B_tmp[i0]] + P, [B_tmp[i0]])
                        V("dve", lambda e, ub=ub, acc=acc, t=t: e.scalar_tensor_tensor(out=acc, in0=ub[:, 2:TB + 2], scalar=col(C_CONVW + 2 * 8 + t), in1=acc, op0=ALU.mult, op1=ALU.add), [B_A2[t], B_tmp[i0]] + P, [B_tmp[i0]])
                        V("dve", lambda e, ub=ub, acc=acc, t=t, xc=xc: e.scalar_tensor_tensor(out=xc, in0=ub[:, 3:TB + 3], scalar=col(C_CONVW + 3 * 8 + t), in1=acc, op0=ALU.mult, op1=ALU.add), [B_A2[t], B_tmp[i0]] + P, [B_xc[0]])
                        br_, bi_ = LBANKS[0], LBANKS[1]
                        tr_, ti_, a_, a2_ = T(i1), T(i2), T(i3), T(i4)
                        V("pe", lambda e, t=t, xc=xc, br_=br_: e.matmul(ps[:, br_, :], lhsT=wr_sb[:, t * 128:(t + 1) * 128], rhs=xc, start=True, stop=True), [B_xc[0]] + P, [B_ps[br_]])
                        V("act", lambda e, tr_=tr_, br_=br_, t=t: e.activation(out=tr_, in_=ps[:, br_, :], func=AF.Tanh, bias=hc(C_BR + t), scale=0.5), [B_ps[br_], B_cst], [B_tmp[i1]])
                        V("pe", lambda e, t=t, xc=xc, bi_=bi_: e.matmul(ps[:, bi_, :], lhsT=wi_sb[:, t * 128:(t + 1) * 128], rhs=xc, start=True, stop=True), [B_xc[0]] + P, [B_ps[bi_]])
                        V("act", lambda e, ti_=ti_, bi_=bi_, t=t: e.activation(out=ti_, in_=ps[:, bi_, :], func=AF.Tanh, bias=hc(C_BI + t), scale=0.5), [B_ps[bi_], B_cst], [B_tmp[i2]])
                        V("act", lambda e, tr_=tr_, a_=a_, t=t: e.activation(out=a_, in_=tr_, func=AF.Exp, bias=dcol[:, t:t + 1], scale=dcol[:, t:t + 1]), [B_tmp[i1], B_cst], [B_tmp[i3]])
                        V("act", lambda e, tr_=tr_, a2_=a2_, t=t: e.activation(out=a2_, in_=tr_, func=AF.Exp, bias=dcol[:, 8 + t:9 + t], scale=dcol[:, 8 + t:9 + t]), [B_tmp[i1], B_cst], [B_tmp[i4]])
                        yield 3.0
                        V("dve", lambda e, ti_=ti_, xc=xc: e.scalar_tensor_tensor(out=ti_, in0=ti_, scalar=1.0, in1=xc, op0=ALU.add, op1=ALU.mult), [B_tmp[i2], B_xc[0]], [B_tmp[i2]])
                        V("act", lambda e, a2_=a2_: e.activation(out=a2_, in_=a2_, func=AF.Relu, bias=one_col, scale=-1.0), [B_tmp[i4], B_cst], [B_tmp[i4]])
                        V("act", lambda e, a2_=a2_: e.activation(out=a2_, in_=a2_, func=AF.Sqrt), [B_tmp[i4]], [B_tmp[i4]])
                        yield 0.7
                        V("dve", lambda e, ti_=ti_, a2_=a2_: e.tensor_tensor(out=ti_, in0=ti_, in1=a2_, op=ALU.mult), [B_tmp[i2], B_tmp[i4]], [B_tmp[i2]])
                        V("dve", lambda e, tr_=tr_, a_=a_, ti_=ti_, t=t: e.tensor_tensor_scan(out=tr_, data0=a_, data1=ti_, initial=carry_lru[:, t:t + 1], op0=ALU.mult, op1=ALU.add),
                          [B_tmp[i3], B_tmp[i2], B_clru], [B_tmp[i1]])
                        V("dve", lambda e, tr_=tr_, t=t: e.tensor_copy(out=carry_lru[:, t:t + 1], in_=tr_[:, TB - 1:TB]), [B_tmp[i1]], [B_clru])
                        V("act", lambda e, tr_=tr_, t=t: e.activation(out=A3[:, t * TB:(t + 1) * TB], in_=tr_, func=AF.Copy), [B_tmp[i1]], [B_A3[t]])
                        yield 2.2
                    V("dve", lambda e: e.tensor_copy(out=small[:, 48:48 + 24].rearrange("p (k t) -> p k t", t=3), in_=a2v[:, :, TB:TB + 3]), B_A2, [B_small])
                    if b <= 1:
                        dbg(f"yb{b}", A3[:], B_A3)

                yb = SBANKS[3]

                def v3(ap):
                    return ap.rearrange("p (s t) -> p s t", t=TC)

                def emit_bmm(i):
                    f_ = i // 4
                    ua_ = A4[:, f_ * TB:(f_ + 1) * TB]
                    V("pe", lambda e: e.matmul(ps[:, XBANKS[0], :], lhsT=Bsb[:, i * 128:(i + 1) * 128], rhs=ua_, start=True, stop=True), [B_A4[f_]] + P, [B_ps[XBANKS[0]]])
                    V("pe", lambda e: e.matmul(ps[:, XBANKS[1], :], lhsT=Bsb[:, 2048 + i * 128:2048 + (i + 1) * 128], rhs=ua_, start=True, stop=True), [B_A4[f_]] + P, [B_ps[XBANKS[1]]])

                def S5_gen():
                    for f in range(4):
                        pend = []
                        ua = A4[:, f * TB:(f + 1) * TB]
                        for q in range(4):
                            i = 4 * f + q
                            bxr, bxi = XBANKS[0], XBANKS[1]
                            if i == 0:
                                emit_bmm(0)
                            er3 = Er[:, i * TC:(i + 1) * TC].unsqueeze(1).broadcast_to([128, NSUB, TC])
                            ei3 = Ei[:, i * TC:(i + 1) * TC].unsqueeze(1).broadcast_to([128, NSUB, TC])
                            j1, j2, j3, j4 = tmpi(), tmpi(), tmpi(), tmpi()
                            t1, t2, t3, t4 = T(j1), T(j2), T(j3), T(j4)
                            pxr, pxi = ps[:, bxr, :], ps[:, bxi, :]
                            V("dve", lambda e, t1=t1, pxr=pxr, er3=er3: e.tensor_tensor(out=v3(t1), in0=v3(pxr), in1=er3, op=ALU.mult), [B_ps[bxr]] + TAB, [B_tmp[j1]])
                            V("dve", lambda e, t2=t2, pxi=pxi, ei3=ei3: e.tensor_tensor(out=v3(t2), in0=v3(pxi), in1=ei3, op=ALU.mult), [B_ps[bxi]] + TAB, [B_tmp[j2]])
                            V("dve", lambda e, t3=t3, pxi=pxi, er3=er3: e.tensor_tensor(out=v3(t3), in0=v3(pxi), in1=er3, op=ALU.mult), [B_ps[bxi]] + TAB, [B_tmp[j3]])
                            V("dve", lambda e, t4=t4, pxr=pxr, ei3=ei3: e.tensor_tensor(out=v3(t4), in0=v3(pxr), in1=ei3, op=ALU.mult), [B_ps[bxr]] + TAB, [B_tmp[j4]])
                            if i + 1 < 16:
                                emit_bmm(i + 1)
                            V("dve", lambda e, t1=t1, t2=t2: e.tensor_tensor(out=t1, in0=t1, in1=t2, op=ALU.add), [B_tmp[j1], B_tmp[j2]], [B_tmp[j1]])
                            V("dve", lambda e, t3=t3, t4=t4: e.tensor_tensor(out=t3, in0=t3, in1=t4, op=ALU.subtract), [B_tmp[j3], B_tmp[j4]], [B_tmp[j3]])
                            m_c = s5c[:, 16 * 4 + i:16 * 4 + i + 1]
                            ebr = s5c[:, 16 * 17 + i:16 * 17 + i + 1]
                            ebi = s5c[:, 16 * 18 + i:16 * 18 + i + 1]
                            m_bc = m_c.broadcast_to([128, TC])
                            for sc in range(NSUB):
                                if sc == 0:
                                    lr_, li_ = carry_s5[:, i:i + 1], carry_s5[:, 16 + i:17 + i]
                                    lB = [B_cs5]
                                else:
                                    lr_, li_ = t2[:, sc * TC - 1:sc * TC], t4[:, sc * TC - 1:sc * TC]
                                    lB = [B_tmp[j2], B_tmp[j4]]
                                o_ = 16 + 4 * (sc % 2)
                                ir, ii, sx1, sx2 = small[:, o_:o_ + 1], small[:, o_ + 1:o_ + 2], small[:, o_ + 2:o_ + 3], small[:, o_ + 3:o_ + 4]
                                Bq = B_s5q[sc % 2]
                                V("dve", lambda e, li_=li_, ebi=ebi, sx1=sx1: e.tensor_scalar(out=sx1, in0=li_, scalar1=ebi, scalar2=None, op0=ALU.mult), lB + TAB, [Bq[0]])
                                V("dve", lambda e, lr_=lr_, ebi=ebi, sx2=sx2: e.tensor_scalar(out=sx2, in0=lr_, scalar1=ebi, scalar2=None, op0=ALU.mult), lB + TAB, [Bq[1]])
                                V("dve", lambda e, lr_=lr_, ebr=ebr, sx1=sx1, ir=ir: e.scalar_tensor_tensor(out=ir, in0=lr_, scalar=ebr, in1=sx1, op0=ALU.mult, op1=ALU.subtract), lB + TAB + [Bq[0]], [Bq[2]])
                                V("dve", lambda e, li_=li_, ebr=ebr, sx2=sx2, ii=ii: e.scalar_tensor_tensor(out=ii, in0=li_, scalar=ebr, in1=sx2, op0=ALU.mult, op1=ALU.add), lB + TAB + [Bq[1]], [Bq[3]])
                                sl = slice(sc * TC, (sc + 1) * TC)
                                V("dve", lambda e, t2=t2, t1=t1, sl=sl, ir=ir, m_bc=m_bc: e.tensor_tensor_scan(out=t2[:, sl], data0=m_bc, data1=t1[:, sl], initial=ir, op0=ALU.mult, op1=ALU.add), [B_tmp[j1], Bq[2]] + TAB, [B_tmp[j2]])
                                V("dve", lambda e, t4=t4, t3=t3, sl=sl, ii=ii, m_bc=m_bc: e.tensor_tensor_scan(out=t4[:, sl], data0=m_bc, data1=t3[:, sl], initial=ii, op0=ALU.mult, op1=ALU.add), [B_tmp[j3], Bq[3]] + TAB, [B_tmp[j4]])
                            V("dve", lambda e, t2=t2, i=i: e.tensor_copy(out=carry_s5[:, i:i + 1], in_=t2[:, TB - 1:TB]), [B_tmp[j2]], [B_cs5])
                            V("dve", lambda e, t4=t4, i=i: e.tensor_copy(out=carry_s5[:, 16 + i:17 + i], in_=t4[:, TB - 1:TB]), [B_tmp[j4]], [B_cs5])
                            yield 13.0
                            so = (i % 2) * 2
                            s_re = sbf[:, so * TB:(so + 1) * TB]
                            s_im = sbf[:, (so + 1) * TB:(so + 2) * TB]
                            V("dve", lambda e, t1=t1, t2=t2, er3=er3: e.tensor_tensor(out=v3(t1), in0=v3(t2), in1=er3, op=ALU.mult), [B_tmp[j2]] + TAB, [B_tmp[j1]])
                            V("dve", lambda e, t3=t3, t4=t4, ei3=ei3: e.tensor_tensor(out=v3(t3), in0=v3(t4), in1=ei3, op=ALU.mult), [B_tmp[j4]] + TAB, [B_tmp[j3]])
                            V("dve", lambda e, t2=t2, ei3=ei3: e.tensor_tensor(out=v3(t2), in0=v3(t2), in1=ei3, op=ALU.mult), [B_tmp[j2]] + TAB, [B_tmp[j2]])
                            V("dve", lambda e, t4=t4, er3=er3: e.tensor_tensor(out=v3(t4), in0=v3(t4), in1=er3, op=ALU.mult), [B_tmp[j4]] + TAB, [B_tmp[j4]])
                            V("dve", lambda e, t1=t1, t3=t3, s_re=s_re: e.tensor_tensor(out=s_re, in0=t1, in1=t3, op=ALU.subtract), [B_tmp[j1], B_tmp[j3]], [B_sbf[so]])
                            V("dve", lambda e, t2=t2, t4=t4, s_im=s_im: e.tensor_tensor(out=s_im, in0=t4, in1=t2, op=ALU.add), [B_tmp[j2], B_tmp[j4]], [B_sbf[so + 1]])
                            for fn_ in pend:
                                fn_()
                            pend = []

                            def cmm(i=i, q=q, s_re=s_re, s_im=s_im, so=so):
                                V("pe", lambda e: e.matmul(ps[32 * q:32 * q + 32, yb, :], lhsT=Csb[:, i * 32:(i + 1) * 32], rhs=s_re, start=True, stop=False, tile_position=(0, 32 * q)), [B_sbf[so]] + TAB, [B_ps[yb]])
                                V("pe", lambda e: e.matmul(ps[32 * q:32 * q + 32, yb, :], lhsT=Csb[:, 512 + i * 32:512 + (i + 1) * 32], rhs=s_im, start=False, stop=True, tile_position=(0, 32 * q)), [B_sbf[so + 1]] + TAB, [B_ps[yb]])
                            pend.append(cmm)
                            if q == 3:
                                for fn_ in pend:
                                    fn_()
                                pend = []
                                k1, k2 = j1, j3
                                y_, w_ = T(k1), T(k2)
                                V("dve", lambda e, f=f, y_=y_: e.scalar_tensor_tensor(out=y_, in0=A4[:, f * TB:(f + 1) * TB], scalar=col(C_S5D + f), in1=ps[:, yb, :], op0=ALU.mult, op1=ALU.add), [B_A4[f], B_ps[yb]] + P, [B_tmp[k1]])
                                V("act", lambda e, f=f, y_=y_: e.activation(out=A4[:, f * TB:(f + 1) * TB], in_=y_, func=AF.Gelu_apprx_tanh), [B_tmp[k1]], [B_A4[f]])
                            yield 5.0

                gC, gL, gS5 = SC_rest(), LRU_gen(), S5_gen()
                dC = dL = dS5 = False
                while not (dL and dS5):
                    adv = False
                    if not dC:
                        adv = True
                        try:
                            c_ = next(gC)
                            yield c_
                        except StopIteration:
                            dC = True
                    elif not dL and prog["merged"] >= b - 1:
                        adv = True
                        try:
                            c_ = next(gL)
                            yield c_
                        except StopIteration:
                            dL = True
                    if not dS5:
                        adv = True
                        try:
                            c_ = next(gS5)
                            yield c_
                        except StopIteration:
                            dS5 = True
                    if not adv:
                        yield "W"
                if b <= 1:
                    dbg(f"z{b}", A4[:], B_A4)
                while prog["down"] < b - 1:
                    yield "W"
                glu_tmp = []
                for f in range(4):
                    bn = bankS()
                    for k in range(4):
                        V("pe", lambda e, f=f, k=k, bn=bn: e.matmul(ps[:, bn, :], lhsT=w_glu_sb[:, k * 512 + f * 128:k * 512 + (f + 1) * 128], rhs=A4[:, k * TB:(k + 1) * TB], start=(k == 0), stop=(k == 3)), [B_A4[k]] + P, [B_ps[bn]])
                    k1 = tmpi()
                    g_ = T(k1)
                    V("act", lambda e, f=f, bn=bn, g_=g_: e.activation(out=g_, in_=ps[:, bn, :], func=AF.Tanh, bias=hc(C_BGLU + f), scale=0.5), [B_ps[bn], B_cst], [B_tmp[k1]])
                    glu_tmp.append((k1, g_))
                for f in range(4):
                    k1, g_ = glu_tmp[f]
                    V("dve", lambda e, f=f, g_=g_: e.scalar_tensor_tensor(out=A6[:, f * TB:(f + 1) * TB], in0=g_, scalar=1.0, in1=A4[:, f * TB:(f + 1) * TB], op0=ALU.add, op1=ALU.mult), [B_tmp[k1], B_A4[f]], [B_A6[f]])
                yield 6.0
                if b <= 1:
                    dbg(f"ya{b}", A6[:], B_A6)

            def F_head(b):
                t0 = b * TB
                fbk = lambda k: FB[:, k * TB:(k + 1) * TB]
                for tt in range(4):
                    pi_ = tmpf(2)
                    xs = T(pi_, 2)
                    Bxs = [B_tmp[pi_], B_tmp[pi_ + 1]]
                    em.dma("sp", f"dxf{(pi_ - NTS) // 2}", [(xs, x_d[t0 + tt * 128:t0 + (tt + 1) * 128, :])], writes=Bxs)
                    for hh in range(2):
                        bn = bankF()
                        for kk in range(4):
                            k = hh * 4 + kk
                            V("pe", lambda e, k=k, kk=kk, bn=bn, xs=xs: e.transpose(out=ps[:, bn, kk * 128:(kk + 1) * 128], in_=xs[:, k * 128:(k + 1) * 128], identity=ident[:]),
                              Bxs + P, [B_ps[bn]])
                        dst = x_fm[:, hh * 4 * TB:(hh * 4 + 4) * TB].rearrange("p (k t) -> p k t", t=TB)[:, :, tt * 128:(tt + 1) * 128]
                        src = ps[:, bn, :].rearrange("p (k t) -> p k t", t=128)
                        V("act", lambda e, dst=dst, src=src: e.activation(out=dst, in_=src, func=AF.Copy), [B_ps[bn]], [B_xfm[hh * 4 + kk] for kk in range(4)])
                    yield 3.0
                yield from rmsnorm_fm(C_GMIX, FB, B_FB, TB)
                yield 5.0
                for hf in range(2):
                    sga, cga = get_slab(w_in_d, 0, 8, 1536 + hf * 512, 512)
                    sgb, cgb = get_slab(w_in_d, 0, 8, 2560 + hf * 512, 512)
                    sa, ca = get_slab(w_a_d, 0, 4, hf * 512, 512)
                    sbo, cbo = get_slab(w_b_d, 0, 8, hf * 512, 512)
                    for jj in range(4):
                        j = hf * 4 + jj
                        k1, k2 = tmpf(), tmpf()
                        ga_, gb_ = T(k1), T(k2)
                        bga, bpa, bgb, bpb = bankF(), bankF(), bankF(), bankF()
                        mm_group(bga, sga, cga, jj * 128, 8, fbk, B_FB)
                        V("act", lambda e, ga_=ga_, bga=bga, j=j: e.activation(out=ga_, in_=ps[:, bga, :], func=AF.Tanh, bias=hc(C_BIN + 12 + j), scale=0.5), [B_ps[bga], B_cst], [B_tmp[k1]])
                        mm_group(bpa, sa, ca, jj * 128, 4, lambda k: A6[:, k * TB:(k + 1) * TB], B_A6)
                        mm_group(bgb, sgb, cgb, jj * 128, 8, fbk, B_FB)
                        V("act", lambda e, gb_=gb_, bgb=bgb, j=j: e.activation(out=gb_, in_=ps[:, bgb, :], func=AF.Tanh, bias=hc(C_BIN + 20 + j), scale=0.5), [B_ps[bgb], B_cst], [B_tmp[k2]])
                        mm_group(bpb, sbo, cbo, jj * 128, 8, lambda k: A3[:, k * TB:(k + 1) * TB], B_A3)
                        yield "Y"
                        V("dve", lambda e, ga_=ga_, bpa=bpa: e.scalar_tensor_tensor(out=ga_, in0=ga_, scalar=1.0, in1=ps[:, bpa, :], op0=ALU.add, op1=ALU.mult), [B_tmp[k1], B_ps[bpa]], [B_tmp[k1]])
                        V("dve", lambda e, gb_=gb_, bpb=bpb: e.scalar_tensor_tensor(out=gb_, in0=gb_, scalar=1.0, in1=ps[:, bpb, :], op0=ALU.add, op1=ALU.mult), [B_tmp[k2], B_ps[bpb]], [B_tmp[k2]])
                        V("dve", lambda e, ga_=ga_, gb_=gb_, j=j: e.tensor_tensor(out=hid[:, j * TB:(j + 1) * TB], in0=gb_, in1=ga_, op=ALU.add), [B_tmp[k1], B_tmp[k2]], [B_hid[j]])
                        if jj == 3:
                            release_slab(sga); release_slab(sgb); release_slab(sa); release_slab(sbo)
                        yield 7.5
                prog["merged"] = b
                if b == 0:
                    dbg("merged", hid[:, 0:8 * TB], B_hid[0:8])
                for hf in range(2):
                    si, cols = get_slab(w_o_d, 0, 8, hf * 512, 512)
                    for jj in range(4):
                        j = hf * 4 + jj
                        bn = bankF()
                        mm_group(bn, si, cols, jj * 128, 8, lambda k: hid[:, k * TB:(k + 1) * TB], B_hid)
                        yield "Y"
                        V("dve", lambda e, j=j, bn=bn: e.scalar_tensor_tensor(out=x_fm[:, j * TB:(j + 1) * TB], in0=ps[:, bn, :], scalar=0.25, in1=x_fm[:, j * TB:(j + 1) * TB], op0=ALU.mult, op1=ALU.add), [B_ps[bn], B_xfm[j]], [B_xfm[j]])
                        if jj == 3:
                            release_slab(si)
                        yield 2.2
                if b == 0:
                    dbg("x1", x_fm[:], B_xfm)

            def F_rest(b):
                t0 = b * TB
                fb_k = lambda k: FB[:, k * TB:(k + 1) * TB]
                yield from rmsnorm_fm(C_GFFN, FB, B_FB, TB)
                yield 4.0
                pend_ev = []
                for s in range(6):
                    cols_ = 512 if s < 5 else 256
                    sg, cg = get_slab(w_fg_d, 0, 8, s * 512, cols_)
                    su, cu = get_slab(w_fu_d, 0, 8, s * 512, cols_)
                    for jj in range(cols_ // 128):
                        c = s * 4 + jj
                        bg_, bu_ = bankF(), bankF()
                        mm_group(bg_, sg, cg, jj * 128, 8, fb_k, B_FB)
                        mm_group(bu_, su, cu, jj * 128, 8, fb_k, B_FB)
                        for fn_ in pend_ev:
                            fn_()
                        pend_ev = []

                        def ev(bg_=bg_, bu_=bu_, c=c):
                            k1 = tmpf()
                            tg = T(k1)
                            V("act", lambda e: e.activation(out=tg, in_=ps[:, bg_, :], func=AF.Silu), [B_ps[bg_]], [B_tmp[k1]])
                            V("dve", lambda e: e.tensor_tensor(out=hid[:, c * TB:(c + 1) * TB], in0=tg, in1=ps[:, bu_, :], op=ALU.mult), [B_tmp[k1], B_ps[bu_]], [B_hid[c]])
                        pend_ev.append(ev)
                        if jj == cols_ // 128 - 1:
                            release_slab(sg); release_slab(su)
                        yield 4.3
                for fn_ in pend_ev:
                    fn_()
                pend_ev = []
                for j in range(8):
                    si, cols = get_slab(w_fd_d, 0, 22, j * 128, 128)
                    bn = bankF()
                    for k in range(22):
                        V("pe", lambda e, k=k, si=si, cols=cols, bn=bn: e.matmul(ps[:, bn, :], lhsT=slab_lhsT(si, cols, k, 0), rhs=hid[:, k * TB:(k + 1) * TB], start=(k == 0), stop=(k == 21)), [B_slot[si[1]], B_hid[k]], [B_ps[bn]])
                    for fn_ in pend_ev:
                        fn_()
                    pend_ev = []

                    def evd(j=j, bn=bn):
                        V("dve", lambda e: e.tensor_tensor(out=x_fm[:, j * TB:(j + 1) * TB], in0=ps[:, bn, :], in1=x_fm[:, j * TB:(j + 1) * TB], op=ALU.add), [B_ps[bn], B_xfm[j]], [B_xfm[j]])
                    pend_ev.append(evd)
                    release_slab(si)
                    yield 6.0
                for fn_ in pend_ev:
                    fn_()
                pend_ev = []
                prog["down"] = b
                if b == 0:
                    dbg("x2", x_fm[:], B_xfm)
                yield from rmsnorm_fm(C_GPLEG, FB, B_FB, TB)
                yield 4.0
                pi_ = tmpf(2)
                pstage = T(pi_, 2)
                Bp = [B_tmp[pi_], B_tmp[pi_ + 1]]
                em.dma("sp", "dp", [(pstage.rearrange("p (t c) -> p t c", c=PLE), p_d[t0:t0 + TB, :].rearrange("(t r) c -> r t c", r=128))], writes=Bp)
                for kk in range(2):
                    bn = bankF()
                    for tt in range(4):
                        V("pe", lambda e, kk=kk, tt=tt, bn=bn, pstage=pstage: e.transpose(out=ps[:, bn, tt * 128:(tt + 1) * 128], in_=pstage[:, tt * PLE + kk * 128:tt * PLE + (kk + 1) * 128], identity=ident[:]), Bp + P, [B_ps[bn]])
                    V("act", lambda e, kk=kk, bn=bn: e.activation(out=p_fm[:, kk * TB:(kk + 1) * TB], in_=ps[:, bn, :], func=AF.Copy), [B_ps[bn]], [B_pfm[kk]])
                spl, cpl = get_slab(w_pl_d, 0, 2, 0, 1024)
                spg = [get_slab(w_pg_d, 0, 8, hf * 512, 512) for hf in range(2)]
                for tt in range(4):
                    tsl = slice(tt * 128, (tt + 1) * 128)
                    ei_ = tmpf(2)
                    et = T(ei_, 2)
                    Be = [B_tmp[ei_], B_tmp[ei_ + 1]]
                    ti2 = tmpf(2)
                    tp = T(ti2, 2)
                    Bt = [B_tmp[ti2], B_tmp[ti2 + 1]]
                    for hf in range(2):
                        be = bankF()
                        for kk in range(2):
                            V("pe", lambda e, hf=hf, kk=kk, tsl=tsl, spl=spl, be=be: e.matmul(ps[:, be, :], lhsT=p_fm[:, kk * TB:(kk + 1) * TB][:, tsl], rhs=slots[spl[1]][:, kk * 1024 + hf * 512:kk * 1024 + (hf + 1) * 512], start=(kk == 0), stop=(kk == 1)), [B_pfm[kk], B_slot[spl[1]]], [B_ps[be]])
                        V("act", lambda e, hf=hf, be=be, et=et: e.activation(out=et[:, hf * 512:(hf + 1) * 512], in_=ps[:, be, :], func=AF.Copy), [B_ps[be]], [Be[hf]])
                    sse, rs = small[:, 1:2], small[:, 2:3]
                    V("dve", lambda e, sse=sse: e.memset(sse, 0.0), [B_smallF], [B_smallF])
                    V("act", lambda e, tp=tp, et=et, sse=sse: e.activation(out=tp, in_=et, func=AF.Square, accum_out=sse), Be + [B_smallF], Bt + [B_smallF])
                    yield "Y"
                    V("dve", lambda e, sse=sse, rs=rs: e.tensor_scalar(out=rs, in0=sse, scalar1=1.0 / D, scalar2=EPS, op0=ALU.mult, op1=ALU.add), [B_smallF], [B_smallF])
                    V("act", lambda e, rs=rs: e.activation(out=rs, in_=rs, func=AF.Sqrt), [B_smallF], [B_smallF])
                    V("dve", lambda e, rs=rs: e.reciprocal(out=rs, in_=rs), [B_smallF], [B_smallF])
                    for hf in range(2):
                        hs = slice(hf * 512, (hf + 1) * 512)
                        sgi, cgi = spg[hf]
                        bg_ = bankF()
                        for k in range(8):
                            V("pe", lambda e, k=k, tsl=tsl, sgi=sgi, bg_=bg_: e.matmul(ps[:, bg_, :], lhsT=fb_k(k)[:, tsl], rhs=slots[sgi[1]][:, k * 512:(k + 1) * 512], start=(k == 0), stop=False), [B_FB[k], B_slot[sgi[1]]], [B_ps[bg_]])
                        V("pe", lambda e, hf=hf, bg_=bg_: e.matmul(ps[:, bg_, :], lhsT=ones_row[0:1, :], rhs=bpg_row[0:1, hf * 512:(hf + 1) * 512], start=False, stop=True), P, [B_ps[bg_]])
                        V("act", lambda e, tp=tp, hs=hs, bg_=bg_: e.activation(out=tp[:, hs], in_=ps[:, bg_, :], func=AF.Tanh, scale=0.5), [B_ps[bg_]], [Bt[hf]])
                        bx = bankF()
                        for kk in range(4):
                            k = hf * 4 + kk
                            V("pe", lambda e, k=k, kk=kk, tsl=tsl, bx=bx: e.transpose(out=ps[:, bx, kk * 128:(kk + 1) * 128], in_=x_fm[:, k * TB:(k + 1) * TB][:, tsl], identity=ident[:]), [B_xfm[k]] + P, [B_ps[bx]])
                        V("dve", lambda e, et=et, hs=hs, rs=rs: e.scalar_tensor_tensor(out=et[:, hs], in0=et[:, hs], scalar=rs, in1=gple_bc[:, hs], op0=ALU.mult, op1=ALU.mult), [Be[hf], B_smallF] + P, [Be[hf]])
                        V("dve", lambda e, et=et, tp=tp, hs=hs: e.scalar_tensor_tensor(out=et[:, hs], in0=tp[:, hs], scalar=1.0, in1=et[:, hs], op0=ALU.add, op1=ALU.mult), [Bt[hf], Be[hf]], [Be[hf]])
                        V("dve", lambda e, et=et, hs=hs, bx=bx: e.scalar_tensor_tensor(out=et[:, hs], in0=et[:, hs], scalar=0.5, in1=ps[:, bx, :], op0=ALU.mult, op1=ALU.add), [Be[hf], B_ps[bx]], [Be[hf]])
                    ss3, r3 = small[:, 3:4], small[:, 4:5]
                    V("dve", lambda e, ss3=ss3: e.memset(ss3, 0.0), [B_smallF], [B_smallF])
                    V("act", lambda e, tp=tp, et=et, ss3=ss3: e.activation(out=tp, in_=et, func=AF.Square, accum_out=ss3), Be + [B_smallF], Bt + [B_smallF])
                    yield "Y"
                    V("dve", lambda e, ss3=ss3, r3=r3: e.tensor_scalar(out=r3, in0=ss3, scalar1=1.0 / D, scalar2=EPS, op0=ALU.mult, op1=ALU.add), [B_smallF], [B_smallF])
                    V("act", lambda e, r3=r3: e.activation(out=r3, in_=r3, func=AF.Sqrt), [B_smallF], [B_smallF])
                    V("dve", lambda e, r3=r3: e.reciprocal(out=r3, in_=r3), [B_smallF], [B_smallF])
                    V("dve", lambda e, et=et, r3=r3: e.scalar_tensor_tensor(out=et, in0=et, scalar=r3, in1=gfin_bc[:], op0=ALU.mult, op1=ALU.mult), Be + [B_smallF] + P, Be)
                    oi = ei_ - NTS
                    em.dma("sp", f"do{oi}", [(out_d[t0 + tt * 128:t0 + (tt + 1) * 128, :], et)], reads=Be, writes=[B_out[oi]])
                    if tt == 3:
                        release_slab(spl); release_slab(spg[0][0]); release_slab(spg[1][0])
                    yield 7.5

            def run_all(g):
                for _ in g:
                    pass

            prog = {"merged": -1, "down": -1}

            def interleave(gF, gS, totF, totS):
                pF = pS = 0.0
                doneF = doneS = False
                s_blocked = False
                force_s = False
                while not (doneF and doneS):
                    pickF = (not doneF) and (doneS or s_blocked or pF / totF <= pS / totS)
                    if force_s and not doneS and not s_blocked:
                        pickF = False
                    force_s = False
                    if pickF:
                        s_blocked = False
                        try:
                            c_ = next(gF)
                            if c_ == "Y":
                                force_s = True
                            else:
                                pF += c_ or 1.0
                        except StopIteration:
                            doneF = True
                    else:
                        try:
                            c_ = next(gS)
                            if c_ == "W":
                                assert not doneF, "scan thread blocked forever"
                                s_blocked = True
                            else:
                                pS += c_ or 1.0
                        except StopIteration:
                            doneS = True

            def F_thread(b):
                yield from F_head(b)
                yield from F_rest(b)

            run_all(S_thread(0))
            for b in range(NB):
                if b + 1 < NB and PIPELINE:
                    interleave(F_thread(b), S_thread(b + 1), 330.0, 400.0)
                else:
                    run_all(F_thread(b))
                    if b + 1 < NB:
                        run_all(S_thread(b + 1))
            em.final_wait("sp", B_out + dbg_bufs)
            return plan_rec


        dry = Em(eng_names, sems)
        plan0 = program(dry, None, False)
        em = Em(eng_names, sems)
        program(em, plan0, debug)

        def replay(name):
            def f(e):
                for t in em.thunks[name]:
                    t(e)
            return f
        block.sync(replay("sp"))
        block.scalar(replay("act"))
        block.vector(replay("dve"))
        block.gpsimd(replay("pool"))
        block.tensor(replay("pe"))
    return nc


def _host_layout(inp):
    f = np.float32
    sq = lambda a: np.ascontiguousarray(np.asarray(a, dtype=f)[0])
    colp = np.zeros((128, NCOL), f)

    def put(c0, vec):
        v = np.asarray(vec, dtype=f).reshape(-1, 128)
        colp[:, c0:c0 + v.shape[0]] = v.T
    put(C_GMIX, sq(inp["g_mix"]))
    put(C_BIN, sq(inp["b_in"]))
    put(C_S5D, sq(inp["s5_d"]).reshape(-1))
    put(C_BGLU, sq(inp["b_glu"]))
    cw = sq(inp["conv_w"])
    for k in range(4):
        put(C_CONVW + 8 * k, cw[k])
    put(C_CONVB, sq(inp["conv_b"]))
    put(C_BR, sq(inp["b_r"]).reshape(-1))
    put(C_BI, sq(inp["b_i"]).reshape(-1))
    put(C_LAM, sq(inp["lru_lambda"]))
    put(C_GFFN, sq(inp["g_ffn"]))
    put(C_GPLEG, sq(inp["g_ple_gate"]))
    lam_re, lam_im, log_dt = sq(inp["lam_re"]), sq(inp["lam_im"]), sq(inp["log_dt"])
    s5p = np.zeros((128, 48), f)
    s5p[:, 0:16] = lam_re.reshape(16, 2, 64).transpose(1, 2, 0).reshape(128, 16)
    s5p[:, 16:32] = lam_im.reshape(16, 2, 64).transpose(1, 2, 0).reshape(128, 16)
    s5p[:, 32:48] = np.broadcast_to(log_dt.reshape(16, 2, 1), (16, 2, 64)).transpose(1, 2, 0).reshape(128, 16)
    rowp = np.stack([sq(inp["b_ple_gate"]), sq(inp["g_ple"]), np.asarray(inp["g_final"], dtype=f)], 0)
    w_r, w_i = sq(inp["w_r"]), sq(inp["w_i"])
    wr_bd = np.zeros((128, 8, 128), f)
    wi_bd = np.zeros((128, 8, 128), f)
    for t in range(8):
        for h2 in range(2):
            wr_bd[h2 * 64:(h2 + 1) * 64, t, h2 * 64:(h2 + 1) * 64] = w_r[2 * t + h2]
            wi_bd[h2 * 64:(h2 + 1) * 64, t, h2 * 64:(h2 + 1) * 64] = w_i[2 * t + h2]
    b_re, b_im = sq(inp["s5_b_re"]), sq(inp["s5_b_im"])
    c_re, c_im = sq(inp["s5_c_re"]), sq(inp["s5_c_im"])
    s5B = np.zeros((128, 2, 16, 128), f)
    s5C = np.zeros((128, 2, 16, 32), f)
    for i in range(16):
        q = i % 4
        for g2 in range(2):
            gidx = 2 * i + g2
            r0 = 32 * q + 16 * g2
            s5B[r0:r0 + 16, 0, i, g2 * 64:(g2 + 1) * 64] = b_re[gidx].T
            s5B[r0:r0 + 16, 1, i, g2 * 64:(g2 + 1) * 64] = b_im[gidx].T
            s5C[g2 * 64:(g2 + 1) * 64, 0, i, g2 * 16:(g2 + 1) * 16] = c_re[gidx].T
            s5C[g2 * 64:(g2 + 1) * 64, 1, i, g2 * 16:(g2 + 1) * 16] = c_im[gidx].T
    ident = np.eye(128, dtype=f)
    iota = np.ascontiguousarray(np.broadcast_to(np.arange(TC, dtype=f)[None, None, :], (128, 16, TC))).reshape(128, 16 * TC)
    shared = {
        "w_in": sq(inp["w_in"]), "w_glu": sq(inp["w_glu"]), "w_a_out": sq(inp["w_a_out"]), "w_b_out": sq(inp["w_b_out"]),
        "w_o": sq(inp["w_o"]), "w_ffn_gate": sq(inp["w_ffn_gate"]), "w_ffn_up": sq(inp["w_ffn_up"]), "w_ffn_down": sq(inp["w_ffn_down"]),
        "w_ple_gate": sq(inp["w_ple_gate"]), "w_ple": sq(inp["w_ple"]),
        "colp": colp, "s5p": s5p, "rowp": np.ascontiguousarray(rowp),
        "wr_bd": wr_bd.reshape(128, -1), "wi_bd": wi_bd.reshape(128, -1),
        "s5B": s5B.reshape(128, -1), "s5C": s5C.reshape(128, -1), "ident": ident, "iota": iota,
    }
    return shared


def kernel(**inputs):
    x = np.asarray(inputs["x"], dtype=np.float32)
    p = np.asarray(inputs["p"], dtype=np.float32)[0]
    shared = _host_layout(inputs)
    nc = build_nc()
    in_maps = []
    for c in range(8):
        m = dict(shared)
        m["x"] = np.ascontiguousarray(x[c])
        m["p"] = np.ascontiguousarray(p[c])
        in_maps.append(m)
    res = run_bass_kernel_spmd(nc, in_maps, core_ids=list(range(8)))
    return np.stack([np.asarray(r["out"], dtype=np.float32) for r in res.results], 0)
```

```python
import math
from contextlib import ExitStack

import numpy as np
import concourse.bass as bass
import concourse.mybir as mybir
from concourse.bass_utils import run_bass_kernel_spmd

F32 = mybir.dt.float32
BF16 = mybir.dt.bfloat16
I32 = mybir.dt.int32
AF = mybir.ActivationFunctionType
ALU = mybir.AluOpType

D = 1024
L = 2048
TB = 512
NB = L // TB
PLE = 256
S5W = 512
FFN = 2816
INC = 3584
TC = 256
ROT_ENG = "dve"
PIPELINE = True
NSUB = TB // TC
NSLOT = 5
SLOT_ELEMS = 4096
EPS = 1e-6
TWO_PI = float(2 * math.pi)
PI = 3.1415925
HALF_PI = float(math.pi / 2)

C_GMIX, C_BIN, C_S5D, C_BGLU, C_CONVW, C_CONVB, C_BR, C_BI, C_LAM, C_GFFN, C_GPLEG, NCOL = 0, 8, 36, 40, 44, 76, 84, 92, 100, 108, 116, 124


class Buf:
    __slots__ = ("name", "w", "r")

    def __init__(self, name=""):
        self.name = name
        self.w = None
        self.r = {}


class Em:
    def __init__(self, engines, sems):
        self.sem = sems
        self.cnt = {k: 0 for k in sems}
        self.waited = {e: {} for e in engines}
        self.thunks = {e: [] for e in engines}
        self.self_sync = True

    def _waits(self, eng_name, reads, writes):
        deps = {}

        def add(d):
            if d is None:
                return
            k, v = d
            if deps.get(k, 0) < v:
                deps[k] = v
        for b in reads:
            add(b.w)
        for b in writes:
            add(b.w)
            for d in b.r.items():
                add(d)
        for k, v in deps.items():
            if k == eng_name and (eng_name == "pe" or not self.self_sync):
                continue
            if self.waited[eng_name].get(k, 0) >= v:
                continue
            self.thunks[eng_name].append(lambda e, s=self.sem[k], v=v: e.wait_ge(s, v))
            self.waited[eng_name][k] = v

    def _commit(self, me, reads, writes):
        for b in reads:
            if b.r.get(me[0], 0) < me[1]:
                b.r[me[0]] = me[1]
        for b in writes:
            b.w = me
            b.r = {}

    def op(self, eng_name, fn, reads=(), writes=()):
        self._waits(eng_name, reads, writes)
        self.cnt[eng_name] += 1
        self.thunks[eng_name].append(lambda e, fn=fn, s=self.sem[eng_name]: fn(e).then_inc(s, 1))
        self._commit((eng_name, self.cnt[eng_name]), reads, writes)

    def dma(self, queue, sem_key, pairs, reads=(), writes=()):
        self._waits(queue, reads, writes)
        for (o, i) in pairs:
            self.cnt[sem_key] += 16
            self.thunks[queue].append(lambda e, o=o, i=i, s=self.sem[sem_key]: e.dma_start(out=o, in_=i).then_inc(s, 16))
        self._commit((sem_key, self.cnt[sem_key]), reads, writes)

    def final_wait(self, eng_name, bufs):
        self._waits(eng_name, bufs, bufs)


def build_nc(debug=False):
    nc = bass.Bass("TRN2", target_bir_lowering=False)

    def din(name, shape):
        return nc.dram_tensor(name, list(shape), F32, kind="ExternalInput").ap()

    x_d = din("x", [L, D])
    p_d = din("p", [L, PLE])
    w_in_d = din("w_in", [D, INC])
    w_glu_d = din("w_glu", [S5W, S5W])
    w_a_d = din("w_a_out", [S5W, D])
    w_b_d = din("w_b_out", [D, D])
    w_o_d = din("w_o", [D, D])
    w_fg_d = din("w_ffn_gate", [D, FFN])
    w_fu_d = din("w_ffn_up", [D, FFN])
    w_fd_d = din("w_ffn_down", [FFN, D])
    w_pg_d = din("w_ple_gate", [D, D])
    w_pl_d = din("w_ple", [PLE, D])
    colp_d = din("colp", [128, NCOL])
    s5p_d = din("s5p", [128, 48])
    rowp_d = din("rowp", [3, D])
    wrbd_d = din("wr_bd", [128, 8 * 128])
    wibd_d = din("wi_bd", [128, 8 * 128])
    s5B_d = din("s5B", [128, 2 * 16 * 128])
    s5C_d = din("s5C", [128, 2 * 16 * 32])
    ident_d = din("ident", [128, 128])
    iota_d = din("iota", [128, 16 * TC])
    out_d = nc.dram_tensor("out", [L, D], F32, kind="ExternalOutput").ap()

    with ExitStack() as es:
        def sb(name, shape, dt):
            return es.enter_context(nc.sbuf_tensor(name, list(shape), dt))

        slots = [sb(f"slot{i}", [128, SLOT_ELEMS], BF16) for i in range(NSLOT)]
        w_glu_sb = sb("w_glu_sb", [128, 4 * 512], BF16)
        wr_sb = sb("wr_sb", [128, 8 * 128], BF16)
        wi_sb = sb("wi_sb", [128, 8 * 128], BF16)
        Bsb = sb("Bsb", [128, 2 * 16 * 128], BF16)
        Csb = sb("Csb", [128, 2 * 16 * 32], BF16)
        Er = sb("Er", [128, 16 * TC], F32)
        Ei = sb("Ei", [128, 16 * TC], F32)
        ident = sb("ident_sb", [128, 128], F32)
        ident_bf = sb("ident_bf", [128, 128], BF16)
        colp = sb("colp_sb", [128, NCOL], F32)
        hcol = sb("hcol", [128, NCOL], F32)
        dcol = sb("dcol", [128, 64], F32)
        s5c = sb("s5c", [128, 16 * 24], F32)
        s5ci = sb("s5ci", [128, 16], I32)
        gple_bc = sb("gple_bc", [128, D], F32)
        gfin_bc = sb("gfin_bc", [128, D], F32)
        bpg_row = sb("bpg_row", [1, D], BF16)
        ones_bf = sb("ones_bf", [128, 128], BF16)
        ones_row = sb("ones_row", [1, 128], BF16)
        x_fm = sb("x_fm", [128, 8 * TB], F32)
        A2 = sb("A2", [128, 8 * (TB + 3)], BF16)
        A3 = sb("A3", [128, 8 * TB], BF16)
        A4 = sb("A4", [128, 4 * TB], BF16)
        FB = sb("FB", [128, 8 * TB], BF16)
        hid = sb("hid", [128, 22 * TB], BF16)
        NTS, NTF = 8, 7
        NT = NTS + NTF
        tmp = sb("tmp", [128, NT * 512], F32)
        A1 = tmp[:].bitcast(BF16)[:, 4 * 1024:8 * 1024]
        A6 = hid[:, 8 * TB:12 * TB]
        sqb = sb("sqb", [128, 2 * TB], BF16)
        p_fm = sqb
        sbf = sb("sbf", [128, 4 * TB], BF16)
        xcb = sb("xcb", [128, TB], BF16)
        carry_s5 = sb("carry_s5", [128, 32], F32)
        carry_lru = sb("carry_lru", [128, 8], F32)
        small = sb("small", [128, 96], F32)
        ps = es.enter_context(nc.psum_tensor("ps", [128, 8, 512], F32))

        eng_names = ["sp", "act", "dve", "pool", "pe"]
        sem_keys = eng_names + [f"dslot{i}" for i in range(NSLOT)] + ["dparam", "dparam2", "dx0", "dx1", "dxf0", "dxf1", "dxf2", "dp", "do0", "do1", "do2", "do3", "do4", "do5", "do6", "ddbg"]
        sems = {k: es.enter_context(nc.semaphore("s_" + k)) for k in sem_keys}
        block = es.enter_context(nc.Block())

        def program(em, plan, dbg_on):
            B_ps = [Buf(f"ps{i}") for i in range(8)]
            B_slot = [Buf(f"slot{i}") for i in range(NSLOT)]
            B_param = Buf("param")
            B_param2 = Buf("param2")
            B_cst = Buf("cst")
            B_xfm = [Buf(f"xfm{k}") for k in range(8)]
            B_A2 = [Buf(f"A2_{k}") for k in range(8)]
            B_A3 = [Buf(f"A3_{k}") for k in range(8)]
            B_A4 = [Buf(f"A4_{k}") for k in range(4)]
            B_FB = [Buf(f"FB_{k}") for k in range(8)]
            B_hid = [Buf(f"hid{k}") for k in range(22)]
            B_pst = Buf("pst")
            B_tmp = [Buf(f"tmp{i}") for i in range(NT)]
            B_A1 = [B_tmp[4 + k // 2] for k in range(8)]
            B_A6 = B_hid[8:12]
            B_sq = [Buf("sq0"), Buf("sq1")]
            B_sbf = [Buf(f"sbf{i}") for i in range(4)]
            B_xc = [Buf("xc0"), Buf("xc1")]
            B_cs5 = Buf("carry_s5")
            B_clru = Buf("carry_lru")
            B_small = Buf("small")
            B_smallF = Buf("smallF")
            B_s5q = [[Buf(f"s5q{a}{c}") for c in range(4)] for a in range(2)]
            B_sqF = [Buf("sqF0"), Buf("sqF1")]
            B_pfm = B_sqF
            B_tab = Buf("tables")
            B_out = [Buf(f"out{i}") for i in range(NTF)]

            st = {"bank": 0, "bankS": 0, "bankF": 0, "tmpS": 0, "tmpF": 0, "tmpL": 0}
            dbg_bufs = []

            def dbg(name, ap, bufs):
                if not dbg_on:
                    return
                shp = [int(v) for v in ap.shape]
                d = nc.dram_tensor("dbg_" + name, shp, ap.dtype, kind="ExternalOutput").ap()
                ob = Buf("dbg_" + name)
                em.dma("sp", "ddbg", [(d, ap)], reads=list(bufs), writes=[ob])
                dbg_bufs.append(ob)

            def bank():
                b = st["bank"]
                st["bank"] = (b + 1) % 8
                return b

            SBANKS = [0, 1, 2, 3]
            LBANKS = [0, 0]
            XBANKS = [1, 2]
            FBANKS = [4, 5, 6, 7]

            def bankS():
                b = st["bankS"]
                st["bankS"] = (b + 1) % len(SBANKS)
                return SBANKS[b]

            def bankF():
                b = st["bankF"]
                st["bankF"] = (b + 1) % len(FBANKS)
                return FBANKS[b]

            def tmpi(n=1):
                i = st["tmpS"]
                if i + n > 4:
                    i = 0
                st["tmpS"] = (i + n) % 4
                return i

            def tmpl():
                i = st["tmpL"]
                st["tmpL"] = (i + 1) % 4
                return 4 + i

            def tmpf(n=1):
                i = st["tmpF"]
                if i + n > NTF:
                    i = 0
                st["tmpF"] = (i + n) % NTF
                return NTS + i

            def T(i, n=1):
                return tmp[:, i * 512:(i + n) * 512]

            def col(c):
                return colp[:, c:c + 1]

            def hc(c):
                return hcol[:, c:c + 1]

            plan_rec = []
            ws = {"issued": 0, "next": 0}

            def _plan():
                return plan if plan is not None else plan_rec

            released = set()

            def _issue_one():
                lst = _plan()
                n = ws["issued"]
                w_ap, k0, kt, c0, cols = lst[n]
                src = w_ap.rearrange("(k p) c -> p k c", p=128)[:, k0:k0 + kt, c0:c0 + cols]
                si = n % NSLOT
                dst = slots[si][:, 0:kt * cols].rearrange("p (k c) -> p k c", c=cols)
                em.dma("pool", f"dslot{si}", [(dst, src)], writes=[B_slot[si]])
                ws["issued"] = n + 1

            def try_issue():
                lst = _plan()
                while ws["issued"] < len(lst) and (ws["issued"] < NSLOT or (ws["issued"] - NSLOT) in released):
                    _issue_one()

            def issue_slab():
                try_issue()

            def get_slab(w_ap, k0, kt, c0, cols):
                n = ws["next"]
                ws["next"] = n + 1
                if plan is None:
                    plan_rec.append((w_ap, k0, kt, c0, cols))
                else:
                    assert plan[n][0] is w_ap and tuple(plan[n][1:]) == (k0, kt, c0, cols), "slab plan mismatch"
                try_issue()
                assert ws["issued"] > n, "slab ring exhausted (too many live slabs)"
                return (n, n % NSLOT), cols

            def release_slab(h):
                released.add(h[0])
                try_issue()

            def slab_lhsT(h, cols, k, c0, n=128):
                return slots[h[1]][:, k * cols + c0:k * cols + c0 + n]

            em.dma("sp", "dparam", [
                (colp[:], colp_d), (s5c[:, 0:48], s5p_d), (ident[:], ident_d),
                (gple_bc[:], rowp_d[1:2, :].broadcast_to([128, D])),
                (gfin_bc[:], rowp_d[2:3, :].broadcast_to([128, D])),
                (tmp[:, 0:1024], s5C_d), (x_fm[:, 0:16 * TC], iota_d),
            ], writes=[B_param, B_tmp[0], B_tmp[1]] + B_xfm)
            em.dma("pool", "dparam2", [
                (w_glu_sb[:].rearrange("p (k c) -> p k c", c=512), w_glu_d.rearrange("(k p) c -> p k c", p=128)),
                (wr_sb[:], wrbd_d), (wi_sb[:], wibd_d), (Bsb[:], s5B_d), (bpg_row[:], rowp_d[0:1, :]),
            ], writes=[B_param2])
            if plan is not None:
                try_issue()

            P = [B_param, B_param2, B_cst]

            def V(eng, fn, reads, writes):
                em.op(eng, fn, reads=reads, writes=writes)

            V("dve", lambda e: e.memset(ones_bf[:], 1.0 / D), [], [B_cst])
            V("dve", lambda e: e.memset(ones_row[:], 1.0), [], [B_cst])
            V("dve", lambda e: e.tensor_copy(out=ident_bf[:], in_=ident[:]), [B_param], [B_cst])
            V("dve", lambda e: e.memset(small[:], 0.0), [], [B_cst])
            V("dve", lambda e: e.memset(small[:, 0:1], EPS), [B_cst], [B_cst])
            V("dve", lambda e: e.memset(carry_s5[:], 0.0), [], [B_cs5])
            V("dve", lambda e: e.memset(carry_lru[:], 0.0), [], [B_clru])
            V("dve", lambda e: e.memset(A2[:], 0.0), [], B_A2)
            V("dve", lambda e: e.memset(small[:, 5:6], 1.0), [B_cst], [B_cst])
            eps_col = small[:, 0:1]
            one_col = small[:, 5:6]
            V("dve", lambda e: e.tensor_scalar(out=hcol[:], in0=colp[:], scalar1=0.5, scalar2=None, op0=ALU.mult), P, [B_cst])
            V("act", lambda e: e.activation(out=dcol[:, 16:24], in_=colp[:, C_LAM:C_LAM + 8], func=AF.Exp, scale=-1.0), P, [B_cst])
            V("act", lambda e: e.activation(out=dcol[:, 24:32], in_=dcol[:, 16:24], func=AF.Ln, bias=one_col, scale=1.0), [B_cst], [B_cst])
            V("dve", lambda e: e.tensor_scalar(out=dcol[:, 0:8], in0=dcol[:, 24:32], scalar1=-4.0, scalar2=None, op0=ALU.mult), [B_cst], [B_cst])
            V("dve", lambda e: e.tensor_scalar(out=dcol[:, 8:16], in0=dcol[:, 24:32], scalar1=-8.0, scalar2=None, op0=ALU.mult), [B_cst], [B_cst])

            def g(i):
                return s5c[:, 16 * i:16 * i + 16]
            S = [B_cst]

            def reduce_angle(dst, src, n, scr_i, scr_f):
                V("dve", lambda e: e.tensor_scalar(out=scr_i, in0=src, scalar1=1.0 / TWO_PI, scalar2=None, op0=ALU.mult), S + P + [B_tab], S + [B_tab])
                V("dve", lambda e: e.tensor_copy(out=scr_f, in_=scr_i), S + [B_tab], S + [B_tab])
                V("dve", lambda e: e.scalar_tensor_tensor(out=dst, in0=scr_f, scalar=-TWO_PI, in1=src, op0=ALU.mult, op1=ALU.add), S + [B_tab], S + [B_tab])
                V("dve", lambda e: e.tensor_scalar(out=dst, in0=dst, scalar1=-PI, scalar2=PI, op0=ALU.max, op1=ALU.min), S + [B_tab], S + [B_tab])

            def cos_arg(dst, src, scr):
                V("dve", lambda e: e.tensor_scalar(out=dst, in0=src, scalar1=HALF_PI, scalar2=None, op0=ALU.add), S + [B_tab], S + [B_tab])
                V("dve", lambda e: e.tensor_scalar(out=scr, in0=dst, scalar1=PI, scalar2=None, op0=ALU.is_gt), S + [B_tab], S + [B_tab])
                V("dve", lambda e: e.scalar_tensor_tensor(out=dst, in0=scr, scalar=-TWO_PI, in1=dst, op0=ALU.mult, op1=ALU.add), S + [B_tab], S + [B_tab])
                V("dve", lambda e: e.tensor_scalar(out=dst, in0=dst, scalar1=-PI, scalar2=PI, op0=ALU.max, op1=ALU.min), S + [B_tab], S + [B_tab])

            V("act", lambda e: e.activation(out=g(3), in_=g(2), func=AF.Exp), P, S)
            V("dve", lambda e: e.tensor_tensor(out=g(15), in0=g(0), in1=g(3), op=ALU.mult), S, S)
            V("act", lambda e: e.activation(out=g(4), in_=g(15), func=AF.Exp), S, S)
            V("dve", lambda e: e.tensor_tensor(out=g(5), in0=g(1), in1=g(3), op=ALU.mult), S, S)
            reduce_angle(g(6), g(5), 16, s5ci[:], g(15))
            V("act", lambda e: e.activation(out=g(7), in_=g(6), func=AF.Sin), S, S)
            cos_arg(g(16), g(6), g(15))
            V("act", lambda e: e.activation(out=g(8), in_=g(16), func=AF.Sin), S, S)
            V("dve", lambda e: e.tensor_tensor(out=g(9), in0=g(4), in1=g(8), op=ALU.mult), S, S)
            V("dve", lambda e: e.tensor_tensor(out=g(10), in0=g(4), in1=g(7), op=ALU.mult), S, S)
            V("dve", lambda e: e.tensor_tensor(out=g(11), in0=g(0), in1=g(0), op=ALU.mult), S, S)
            V("dve", lambda e: e.tensor_tensor(out=g(15), in0=g(1), in1=g(1), op=ALU.mult), S, S)
            V("dve", lambda e: e.tensor_tensor(out=g(11), in0=g(11), in1=g(15), op=ALU.add), S, S)
            V("dve", lambda e: e.reciprocal(out=g(11), in_=g(11)), S, S)
            V("dve", lambda e: e.tensor_scalar(out=g(12), in0=g(9), scalar1=-1.0, scalar2=None, op0=ALU.add), S, S)
            V("dve", lambda e: e.tensor_tensor(out=g(15), in0=g(12), in1=g(0), op=ALU.mult), S, S)
            V("dve", lambda e: e.tensor_tensor(out=g(16), in0=g(10), in1=g(1), op=ALU.mult), S, S)
            V("dve", lambda e: e.tensor_tensor(out=g(15), in0=g(15), in1=g(16), op=ALU.add), S, S)
            V("dve", lambda e: e.tensor_tensor(out=g(13), in0=g(15), in1=g(11), op=ALU.mult), S, S)
            V("dve", lambda e: e.tensor_tensor(out=g(15), in0=g(10), in1=g(0), op=ALU.mult), S, S)
            V("dve", lambda e: e.tensor_tensor(out=g(16), in0=g(12), in1=g(1), op=ALU.mult), S, S)
            V("dve", lambda e: e.tensor_tensor(out=g(15), in0=g(15), in1=g(16), op=ALU.subtract), S, S)
            V("dve", lambda e: e.tensor_tensor(out=g(14), in0=g(15), in1=g(11), op=ALU.mult), S, S)
            V("dve", lambda e: e.tensor_scalar(out=g(19), in0=g(14), scalar1=-1.0, scalar2=None, op0=ALU.mult), S, S)
            V("dve", lambda e: e.tensor_scalar(out=g(20), in0=g(6), scalar1=float(TC), scalar2=None, op0=ALU.mult), S, S)
            reduce_angle(g(21), g(20), 16, s5ci[:], g(15))
            V("act", lambda e: e.activation(out=g(18), in_=g(21), func=AF.Sin), S, S)
            cos_arg(g(22), g(21), g(15))
            V("act", lambda e: e.activation(out=g(17), in_=g(22), func=AF.Sin), S, S)

            Craw = tmp
            for i in range(16):
                cr_i = Craw[:, i * 32:(i + 1) * 32]
                ci_i = Craw[:, 512 + i * 32:512 + (i + 1) * 32]
                sc = small[:, 16:48]
                fr_c = s5c[:, 16 * 13 + i:16 * 13 + i + 1]
                fi_c = s5c[:, 16 * 14 + i:16 * 14 + i + 1]
                nfi_c = s5c[:, 16 * 19 + i:16 * 19 + i + 1]
                o_re = Csb[:, i * 32:(i + 1) * 32]
                o_im = Csb[:, 512 + i * 32:512 + (i + 1) * 32]
                V("dve", lambda e, ci_i=ci_i, fi_c=fi_c, sc=sc: e.tensor_scalar(out=sc, in0=ci_i, scalar1=fi_c, scalar2=None, op0=ALU.mult), S + [B_tmp[0], B_tmp[1]], S)
                V("dve", lambda e, cr_i=cr_i, fr_c=fr_c, sc=sc, o_re=o_re: e.scalar_tensor_tensor(out=o_re, in0=cr_i, scalar=fr_c, in1=sc, op0=ALU.mult, op1=ALU.subtract), S + [B_tmp[0], B_tmp[1]], S + [B_tab])
                V("dve", lambda e, ci_i=ci_i, fr_c=fr_c, sc=sc: e.tensor_scalar(out=sc, in0=ci_i, scalar1=fr_c, scalar2=None, op0=ALU.mult), S + [B_tmp[0], B_tmp[1]], S)
                V("dve", lambda e, cr_i=cr_i, nfi_c=nfi_c, sc=sc, o_im=o_im: e.scalar_tensor_tensor(out=o_im, in0=cr_i, scalar=nfi_c, in1=sc, op0=ALU.mult, op1=ALU.subtract), S + [B_tmp[0], B_tmp[1]], S + [B_tab])

            iota_sb = x_fm[:, 0:16 * TC]
            ph = hid[:].bitcast(F32)[:, 0:16 * TC]
            phi_i = tmp[:, 0:16 * TC].bitcast(I32)
            TT = B_xfm + B_hid + [B_tab]
            for i in range(16):
                thr_c = s5c[:, 16 * 6 + i:16 * 6 + i + 1]
                V("dve", lambda e, i=i, thr_c=thr_c: e.tensor_scalar(out=ph[:, i * TC:(i + 1) * TC], in0=iota_sb[:, i * TC:(i + 1) * TC], scalar1=thr_c, scalar2=None, op0=ALU.mult), S + TT, TT)
            scr_f = x_fm[:, 0:16 * TC]
            TT2 = TT + B_tmp[0:8]
            V("dve", lambda e: e.tensor_scalar(out=phi_i, in0=ph, scalar1=1.0 / TWO_PI, scalar2=None, op0=ALU.mult), S + TT2, TT2)
            V("dve", lambda e: e.tensor_copy(out=scr_f, in_=phi_i), TT2, TT2)
            V("dve", lambda e: e.scalar_tensor_tensor(out=ph, in0=scr_f, scalar=-TWO_PI, in1=ph, op0=ALU.mult, op1=ALU.add), TT2, TT2)
            V("dve", lambda e: e.tensor_scalar(out=ph, in0=ph, scalar1=-PI, scalar2=PI, op0=ALU.max, op1=ALU.min), TT2, TT2)
            V("act", lambda e: e.activation(out=Ei[:], in_=ph, func=AF.Sin), TT2, TT2)
            V("dve", lambda e: e.tensor_scalar(out=ph, in0=ph, scalar1=HALF_PI, scalar2=None, op0=ALU.add), TT2, TT2)
            V("dve", lambda e: e.tensor_scalar(out=scr_f, in0=ph, scalar1=PI, scalar2=None, op0=ALU.is_gt), TT2, TT2)
            V("dve", lambda e: e.scalar_tensor_tensor(out=ph, in0=scr_f, scalar=-TWO_PI, in1=ph, op0=ALU.mult, op1=ALU.add), TT2, TT2)
            V("dve", lambda e: e.tensor_scalar(out=ph, in0=ph, scalar1=-PI, scalar2=PI, op0=ALU.max, op1=ALU.min), TT2, TT2)
            V("act", lambda e: e.activation(out=Er[:], in_=ph, func=AF.Sin), TT2, TT2)
            TAB = [B_tab, B_cst]
            dbg("Er", Er[:], TAB)
            dbg("Ei", Ei[:], TAB)
            dbg("s5c", s5c[:], TAB)
            dbg("Csb", Csb[:], TAB)
            dbg("dcol", dcol[:], TAB)

            def rmsnorm_fm(gc0, dst, dstB, dst_stride):
                bn = bankF()
                for k in range(8):
                    sq = sqb[:, (k % 2) * TB:(k % 2 + 1) * TB]
                    V("act", lambda e, k=k, sq=sq: e.activation(out=sq, in_=x_fm[:, k * TB:(k + 1) * TB], func=AF.Square), [B_xfm[k]], [B_sqF[k % 2]])
                    V("pe", lambda e, k=k, sq=sq, bn=bn: e.matmul(ps[:, bn, :], lhsT=ones_bf[:], rhs=sq, start=(k == 0), stop=(k == 7)), [B_sqF[k % 2], B_cst], [B_ps[bn]])
                ti = tmpf()
                rt = T(ti)
                V("act", lambda e: e.activation(out=rt, in_=ps[:, bn, :], func=AF.Sqrt, bias=eps_col, scale=1.0), [B_ps[bn], B_cst], [B_tmp[ti]])
                yield "Y"
                V("dve", lambda e: e.reciprocal(out=rt, in_=rt), [B_tmp[ti]], [B_tmp[ti]])
                for k in range(8):
                    V("dve", lambda e, k=k: e.scalar_tensor_tensor(out=dst[:, k * dst_stride:k * dst_stride + TB], in0=x_fm[:, k * TB:(k + 1) * TB], scalar=col(gc0 + k), in1=rt, op0=ALU.mult, op1=ALU.mult),
                      [B_xfm[k], B_tmp[ti]] + P, [dstB[k]])

            def mm_group(bn, si, cols, c0, kt, rhs_fn, rhsB, extraB=()):
                for k in range(kt):
                    V("pe", lambda e, k=k: e.matmul(ps[:, bn, :], lhsT=slab_lhsT(si, cols, k, c0), rhs=rhs_fn(k), start=(k == 0), stop=(k == kt - 1)),
                      [B_slot[si[1]], rhsB[k]] + list(extraB), [B_ps[bn]])

            A2S = TB + 3

            def h_k(k):
                return A1[:, k * TB:(k + 1) * TB]

            A2S = TB + 3
            a2v = A2[:].rearrange("p (k t) -> p k t", t=A2S)

            def h_k(k):
                return A1[:, k * TB:(k + 1) * TB]

            def S_thread(b):
                t0 = b * TB
                def SA_a(tt):
                    xs = T(2 * (tt % 2), 2)
                    Bxs = [B_tmp[2 * (tt % 2)], B_tmp[2 * (tt % 2) + 1]]
                    em.dma("sp", f"dx{tt % 2}", [(xs, x_d[t0 + tt * 128:t0 + (tt + 1) * 128, :])], writes=Bxs)
                    xn = A2[:, tt * 1024:(tt + 1) * 1024]
                    ssx, rsx = small[:, 72 + 2 * tt:73 + 2 * tt], small[:, 73 + 2 * tt:74 + 2 * tt]
                    V("dve", lambda e: e.memset(ssx, 0.0), [B_small], [B_small])
                    V("act", lambda e: e.activation(out=xn, in_=xs, func=AF.Square, accum_out=ssx), Bxs + [B_small], list(B_A2) + [B_small])
                    V("dve", lambda e: e.tensor_scalar(out=rsx, in0=ssx, scalar1=1.0 / D, scalar2=EPS, op0=ALU.mult, op1=ALU.add), [B_small], [B_small])

                def SA_b(tt):
                    xs = T(2 * (tt % 2), 2)
                    Bxs = [B_tmp[2 * (tt % 2)], B_tmp[2 * (tt % 2) + 1]]
                    xn = A2[:, tt * 1024:(tt + 1) * 1024]
                    Bxn = list(B_A2)
                    rsx = small[:, 73 + 2 * tt:74 + 2 * tt]
                    V("act", lambda e: e.activation(out=rsx, in_=rsx, func=AF.Sqrt), [B_small], [B_small])
                    V("dve", lambda e: e.reciprocal(out=rsx, in_=rsx), [B_small], [B_small])
                    V("act", lambda e: e.activation(out=xn, in_=xs, func=AF.Copy, scale=rsx), Bxs + [B_small], Bxn)
                    for hh in range(2):
                        bn = bankS()
                        pb = ps[:, bn, :].bitcast(BF16)
                        for kk in range(4):
                            k = hh * 4 + kk
                            V("pe", lambda e, k=k, kk=kk, pb=pb: e.transpose(out=pb[:, kk * 128:(kk + 1) * 128], in_=xn[:, k * 128:(k + 1) * 128], identity=ident_bf[:]), Bxn + P, [B_ps[bn]])
                        gcol3 = colp[:, C_GMIX + hh * 4:C_GMIX + hh * 4 + 4].unsqueeze(2).broadcast_to([128, 4, 128])
                        dst3 = A1[:, hh * 4 * TB:(hh * 4 + 4) * TB].rearrange("p (k t) -> p k t", t=TB)[:, :, tt * 128:(tt + 1) * 128]
                        src3 = pb[:, 0:512].rearrange("p (k t) -> p k t", t=128)
                        V("dve", lambda e, dst3=dst3, src3=src3, gcol3=gcol3: e.tensor_tensor(out=dst3, in0=src3, in1=gcol3, op=ALU.mult), [B_ps[bn]] + P, [B_A1[hh * 4 + kk] for kk in range(4)])

                for step in ("a0", "a1", "b0", "a2", "b1", "a3", "b2", "b3"):
                    (SA_a if step[0] == "a" else SA_b)(int(step[1]))
                    if step[0] == "b":
                        yield 3.0
                if b == 0:
                    dbg("h", A1[:], B_A1)
                def SC_slab(s):
                    si, cols = get_slab(w_in_d, 0, 8, s * 512, 512)
                    for j in range(4):
                        oc = s * 4 + j
                        bn = bankS() if s == 0 else LBANKS[0]
                        mm_group(bn, si, cols, j * 128, 8, h_k, B_A1)
                        if oc < 4:
                            dst, dB = A4[:, oc * TB:(oc + 1) * TB], B_A4[oc]
                        else:
                            t = oc - 4
                            dst, dB = A2[:, t * A2S + 3:t * A2S + 3 + TB], B_A2[t]
                        V("act", lambda e, dst=dst, bn=bn, oc=oc: e.activation(out=dst, in_=ps[:, bn, :], func=AF.Identity, bias=col(C_BIN + oc), scale=1.0), [B_ps[bn]] + P, [dB])
                    release_slab(si)

                SC_slab(0)
                yield 8.0

                def SC_rest():
                    for s in (1, 2):
                        SC_slab(s)
                        yield 8.0
                    if b > 0:
                        V("dve", lambda e: e.tensor_copy(out=a2v[:, :, 0:3], in_=small[:, 48:48 + 24].rearrange("p (k t) -> p k t", t=3)), B_A2 + [B_small], B_A2)
                    else:
                        V("dve", lambda e: e.memset(a2v[:, :, 0:3], 0.0), B_A2, B_A2)
                if b <= 1:
                    dbg(f"ua{b}", A4[:], B_A4)
                    dbg(f"ub{b}", A2[:], B_A2)

                def LRU_gen():
                    for t in range(8):
                        ub = A2[:, t * A2S:(t + 1) * A2S]
                        xc = xcb[:, 0:TB]
                        i0, i1, i2, i3 = tmpl(), tmpl(), tmpl(), tmpl()
                        i4 = i0
                        acc = T(i0)
                        V("dve", lambda e, ub=ub, acc=acc, t=t: e.tensor_scalar(out=acc, in0=ub[:, 0:TB], scalar1=col(C_CONVW + 0 * 8 + t), scalar2=col(C_CONVB + t), op0=ALU.mult, op1=ALU.add), [B_A2[t]] + P, [B_tmp[i0]])
                        V("dve", lambda e, ub=ub, acc=acc, t=t: e.scalar_tensor_tensor(out=acc, in0=ub[:, 1:TB + 1], scalar=col(C_CONVW + 1 * 8 + t), in1=acc, op0=ALU.mult, op1=ALU.add), [B_A2[t], B_tmp[i0]] + P, [B_tmp[i0]])
                        V("dve", lambda e, ub=ub, acc=acc, t=t: e.scalar_tensor_tensor(out=acc, in0=ub[:, 2:TB + 2], scalar=col(C_CONVW + 2 * 8 + t), in1=acc, op0=ALU.mult, op1=ALU.add), [B_A2[t], B_tmp[i0]] + P, [B_tmp[i0]])
                        V("dve", lambda e, ub=ub, acc=acc, t=t, xc=xc: e.scalar_tensor_tensor(out=xc, in0=ub[:, 3:TB + 3], scalar=col(C_CONVW + 3 * 8 + t), in1=acc, op0=ALU.mult, op1=ALU.add), [B_A2[t], B_tmp[i0]] + P, [B_xc[0]])
                        br_, bi_ = LBANKS[0], LBANKS[1]
                        tr_, ti_, a_, a2_ = T(i1), T(i2), T(i3), T(i4)
                        V("pe", lambda e, t=t, xc=xc, br_=br_: e.matmul(ps[:, br_, :], lhsT=wr_sb[:, t * 128:(t + 1) * 128], rhs=xc, start=True, stop=True), [B_xc[0]] + P, [B_ps[br_]])
                        V("act", lambda e, tr_=tr_, br_=br_, t=t: e.activation(out=tr_, in_=ps[:, br_, :], func=AF.Tanh, bias=hc(C_BR + t), scale=0.5), [B_ps[br_], B_cst], [B_tmp[i1]])
                        V("pe", lambda e, t=t, xc=xc, bi_=bi_: e.matmul(ps[:, bi_, :], lhsT=wi_sb[:, t * 128:(t + 1) * 128], rhs=xc, start=True, stop=True), [B_xc[0]] + P, [B_ps[bi_]])
                        V("act", lambda e, ti_=ti_, bi_=bi_, t=t: e.activation(out=ti_, in_=ps[:, bi_, :], func=AF.Tanh, bias=hc(C_BI + t), scale=0.5), [B_ps[bi_], B_cst], [B_tmp[i2]])
                        V("act", lambda e, tr_=tr_, a_=a_, t=t: e.activation(out=a_, in_=tr_, func=AF.Exp, bias=dcol[:, t:t + 1], scale=dcol[:, t:t + 1]), [B_tmp[i1], B_cst], [B_tmp[i3]])
                        V("act", lambda e, tr_=tr_, a2_=a2_, t=t: e.activation(out=a2_, in_=tr_, func=AF.Exp, bias=dcol[:, 8 + t:9 + t], scale=dcol[:, 8 + t:9 + t]), [B_tmp[i1], B_cst], [B_tmp[i4]])
                        yield 3.0
                        V("dve", lambda e, ti_=ti_, xc=xc: e.scalar_tensor_tensor(out=ti_, in0=ti_, scalar=1.0, in1=xc, op0=ALU.add, op1=ALU.mult), [B_tmp[i2], B_xc[0]], [B_tmp[i2]])
                        V("act", lambda e, a2_=a2_: e.activation(out=a2_, in_=a2_, func=AF.Relu, bias=one_col, scale=-1.0), [B_tmp[i4], B_cst], [B_tmp[i4]])
                        V("act", lambda e, a2_=a2_: e.activation(out=a2_, in_=a2_, func=AF.Sqrt), [B_tmp[i4]], [B_tmp[i4]])
                        yield 0.7
                        V("dve", lambda e, ti_=ti_, a2_=a2_: e.tensor_tensor(out=ti_, in0=ti_, in1=a2_, op=ALU.mult), [B_tmp[i2], B_tmp[i4]], [B_tmp[i2]])
                        V("dve", lambda e, tr_=tr_, a_=a_, ti_=ti_, t=t: e.tensor_tensor_scan(out=tr_, data0=a_, data1=ti_, initial=carry_lru[:, t:t + 1], op0=ALU.mult, op1=ALU.add),
                          [B_tmp[i3], B_tmp[i2], B_clru], [B_tmp[i1]])
                        V("dve", lambda e, tr_=tr_, t=t: e.tensor_copy(out=carry_lru[:, t:t + 1], in_=tr_[:, TB - 1:TB]), [B_tmp[i1]], [B_clru])
                        V("act", lambda e, tr_=tr_, t=t: e.activation(out=A3[:, t * TB:(t + 1) * TB], in_=tr_, func=AF.Copy), [B_tmp[i1]], [B_A3[t]])
                        yield 2.2
                    V("dve", lambda e: e.tensor_copy(out=small[:, 48:48 + 24].rearrange("p (k t) -> p k t", t=3), in_=a2v[:, :, TB:TB + 3]), B_A2, [B_small])
                    if b <= 1:
                        dbg(f"yb{b}", A3[:], B_A3)

                yb = SBANKS[3]

                def v3(ap):
                    return ap.rearrange("p (s t) -> p s t", t=TC)

                def emit_bmm(i):
                    f_ = i // 4
                    ua_ = A4[:, f_ * TB:(f_ + 1) * TB]
                    V("pe", lambda e: e.matmul(ps[:, XBANKS[0], :], lhsT=Bsb[:, i * 128:(i + 1) * 128], rhs=ua_, start=True, stop=True), [B_A4[f_]] + P, [B_ps[XBANKS[0]]])
                    V("pe", lambda e: e.matmul(ps[:, XBANKS[1], :], lhsT=Bsb[:, 2048 + i * 128:2048 + (i + 1) * 128], rhs=ua_, start=True, stop=True), [B_A4[f_]] + P, [B_ps[XBANKS[1]]])

                def S5_gen():
                    for f in range(4):
                        pend = []
                        ua = A4[:, f * TB:(f + 1) * TB]
                        for q in range(4):
                            i = 4 * f + q
                            bxr, bxi = XBANKS[0], XBANKS[1]
                            if i == 0:
                                emit_bmm(0)
                            er3 = Er[:, i * TC:(i + 1) * TC].unsqueeze(1).broadcast_to([128, NSUB, TC])
                            ei3 = Ei[:, i * TC:(i + 1) * TC].unsqueeze(1).broadcast_to([128, NSUB, TC])
                            j1, j2, j3, j4 = tmpi(), tmpi(), tmpi(), tmpi()
                            t1, t2, t3, t4 = T(j1), T(j2), T(j3), T(j4)
                            pxr, pxi = ps[:, bxr, :], ps[:, bxi, :]
                            V("dve", lambda e, t1=t1, pxr=pxr, er3=er3: e.tensor_tensor(out=v3(t1), in0=v3(pxr), in1=er3, op=ALU.mult), [B_ps[bxr]] + TAB, [B_tmp[j1]])
                            V("dve", lambda e, t2=t2, pxi=pxi, ei3=ei3: e.tensor_tensor(out=v3(t2), in0=v3(pxi), in1=ei3, op=ALU.mult), [B_ps[bxi]] + TAB, [B_tmp[j2]])
                            V("dve", lambda e, t3=t3, pxi=pxi, er3=er3: e.tensor_tensor(out=v3(t3), in0=v3(pxi), in1=er3, op=ALU.mult), [B_ps[bxi]] + TAB, [B_tmp[j3]])
                            V("dve", lambda e, t4=t4, pxr=pxr, ei3=ei3: e.tensor_tensor(out=v3(t4), in0=v3(pxr), in1=ei3, op=ALU.mult), [B_ps[bxr]] + TAB, [B_tmp[j4]])
                            if i + 1 < 16:
                                emit_bmm(i + 1)
                            V("dve", lambda e, t1=t1, t2=t2: e.tensor_tensor(out=t1, in0=t1, in1=t2, op=ALU.add), [B_tmp[j1], B_tmp[j2]], [B_tmp[j1]])
                            V("dve", lambda e, t3=t3, t4=t4: e.tensor_tensor(out=t3, in0=t3, in1=t4, op=ALU.subtract), [B_tmp[j3], B_tmp[j4]], [B_tmp[j3]])
                            m_c = s5c[:, 16 * 4 + i:16 * 4 + i + 1]
                            ebr = s5c[:, 16 * 17 + i:16 * 17 + i + 1]
                            ebi = s5c[:, 16 * 18 + i:16 * 18 + i + 1]
                            m_bc = m_c.broadcast_to([128, TC])
                            for sc in range(NSUB):
                                if sc == 0:
                                    lr_, li_ = carry_s5[:, i:i + 1], carry_s5[:, 16 + i:17 + i]
                                    lB = [B_cs5]
                                else:
                                    lr_, li_ = t2[:, sc * TC - 1:sc * TC], t4[:, sc * TC - 1:sc * TC]
                                    lB = [B_tmp[j2], B_tmp[j4]]
                                o_ = 16 + 4 * (sc % 2)
                                ir, ii, sx1, sx2 = small[:, o_:o_ + 1], small[:, o_ + 1:o_ + 2], small[:, o_ + 2:o_ + 3], small[:, o_ + 3:o_ + 4]
                                Bq = B_s5q[sc % 2]
                                V("dve", lambda e, li_=li_, ebi=ebi, sx1=sx1: e.tensor_scalar(out=sx1, in0=li_, scalar1=ebi, scalar2=None, op0=ALU.mult), lB + TAB, [Bq[0]])
                                V("dve", lambda e, lr_=lr_, ebi=ebi, sx2=sx2: e.tensor_scalar(out=sx2, in0=lr_, scalar1=ebi, scalar2=None, op0=ALU.mult), lB + TAB, [Bq[1]])
                                V("dve", lambda e, lr_=lr_, ebr=ebr, sx1=sx1, ir=ir: e.scalar_tensor_tensor(out=ir, in0=lr_, scalar=ebr, in1=sx1, op0=ALU.mult, op1=ALU.subtract), lB + TAB + [Bq[0]], [Bq[2]])
                                V("dve", lambda e, li_=li_, ebr=ebr, sx2=sx2, ii=ii: e.scalar_tensor_tensor(out=ii, in0=li_, scalar=ebr, in1=sx2, op0=ALU.mult, op1=ALU.add), lB + TAB + [Bq[1]], [Bq[3]])
                                sl = slice(sc * TC, (sc + 1) * TC)
                                V("dve", lambda e, t2=t2, t1=t1, sl=sl, ir=ir, m_bc=m_bc: e.tensor_tensor_scan(out=t2[:, sl], data0=m_bc, data1=t1[:, sl], initial=ir, op0=ALU.mult, op1=ALU.add), [B_tmp[j1], Bq[2]] + TAB, [B_tmp[j2]])
                                V("dve", lambda e, t4=t4, t3=t3, sl=sl, ii=ii, m_bc=m_bc: e.tensor_tensor_scan(out=t4[:, sl], data0=m_bc, data1=t3[:, sl], initial=ii, op0=ALU.mult, op1=ALU.add), [B_tmp[j3], Bq[3]] + TAB, [B_tmp[j4]])
                            V("dve", lambda e, t2=t2, i=i: e.tensor_copy(out=carry_s5[:, i:i + 1], in_=t2[:, TB - 1:TB]), [B_tmp[j2]], [B_cs5])
                            V("dve", lambda e, t4=t4, i=i: e.tensor_copy(out=carry_s5[:, 16 + i:17 + i], in_=t4[:, TB - 1:TB]), [B_tmp[j4]], [B_cs5])
                            yield 13.0
                            so = (i % 2) * 2
                            s_re = sbf[:, so * TB:(so + 1) * TB]
                            s_im = sbf[:, (so + 1) * TB:(so + 2) * TB]
                            V("dve", lambda e, t1=t1, t2=t2, er3=er3: e.tensor_tensor(out=v3(t1), in0=v3(t2), in1=er3, op=ALU.mult), [B_tmp[j2]] + TAB, [B_tmp[j1]])
                            V("dve", lambda e, t3=t3, t4=t4, ei3=ei3: e.tensor_tensor(out=v3(t3), in0=v3(t4), in1=ei3, op=ALU.mult), [B_tmp[j4]] + TAB, [B_tmp[j3]])
                            V("dve", lambda e, t2=t2, ei3=ei3: e.tensor_tensor(out=v3(t2), in0=v3(t2), in1=ei3, op=ALU.mult), [B_tmp[j2]] + TAB, [B_tmp[j2]])
                            V("dve", lambda e, t4=t4, er3=er3: e.tensor_tensor(out=v3(t4), in0=v3(t4), in1=er3, op=ALU.mult), [B_tmp[j4]] + TAB, [B_tmp[j4]])
                            V("dve", lambda e, t1=t1, t3=t3, s_re=s_re: e.tensor_tensor(out=s_re, in0=t1, in1=t3, op=ALU.subtract), [B_tmp[j1], B_tmp[j3]], [B_sbf[so]])
                            V("dve", lambda e, t2=t2, t4=t4, s_im=s_im: e.tensor_tensor(out=s_im, in0=t4, in1=t2, op=ALU.add), [B_tmp[j2], B_tmp[j4]], [B_sbf[so + 1]])
                            for fn_ in pend:
                                fn_()
                            pend = []

                            def cmm(i=i, q=q, s_re=s_re, s_im=s_im, so=so):
                                V("pe", lambda e: e.matmul(ps[32 * q:32 * q + 32, yb, :], lhsT=Csb[:, i * 32:(i + 1) * 32], rhs=s_re, start=True, stop=False, tile_position=(0, 32 * q)), [B_sbf[so]] + TAB, [B_ps[yb]])
                                V("pe", lambda e: e.matmul(ps[32 * q:32 * q + 32, yb, :], lhsT=Csb[:, 512 + i * 32:512 + (i + 1) * 32], rhs=s_im, start=False, stop=True, tile_position=(0, 32 * q)), [B_sbf[so + 1]] + TAB, [B_ps[yb]])
                            pend.append(cmm)
                            if q == 3:
                                for fn_ in pend:
                                    fn_()
                                pend = []
                                k1, k2 = j1, j3
                                y_, w_ = T(k1), T(k2)
                                V("dve", lambda e, f=f, y_=y_: e.scalar_tensor_tensor(out=y_, in0=A4[:, f * TB:(f + 1) * TB], scalar=col(C_S5D + f), in1=ps[:, yb, :], op0=ALU.mult, op1=ALU.add), [B_A4[f], B_ps[yb]] + P, [B_tmp[k1]])
                                V("act", lambda e, f=f, y_=y_: e.activation(out=A4[:, f * TB:(f + 1) * TB], in_=y_, func=AF.Gelu_apprx_tanh), [B_tmp[k1]], [B_A4[f]])
                            yield 5.0

                gC, gL, gS5 = SC_rest(), LRU_gen(), S5_gen()
                dC = dL = dS5 = False
                while not (dL and dS5):
                    adv = False
                    if not dC:
                        adv = True
                        try:
                            c_ = next(gC)
                            yield c_
                        except StopIteration:
                            dC = True
                    elif not dL and prog["merged"] >= b - 1:
                        adv = True
                        try:
                            c_ = next(gL)
                            yield c_
                        except StopIteration:
                            dL = True
                    if not dS5:
                        adv = True
                        try:
                            c_ = next(gS5)
                            yield c_
                        except StopIteration:
                            dS5 = True
                    if not adv:
                        yield "W"
                if b <= 1:
                    dbg(f"z{b}", A4[:], B_A4)
                while prog["down"] < b - 1:
                    yield "W"
                glu_tmp = []
                for f in range(4):
                    bn = bankS()
                    for k in range(4):
                        V("pe", lambda e, f=f, k=k, bn=bn: e.matmul(ps[:, bn, :], lhsT=w_glu_sb[:, k * 512 + f * 128:k * 512 + (f + 1) * 128], rhs=A4[:, k * TB:(k + 1) * TB], start=(k == 0), stop=(k == 3)), [B_A4[k]] + P, [B_ps[bn]])
                    k1 = tmpi()
                    g_ = T(k1)
                    V("act", lambda e, f=f, bn=bn, g_=g_: e.activation(out=g_, in_=ps[:, bn, :], func=AF.Tanh, bias=hc(C_BGLU + f), scale=0.5), [B_ps[bn], B_cst], [B_tmp[k1]])
                    glu_tmp.append((k1, g_))
                for f in range(4):
                    k1, g_ = glu_tmp[f]
                    V("dve", lambda e, f=f, g_=g_: e.scalar_tensor_tensor(out=A6[:, f * TB:(f + 1) * TB], in0=g_, scalar=1.0, in1=A4[:, f * TB:(f + 1) * TB], op0=ALU.add, op1=ALU.mult), [B_tmp[k1], B_A4[f]], [B_A6[f]])
                yield 6.0
                if b <= 1:
                    dbg(f"ya{b}", A6[:], B_A6)

            def F_head(b):
                t0 = b * TB
                fbk = lambda k: FB[:, k * TB:(k + 1) * TB]
                for tt in range(4):
                    pi_ = tmpf(2)
                    xs = T(pi_, 2)
                    Bxs = [B_tmp[pi_], B_tmp[pi_ + 1]]
                    em.dma("sp", f"dxf{(pi_ - NTS) // 2}", [(xs, x_d[t0 + tt * 128:t0 + (tt + 1) * 128, :])], writes=Bxs)
                    for hh in range(2):
                        bn = bankF()
                        for kk in range(4):
                            k = hh * 4 + kk
                            V("pe", lambda e, k=k, kk=kk, bn=bn, xs=xs: e.transpose(out=ps[:, bn, kk * 128:(kk + 1) * 128], in_=xs[:, k * 128:(k + 1) * 128], identity=ident[:]),
                              Bxs + P, [B_ps[bn]])
                        dst = x_fm[:, hh * 4 * TB:(hh * 4 + 4) * TB].rearrange("p (k t) -> p k t", t=TB)[:, :, tt * 128:(tt + 1) * 128]
                        src = ps[:, bn, :].rearrange("p (k t) -> p k t", t=128)
                        V("act", lambda e, dst=dst, src=src: e.activation(out=dst, in_=src, func=AF.Copy), [B_ps[bn]], [B_xfm[hh * 4 + kk] for kk in range(4)])
                    yield 3.0
                yield from rmsnorm_fm(C_GMIX, FB, B_FB, TB)
                yield 5.0
                for hf in range(2):
                    sga, cga = get_slab(w_in_d, 0, 8, 1536 + hf * 512, 512)
                    sgb, cgb = get_slab(w_in_d, 0, 8, 2560 + hf * 512, 512)
                    sa, ca = get_slab(w_a_d, 0, 4, hf * 512, 512)
                    sbo, cbo = get_slab(w_b_d, 0, 8, hf * 512, 512)
                    for jj in range(4):
                        j = hf * 4 + jj
                        k1, k2 = tmpf(), tmpf()
                        ga_, gb_ = T(k1), T(k2)
                        bga, bpa, bgb, bpb = bankF(), bankF(), bankF(), bankF()
                        mm_group(bga, sga, cga, jj * 128, 8, fbk, B_FB)
                        V("act", lambda e, ga_=ga_, bga=bga, j=j: e.activation(out=ga_, in_=ps[:, bga, :], func=AF.Tanh, bias=hc(C_BIN + 12 + j), scale=0.5), [B_ps[bga], B_cst], [B_tmp[k1]])
                        mm_group(bpa, sa, ca, jj * 128, 4, lambda k: A6[:, k * TB:(k + 1) * TB], B_A6)
                        mm_group(bgb, sgb, cgb, jj * 128, 8, fbk, B_FB)
                        V("act", lambda e, gb_=gb_, bgb=bgb, j=j: e.activation(out=gb_, in_=ps[:, bgb, :], func=AF.Tanh, bias=hc(C_BIN + 20 + j), scale=0.5), [B_ps[bgb], B_cst], [B_tmp[k2]])
                        mm_group(bpb, sbo, cbo, jj * 128, 8, lambda k: A3[:, k * TB:(k + 1) * TB], B_A3)
                        yield "Y"
                        V("dve", lambda e, ga_=ga_, bpa=bpa: e.scalar_tensor_tensor(out=ga_, in0=ga_, scalar=1.0, in1=ps[:, bpa, :], op0=ALU.add, op1=ALU.mult), [B_tmp[k1], B_ps[bpa]], [B_tmp[k1]])
                        V("dve", lambda e, gb_=gb_, bpb=bpb: e.scalar_tensor_tensor(out=gb_, in0=gb_, scalar=1.0, in1=ps[:, bpb, :], op0=ALU.add, op1=ALU.mult), [B_tmp[k2], B_ps[bpb]], [B_tmp[k2]])
                        V("dve", lambda e, ga_=ga_, gb_=gb_, j=j: e.tensor_tensor(out=hid[:, j * TB:(j + 1) * TB], in0=gb_, in1=ga_, op=ALU.add), [B_tmp[k1], B_tmp[k2]], [B_hid[j]])
                        if jj == 3:
                            release_slab(sga); release_slab(sgb); release_slab(sa); release_slab(sbo)
                        yield 7.5
                prog["merged"] = b
                if b == 0:
                    dbg("merged", hid[:, 0:8 * TB], B_hid[0:8])
                for hf in range(2):
                    si, cols = get_slab(w_o_d, 0, 8, hf * 512, 512)
                    for jj in range(4):
                        j = hf * 4 + jj
                        bn = bankF()
                        mm_group(bn, si, cols, jj * 128, 8, lambda k: hid[:, k * TB:(k + 1) * TB], B_hid)
                        yield "Y"
                        V("dve", lambda e, j=j, bn=bn: e.scalar_tensor_tensor(out=x_fm[:, j * TB:(j + 1) * TB], in0=ps[:, bn, :], scalar=0.25, in1=x_fm[:, j * TB:(j + 1) * TB], op0=ALU.mult, op1=ALU.add), [B_ps[bn], B_xfm[j]], [B_xfm[j]])
                        if jj == 3:
                            release_slab(si)
                        yield 2.2
                if b == 0:
                    dbg("x1", x_fm[:], B_xfm)

            def F_rest(b):
                t0 = b * TB
                fb_k = lambda k: FB[:, k * TB:(k + 1) * TB]
                yield from rmsnorm_fm(C_GFFN, FB, B_FB, TB)
                yield 4.0
                pend_ev = []
                for s in range(6):
                    cols_ = 512 if s < 5 else 256
                    sg, cg = get_slab(w_fg_d, 0, 8, s * 512, cols_)
                    su, cu = get_slab(w_fu_d, 0, 8, s * 512, cols_)
                    for jj in range(cols_ // 128):
                        c = s * 4 + jj
                        bg_, bu_ = bankF(), bankF()
                        mm_group(bg_, sg, cg, jj * 128, 8, fb_k, B_FB)
                        mm_group(bu_, su, cu, jj * 128, 8, fb_k, B_FB)
                        for fn_ in pend_ev:
                            fn_()
                        pend_ev = []

                        k1 = tmpf()
                        tg = T(k1)
                        V("act", lambda e, tg=tg, bg_=bg_: e.activation(out=tg, in_=ps[:, bg_, :], func=AF.Silu), [B_ps[bg_]], [B_tmp[k1]])

                        def ev(bu_=bu_, c=c, k1=k1, tg=tg):
                            V("dve", lambda e: e.tensor_tensor(out=hid[:, c * TB:(c + 1) * TB], in0=tg, in1=ps[:, bu_, :], op=ALU.mult), [B_tmp[k1], B_ps[bu_]], [B_hid[c]])
                        pend_ev.append(ev)
                        if jj == cols_ // 128 - 1:
                            release_slab(sg); release_slab(su)
                        yield 4.3
                for fn_ in pend_ev:
                    fn_()
                pend_ev = []
                for j in range(8):
                    si, cols = get_slab(w_fd_d, 0, 22, j * 128, 128)
                    bn = bankF()
                    for k in range(22):
                        V("pe", lambda e, k=k, si=si, cols=cols, bn=bn: e.matmul(ps[:, bn, :], lhsT=slab_lhsT(si, cols, k, 0), rhs=hid[:, k * TB:(k + 1) * TB], start=(k == 0), stop=(k == 21)), [B_slot[si[1]], B_hid[k]], [B_ps[bn]])
                    for fn_ in pend_ev:
                        fn_()
                    pend_ev = []

                    def evd(j=j, bn=bn):
                        V("dve", lambda e: e.tensor_tensor(out=x_fm[:, j * TB:(j + 1) * TB], in0=ps[:, bn, :], in1=x_fm[:, j * TB:(j + 1) * TB], op=ALU.add), [B_ps[bn], B_xfm[j]], [B_xfm[j]])
                    pend_ev.append(evd)
                    release_slab(si)
                    yield 6.0
                for fn_ in pend_ev:
                    fn_()
                pend_ev = []
                prog["down"] = b
                if b == 0:
                    dbg("x2", x_fm[:], B_xfm)
                yield from rmsnorm_fm(C_GPLEG, FB, B_FB, TB)
                yield 4.0
                pi_ = tmpf(2)
                pstage = T(pi_, 2)
                Bp = [B_tmp[pi_], B_tmp[pi_ + 1]]
                em.dma("sp", "dp", [(pstage.rearrange("p (t c) -> p t c", c=PLE), p_d[t0:t0 + TB, :].rearrange("(t r) c -> r t c", r=128))], writes=Bp)
                for kk in range(2):
                    bn = bankF()
                    for tt in range(4):
                        V("pe", lambda e, kk=kk, tt=tt, bn=bn, pstage=pstage: e.transpose(out=ps[:, bn, tt * 128:(tt + 1) * 128], in_=pstage[:, tt * PLE + kk * 128:tt * PLE + (kk + 1) * 128], identity=ident[:]), Bp + P, [B_ps[bn]])
                    V("act", lambda e, kk=kk, bn=bn: e.activation(out=p_fm[:, kk * TB:(kk + 1) * TB], in_=ps[:, bn, :], func=AF.Copy), [B_ps[bn]], [B_pfm[kk]])
                spl, cpl = get_slab(w_pl_d, 0, 2, 0, 1024)
                spg = [get_slab(w_pg_d, 0, 8, hf * 512, 512) for hf in range(2)]
                for tt in range(4):
                    tsl = slice(tt * 128, (tt + 1) * 128)
                    ei_ = tmpf(2)
                    et = T(ei_, 2)
                    Be = [B_tmp[ei_], B_tmp[ei_ + 1]]
                    ti2 = tmpf(2)
                    tp = T(ti2, 2)
                    Bt = [B_tmp[ti2], B_tmp[ti2 + 1]]
                    for hf in range(2):
                        be = bankF()
                        for kk in range(2):
                            V("pe", lambda e, hf=hf, kk=kk, tsl=tsl, spl=spl, be=be: e.matmul(ps[:, be, :], lhsT=p_fm[:, kk * TB:(kk + 1) * TB][:, tsl], rhs=slots[spl[1]][:, kk * 1024 + hf * 512:kk * 1024 + (hf + 1) * 512], start=(kk == 0), stop=(kk == 1)), [B_pfm[kk], B_slot[spl[1]]], [B_ps[be]])
                        V("act", lambda e, hf=hf, be=be, et=et: e.activation(out=et[:, hf * 512:(hf + 1) * 512], in_=ps[:, be, :], func=AF.Copy), [B_ps[be]], [Be[hf]])
                    sse, rs = small[:, 1:2], small[:, 2:3]
                    V("dve", lambda e, sse=sse: e.memset(sse, 0.0), [B_smallF], [B_smallF])
                    V("act", lambda e, tp=tp, et=et, sse=sse: e.activation(out=tp, in_=et, func=AF.Square, accum_out=sse), Be + [B_smallF], Bt + [B_smallF])
                    yield "Y"
                    V("dve", lambda e, sse=sse, rs=rs: e.tensor_scalar(out=rs, in0=sse, scalar1=1.0 / D, scalar2=EPS, op0=ALU.mult, op1=ALU.add), [B_smallF], [B_smallF])
                    V("act", lambda e, rs=rs: e.activation(out=rs, in_=rs, func=AF.Sqrt), [B_smallF], [B_smallF])
                    V("dve", lambda e, rs=rs: e.reciprocal(out=rs, in_=rs), [B_smallF], [B_smallF])
                    for hf in range(2):
                        hs = slice(hf * 512, (hf + 1) * 512)
                        sgi, cgi = spg[hf]
                        bg_ = bankF()
                        for k in range(8):
                            V("pe", lambda e, k=k, tsl=tsl, sgi=sgi, bg_=bg_: e.matmul(ps[:, bg_, :], lhsT=fb_k(k)[:, tsl], rhs=slots[sgi[1]][:, k * 512:(k + 1) * 512], start=(k == 0), stop=False), [B_FB[k], B_slot[sgi[1]]], [B_ps[bg_]])
                        V("pe", lambda e, hf=hf, bg_=bg_: e.matmul(ps[:, bg_, :], lhsT=ones_row[0:1, :], rhs=bpg_row[0:1, hf * 512:(hf + 1) * 512], start=False, stop=True), P, [B_ps[bg_]])
                        V("act", lambda e, tp=tp, hs=hs, bg_=bg_: e.activation(out=tp[:, hs], in_=ps[:, bg_, :], func=AF.Tanh, scale=0.5), [B_ps[bg_]], [Bt[hf]])
                        bx = bankF()
                        for kk in range(4):
                            k = hf * 4 + kk
                            V("pe", lambda e, k=k, kk=kk, tsl=tsl, bx=bx: e.transpose(out=ps[:, bx, kk * 128:(kk + 1) * 128], in_=x_fm[:, k * TB:(k + 1) * TB][:, tsl], identity=ident[:]), [B_xfm[k]] + P, [B_ps[bx]])
                        V("dve", lambda e, et=et, hs=hs, rs=rs: e.scalar_tensor_tensor(out=et[:, hs], in0=et[:, hs], scalar=rs, in1=gple_bc[:, hs], op0=ALU.mult, op1=ALU.mult), [Be[hf], B_smallF] + P, [Be[hf]])
                        V("dve", lambda e, et=et, tp=tp, hs=hs: e.scalar_tensor_tensor(out=et[:, hs], in0=tp[:, hs], scalar=1.0, in1=et[:, hs], op0=ALU.add, op1=ALU.mult), [Bt[hf], Be[hf]], [Be[hf]])
                        V("dve", lambda e, et=et, hs=hs, bx=bx: e.scalar_tensor_tensor(out=et[:, hs], in0=et[:, hs], scalar=0.5, in1=ps[:, bx, :], op0=ALU.mult, op1=ALU.add), [Be[hf], B_ps[bx]], [Be[hf]])
                    ss3, r3 = small[:, 3:4], small[:, 4:5]
                    V("dve", lambda e, ss3=ss3: e.memset(ss3, 0.0), [B_smallF], [B_smallF])
                    V("act", lambda e, tp=tp, et=et, ss3=ss3: e.activation(out=tp, in_=et, func=AF.Square, accum_out=ss3), Be + [B_smallF], Bt + [B_smallF])
                    yield "Y"
                    V("dve", lambda e, ss3=ss3, r3=r3: e.tensor_scalar(out=r3, in0=ss3, scalar1=1.0 / D, scalar2=EPS, op0=ALU.mult, op1=ALU.add), [B_smallF], [B_smallF])
                    V("act", lambda e, r3=r3: e.activation(out=r3, in_=r3, func=AF.Sqrt), [B_smallF], [B_smallF])
                    V("dve", lambda e, r3=r3: e.reciprocal(out=r3, in_=r3), [B_smallF], [B_smallF])
                    V("dve", lambda e, et=et, r3=r3: e.scalar_tensor_tensor(out=et, in0=et, scalar=r3, in1=gfin_bc[:], op0=ALU.mult, op1=ALU.mult), Be + [B_smallF] + P, Be)
                    oi = ei_ - NTS
                    em.dma("sp", f"do{oi}", [(out_d[t0 + tt * 128:t0 + (tt + 1) * 128, :], et)], reads=Be, writes=[B_out[oi]])
                    if tt == 3:
                        release_slab(spl); release_slab(spg[0][0]); release_slab(spg[1][0])
                    yield 7.5

            def run_all(g):
                for _ in g:
                    pass

            prog = {"merged": -1, "down": -1}

            def interleave(gF, gS, totF, totS):
                pF = pS = 0.0
                doneF = doneS = False
                s_blocked = False
                force_s = False
                while not (doneF and doneS):
                    pickF = (not doneF) and (doneS or s_blocked or pF / totF <= pS / totS)
                    if force_s and not doneS and not s_blocked:
                        pickF = False
                    force_s = False
                    if pickF:
                        s_blocked = False
                        try:
                            c_ = next(gF)
                            if c_ == "Y":
                                force_s = True
                            else:
                                pF += c_ or 1.0
                        except StopIteration:
                            doneF = True
                    else:
                        try:
                            c_ = next(gS)
                            if c_ == "W":
                                assert not doneF, "scan thread blocked forever"
                                s_blocked = True
                            else:
                                pS += c_ or 1.0
                        except StopIteration:
                            doneS = True

            def F_thread(b):
                yield from F_head(b)
                yield from F_rest(b)

            run_all(S_thread(0))
            for b in range(NB):
                if b + 1 < NB and PIPELINE:
                    interleave(F_thread(b), S_thread(b + 1), 330.0, 400.0)
                else:
                    run_all(F_thread(b))
                    if b + 1 < NB:
                        run_all(S_thread(b + 1))
            em.final_wait("sp", B_out + dbg_bufs)
            return plan_rec


        dry = Em(eng_names, sems)
        plan0 = program(dry, None, False)
        em = Em(eng_names, sems)
        program(em, plan0, debug)

        def replay(name):
            def f(e):
                for t in em.thunks[name]:
                    t(e)
            return f
        block.sync(replay("sp"))
        block.scalar(replay("act"))
        block.vector(replay("dve"))
        block.gpsimd(replay("pool"))
        block.tensor(replay("pe"))
    return nc


def _host_layout(inp):
    f = np.float32
    sq = lambda a: np.ascontiguousarray(np.asarray(a, dtype=f)[0])
    colp = np.zeros((128, NCOL), f)

    def put(c0, vec):
        v = np.asarray(vec, dtype=f).reshape(-1, 128)
        colp[:, c0:c0 + v.shape[0]] = v.T
    put(C_GMIX, sq(inp["g_mix"]))
    put(C_BIN, sq(inp["b_in"]))
    put(C_S5D, sq(inp["s5_d"]).reshape(-1))
    put(C_BGLU, sq(inp["b_glu"]))
    cw = sq(inp["conv_w"])
    for k in range(4):
        put(C_CONVW + 8 * k, cw[k])
    put(C_CONVB, sq(inp["conv_b"]))
    put(C_BR, sq(inp["b_r"]).reshape(-1))
    put(C_BI, sq(inp["b_i"]).reshape(-1))
    put(C_LAM, sq(inp["lru_lambda"]))
    put(C_GFFN, sq(inp["g_ffn"]))
    put(C_GPLEG, sq(inp["g_ple_gate"]))
    lam_re, lam_im, log_dt = sq(inp["lam_re"]), sq(inp["lam_im"]), sq(inp["log_dt"])
    s5p = np.zeros((128, 48), f)
    s5p[:, 0:16] = lam_re.reshape(16, 2, 64).transpose(1, 2, 0).reshape(128, 16)
    s5p[:, 16:32] = lam_im.reshape(16, 2, 64).transpose(1, 2, 0).reshape(128, 16)
    s5p[:, 32:48] = np.broadcast_to(log_dt.reshape(16, 2, 1), (16, 2, 64)).transpose(1, 2, 0).reshape(128, 16)
    rowp = np.stack([sq(inp["b_ple_gate"]), sq(inp["g_ple"]), np.asarray(inp["g_final"], dtype=f)], 0)
    w_r, w_i = sq(inp["w_r"]), sq(inp["w_i"])
    wr_bd = np.zeros((128, 8, 128), f)
    wi_bd = np.zeros((128, 8, 128), f)
    for t in range(8):
        for h2 in range(2):
            wr_bd[h2 * 64:(h2 + 1) * 64, t, h2 * 64:(h2 + 1) * 64] = w_r[2 * t + h2]
            wi_bd[h2 * 64:(h2 + 1) * 64, t, h2 * 64:(h2 + 1) * 64] = w_i[2 * t + h2]
    b_re, b_im = sq(inp["s5_b_re"]), sq(inp["s5_b_im"])
    c_re, c_im = sq(inp["s5_c_re"]), sq(inp["s5_c_im"])
    s5B = np.zeros((128, 2, 16, 128), f)
    s5C = np.zeros((128, 2, 16, 32), f)
    for i in range(16):
        q = i % 4
        for g2 in range(2):
            gidx = 2 * i + g2
            r0 = 32 * q + 16 * g2
            s5B[r0:r0 + 16, 0, i, g2 * 64:(g2 + 1) * 64] = b_re[gidx].T
            s5B[r0:r0 + 16, 1, i, g2 * 64:(g2 + 1) * 64] = b_im[gidx].T
            s5C[g2 * 64:(g2 + 1) * 64, 0, i, g2 * 16:(g2 + 1) * 16] = c_re[gidx].T
            s5C[g2 * 64:(g2 + 1) * 64, 1, i, g2 * 16:(g2 + 1) * 16] = c_im[gidx].T
    ident = np.eye(128, dtype=f)
    iota = np.ascontiguousarray(np.broadcast_to(np.arange(TC, dtype=f)[None, None, :], (128, 16, TC))).reshape(128, 16 * TC)
    shared = {
        "w_in": sq(inp["w_in"]), "w_glu": sq(inp["w_glu"]), "w_a_out": sq(inp["w_a_out"]), "w_b_out": sq(inp["w_b_out"]),
        "w_o": sq(inp["w_o"]), "w_ffn_gate": sq(inp["w_ffn_gate"]), "w_ffn_up": sq(inp["w_ffn_up"]), "w_ffn_down": sq(inp["w_ffn_down"]),
        "w_ple_gate": sq(inp["w_ple_gate"]), "w_ple": sq(inp["w_ple"]),
        "colp": colp, "s5p": s5p, "rowp": np.ascontiguousarray(rowp),
        "wr_bd": wr_bd.reshape(128, -1), "wi_bd": wi_bd.reshape(128, -1),
        "s5B": s5B.reshape(128, -1), "s5C": s5C.reshape(128, -1), "ident": ident, "iota": iota,
    }
    return shared


def kernel(**inputs):
    x = np.asarray(inputs["x"], dtype=np.float32)
    p = np.asarray(inputs["p"], dtype=np.float32)[0]
    shared = _host_layout(inputs)
    nc = build_nc()
    in_maps = []
    for c in range(8):
        m = dict(shared)
        m["x"] = np.ascontiguousarray(x[c])
        m["p"] = np.ascontiguousarray(p[c])
        in_maps.append(m)
    res = run_bass_kernel_spmd(nc, in_maps, core_ids=list(range(8)))
    return np.stack([np.asarray(r["out"], dtype=np.float32) for r in res.results], 0)
```
